# Optimizing a Trainium2 kernel written in Bass

```python
import jax
import jax.numpy as jnp
from jax import lax
import numpy as np

D_MODEL = 1024
BATCH = 32
SEQ = 256
DEPTH = 4
DEC_BATCH = 2
DEC_SEQ = 2048
PAST_LEN = 512

GRID_W = 64
N_MIXERS = 4
D_FF = 2816
EPS = 1e-6
N_MLSTM = (DEPTH + 3) // N_MIXERS
N_FOURIER = (DEPTH + 2) // N_MIXERS
N_GMLP = (DEPTH + 1) // N_MIXERS
N_NA = DEPTH // N_MIXERS

MLSTM_HEADS = 4
MLSTM_DK = D_MODEL // MLSTM_HEADS
MLSTM_DV = D_MODEL // MLSTM_HEADS
MLSTM_CHUNK = 128

FOURIER_GROUPS = 4

GMLP_WIDTH = D_MODEL
GMLP_GROUPS = 4
GMLP_CHUNK = 128

NA_HEADS = 16
NA_HD = D_MODEL // NA_HEADS
NA_KH = 8
NA_KW = 16
NA_QCB = 16
NA_KCB = 32
ATTN_QBLOCK = 128

kernel_name = 'hybrid_flow_backbone_step'


def rmsnorm(x, g):
    xf = x.astype(jnp.float32)
    y = xf * lax.rsqrt(jnp.mean(xf * xf, axis=-1, keepdims=True) + EPS)
    return y.astype(x.dtype) * g


def adaln(cond, w, b):
    m = jax.nn.silu(cond) @ w + b
    return m.reshape(m.shape[0], 1, 9, D_MODEL)


def sub_input(x, g, mods, idx):
    shift = mods[:, :, 3 * idx]
    scale = mods[:, :, 3 * idx + 1]
    gate = mods[:, :, 3 * idx + 2]
    return rmsnorm(x, g) * (1 + scale) + shift, gate


def swiglu(h, w1, w3, w2):
    return (jax.nn.silu(h @ w1) * (h @ w3)) @ w2


def mlstm_scan(q, k, v, i_pre, logf, C0, n0, m0):
    B, S = q.shape[0], q.shape[1]
    nc = S // MLSTM_CHUNK

    def to_chunks(a):
        return jnp.moveaxis(a.reshape(B, nc, MLSTM_CHUNK, *a.shape[2:]), 1, 0)

    xs = (to_chunks(q), to_chunks(k), to_chunks(v), to_chunks(i_pre), to_chunks(logf))
    causal = jnp.tril(jnp.ones((MLSTM_CHUNK, MLSTM_CHUNK), bool))[None, :, :, None]

    def step(carry, inp):
        C, n, m = carry
        qc, kc, vc, ic, fc = inp
        b = jnp.cumsum(fc, axis=1)
        dmat = b[:, :, None, :] - b[:, None, :, :] + ic[:, None, :, :]
        dmat = jnp.where(causal, dmat, -jnp.inf)
        inter = b + m[:, None, :]
        m_t = jnp.maximum(inter, jnp.max(dmat, axis=2))
        a = jnp.exp(dmat - m_t[:, :, None, :]) * jnp.einsum('bthk,bshk->btsh', qc, kc)
        w_inter = jnp.exp(inter - m_t)
        num = jnp.einsum('btsh,bshv->bthv', a, vc) + w_inter[..., None] * jnp.einsum('bhkv,bthk->bthv', C, qc)
        den = jnp.sum(a, axis=2) + w_inter * jnp.einsum('bhk,bthk->bth', n, qc)
        h = num / jnp.maximum(jnp.abs(den), jnp.exp(-m_t))[..., None]
        b_last = b[:, -1, :]
        g_s = b_last[:, None, :] - b + ic
        m_new = jnp.maximum(b_last + m, jnp.max(g_s, axis=1))
        w_s = jnp.exp(g_s - m_new[:, None, :])
        decay = jnp.exp(b_last + m - m_new)
        C_new = decay[..., None, None] * C + jnp.einsum('bsh,bshk,bshv->bhkv', w_s, kc, vc)
        n_new = decay[..., None] * n + jnp.einsum('bsh,bshk->bhk', w_s, kc)
        return (C_new, n_new, m_new), h

    init = (C0.astype(jnp.float32), n0.astype(jnp.float32), m0.astype(jnp.float32))
    (C, n, m), hs = lax.scan(step, init, xs)
    h = jnp.moveaxis(hs, 0, 1).reshape(B, S, MLSTM_HEADS, MLSTM_DV)
    return h, C, n, m


def mlstm_mixer(h, w_qkv, w_if, b_if, w_og, head_g, w_out, C0, n0, m0):
    B, S, _ = h.shape
    qkv = h @ w_qkv
    q = qkv[..., :D_MODEL].reshape(B, S, MLSTM_HEADS, MLSTM_DK).astype(jnp.float32)
    k = qkv[..., D_MODEL:2 * D_MODEL].reshape(B, S, MLSTM_HEADS, MLSTM_DK).astype(jnp.float32) * (MLSTM_DK ** -0.5)
    v = qkv[..., 2 * D_MODEL:].reshape(B, S, MLSTM_HEADS, MLSTM_DV).astype(jnp.float32)
    hs, Cs, ns, ms = [], [], [], []
    for d in range(2):
        g = (h @ w_if[d] + b_if[d]).astype(jnp.float32)
        seq = (q, k, v, g[..., :MLSTM_HEADS], jax.nn.log_sigmoid(g[..., MLSTM_HEADS:]))
        if d == 1:
            seq = tuple(jnp.flip(a, axis=1) for a in seq)
        hd, C, n, m = mlstm_scan(*seq, C0[:, d], n0[:, d], m0[:, d])
        if d == 1:
            hd = jnp.flip(hd, axis=1)
        hs.append(hd)
        Cs.append(C)
        ns.append(n)
        ms.append(m)
    hsum = hs[0] + hs[1]
    hn = hsum * lax.rsqrt(jnp.mean(hsum * hsum, axis=-1, keepdims=True) + EPS)
    hn = hn.reshape(B, S, D_MODEL).astype(h.dtype) * head_g
    y = (jax.nn.sigmoid(h @ w_og) * hn) @ w_out
    return y, (jnp.stack(Cs, axis=1), jnp.stack(ns, axis=1), jnp.stack(ms, axis=1))


def fourier_mixer(h, w_out, b_out):
    B, S, _ = h.shape
    hg = h.astype(jnp.float32).reshape(B, S, FOURIER_GROUPS, D_MODEL // FOURIER_GROUPS)
    f = jnp.fft.fft2(hg, axes=(1, 3), norm='ortho').real
    return f.reshape(B, S, D_MODEL).astype(h.dtype) @ w_out + b_out


def gmlp_mixer(h, w_in, b_in, v_g, w_s, b_s, w_out):
    B, S, _ = h.shape
    z = jax.nn.gelu(h @ w_in + b_in)
    u, v = z[..., :GMLP_WIDTH], z[..., GMLP_WIDTH:]
    v = rmsnorm(v, v_g)
    vg = v.reshape(B, S // GMLP_CHUNK, GMLP_CHUNK, GMLP_GROUPS, GMLP_WIDTH // GMLP_GROUPS)
    sv = jnp.einsum('gts,bnsgc->bntgc', w_s, vg) + b_s.T[None, None, :, :, None]
    return (u * sv.reshape(B, S, GMLP_WIDTH)) @ w_out


def na_qkv(h, w_qkv):
    B, S, _ = h.shape
    qkv = (h @ w_qkv).reshape(B, S, 3, NA_HEADS, NA_HD)
    return qkv[:, :, 0], qkv[:, :, 1], qkv[:, :, 2]


def context_attention(q, k, v):
    B, S, H, hd = q.shape
    nb = S // ATTN_QBLOCK
    qb = jnp.moveaxis(q.reshape(B, nb, ATTN_QBLOCK, H, hd), 1, 0)

    def block(qblk):
        s = jnp.einsum('bqhd,bkhd->bhqk', qblk, k).astype(jnp.float32) * (hd ** -0.5)
        p = jax.nn.softmax(s, axis=-1).astype(v.dtype)
        return jnp.einsum('bhqk,bkhd->bqhd', p, v)

    o = lax.map(block, qb)
    return jnp.moveaxis(o, 0, 1).reshape(B, S, H * hd)


def na_tables(rows):
    kh = min(NA_KH, rows)
    r = np.arange(rows)
    row_start = np.clip(r - NA_KH // 2, 0, rows - kh)
    key_rows = row_start[:, None] + np.arange(kh)[None, :]
    dr_idx = key_rows - r[:, None] + (NA_KH - 1)
    n_cb = GRID_W // NA_QCB
    qcols = np.arange(n_cb)[:, None] * NA_QCB + np.arange(NA_QCB)[None, :]
    col_start = np.clip(np.arange(n_cb) * NA_QCB - NA_KW // 2, 0, GRID_W - NA_KCB)
    key_cols = col_start[:, None] + np.arange(NA_KCB)[None, :]
    q_start = np.clip(qcols - NA_KW // 2, 0, GRID_W - NA_KW)
    kc = key_cols[:, None, :]
    mask = (kc >= q_start[:, :, None]) & (kc < q_start[:, :, None] + NA_KW)
    dc_idx = np.clip(kc - qcols[:, :, None] + (NA_KW - 1), 0, 2 * NA_KW - 2)
    return kh, key_rows, dr_idx, key_cols, mask, dc_idx


def na_latent(q, k, v, k_ctx, v_ctx, rpb):
    B, N, H, hd = q.shape
    rows = N // GRID_W
    kh, key_rows, dr_idx, key_cols, mask, dc_idx = na_tables(rows)
    n_cb = GRID_W // NA_QCB
    qg = q.reshape(B, rows, n_cb, NA_QCB, H, hd)
    kg = k.reshape(B, rows, GRID_W, H, hd)
    vg = v.reshape(B, rows, GRID_W, H, hd)
    ri = key_rows[:, None, :, None]
    ci = key_cols[None, :, None, :]
    kb = kg[:, ri, ci]
    vb = vg[:, ri, ci]
    scale = hd ** -0.5
    bias = jnp.moveaxis(rpb[:, dr_idx[:, None, None, :, None], dc_idx[None, :, :, None, :]], 0, 3)
    s_win = jnp.einsum('brcqhd,brcyxhd->brcqhyx', qg, kb).astype(jnp.float32) * scale + bias.astype(jnp.float32)
    s_win = jnp.where(mask[None, None, :, :, None, None, :], s_win, -jnp.inf)
    s_ctx = jnp.einsum('brcqhd,bkhd->brcqhk', qg, k_ctx).astype(jnp.float32) * scale
    n_win = kh * NA_KCB
    logits = jnp.concatenate([s_win.reshape(*s_win.shape[:5], n_win), s_ctx], axis=-1)
    p = jax.nn.softmax(logits, axis=-1).astype(v.dtype)
    p_win = p[..., :n_win].reshape(s_win.shape)
    out = jnp.einsum('brcqhyx,brcyxhd->brcqhd', p_win, vb) + jnp.einsum('brcqhk,bkhd->brcqhd', p[..., n_win:], v_ctx)
    return out.reshape(B, N, H * hd)


def setup_inputs(seed: int = 0) -> dict:
    key = jax.random.key(seed)
    ks = iter(list(jax.random.split(key, 40)))
    D = D_MODEL

    def nrm(shape, s=1.0):
        return jax.random.normal(next(ks), shape, jnp.float32) * s

    base_if = jnp.concatenate([jnp.zeros((MLSTM_HEADS,), jnp.float32), jnp.linspace(3.0, 6.0, MLSTM_HEADS)])
    return {
        'x_prompt': nrm((BATCH, SEQ, D)),
        'x_sample': nrm((DEC_BATCH, DEC_SEQ, D)),
        'state_mlstm_C': nrm((DEC_BATCH, N_MLSTM, 2, MLSTM_HEADS, MLSTM_DK, MLSTM_DV), 0.02),
        'state_mlstm_n': nrm((DEC_BATCH, N_MLSTM, 2, MLSTM_HEADS, MLSTM_DK), 0.1),
        'state_mlstm_m': nrm((DEC_BATCH, N_MLSTM, 2, MLSTM_HEADS), 0.5),
        'cache_na_k': nrm((DEC_BATCH, N_NA, PAST_LEN, NA_HEADS, NA_HD)),
        'cache_na_v': nrm((DEC_BATCH, N_NA, PAST_LEN, NA_HEADS, NA_HD)),
        'c': nrm((DEC_BATCH, D)),
        'c_ctx': nrm((D,)),
        'w_ada': nrm((DEPTH, D, 9 * D), 0.5 * D ** -0.5),
        'b_ada': nrm((DEPTH, 9 * D), 0.01),
        'norm_g': 1.0 + nrm((DEPTH, 3, D), 0.05),
        'final_g': 1.0 + nrm((D,), 0.05),
        'ffn_w1': nrm((DEPTH, 2, D, D_FF), D ** -0.5),
        'ffn_w3': nrm((DEPTH, 2, D, D_FF), D ** -0.5),
        'ffn_w2': nrm((DEPTH, 2, D_FF, D), D_FF ** -0.5),
        'ml_w_qkv': nrm((N_MLSTM, D, 3 * D), D ** -0.5),
        'ml_w_if': nrm((N_MLSTM, 2, D, 2 * MLSTM_HEADS), 0.1 * D ** -0.5),
        'ml_b_if': base_if + nrm((N_MLSTM, 2, 2 * MLSTM_HEADS), 0.1),
        'ml_w_og': nrm((N_MLSTM, D, D), D ** -0.5),
        'ml_head_g': 1.0 + nrm((N_MLSTM, D), 0.05),
        'ml_w_out': nrm((N_MLSTM, D, D), D ** -0.5),
        'fn_w_out': nrm((N_FOURIER, D, D), D ** -0.5),
        'fn_b_out': nrm((N_FOURIER, D), 0.01),
        'gm_w_in': nrm((N_GMLP, D, 2 * GMLP_WIDTH), D ** -0.5),
        'gm_b_in': nrm((N_GMLP, 2 * GMLP_WIDTH), 0.01),
        'gm_v_g': 1.0 + nrm((N_GMLP, GMLP_WIDTH), 0.05),
        'gm_w_s': nrm((N_GMLP, GMLP_GROUPS, GMLP_CHUNK, GMLP_CHUNK), GMLP_CHUNK ** -0.5),
        'gm_b_s': 1.0 + nrm((N_GMLP, GMLP_GROUPS, GMLP_CHUNK), 0.1),
        'gm_w_out': nrm((N_GMLP, GMLP_WIDTH, D), GMLP_WIDTH ** -0.5),
        'na_w_qkv': nrm((N_NA, D, 3 * D), D ** -0.5),
        'na_w_out': nrm((N_NA, D, D), D ** -0.5),
        'na_rpb': nrm((N_NA, NA_HEADS, 2 * NA_KH - 1, 2 * NA_KW - 1), 0.5),
    }


def reference(x_prompt, x_sample, state_mlstm_C, state_mlstm_n, state_mlstm_m, cache_na_k, cache_na_v, c, c_ctx,
              w_ada, b_ada, norm_g, final_g, ffn_w1, ffn_w3, ffn_w2,
              ml_w_qkv, ml_w_if, ml_b_if, ml_w_og, ml_head_g, ml_w_out,
              fn_w_out, fn_b_out, gm_w_in, gm_b_in, gm_v_g, gm_w_s, gm_b_s, gm_w_out,
              na_w_qkv, na_w_out, na_rpb):
    def run_layer(x, mods, l, mixer):
        h, gate = sub_input(x, norm_g[l, 0], mods, 0)
        x = x + 0.5 * gate * swiglu(h, ffn_w1[l, 0], ffn_w3[l, 0], ffn_w2[l, 0])
        h, gate = sub_input(x, norm_g[l, 1], mods, 1)
        y, aux = mixer(h)
        x = x + gate * y
        h, gate = sub_input(x, norm_g[l, 2], mods, 2)
        x = x + 0.5 * gate * swiglu(h, ffn_w1[l, 1], ffn_w3[l, 1], ffn_w2[l, 1])
        return x, aux

    xp, xs = x_prompt, x_sample
    B = xp.shape[0]
    new_C, new_n, new_m, new_k, new_v = [], [], [], [], []
    for l in range(DEPTH):
        kind, j = l % N_MIXERS, l // N_MIXERS
        mods_p = adaln(c_ctx[None, :], w_ada[l], b_ada[l])
        mods_s = adaln(c, w_ada[l], b_ada[l])
        if kind == 0:
            ml = (ml_w_qkv[j], ml_w_if[j], ml_b_if[j], ml_w_og[j], ml_head_g[j], ml_w_out[j])
            zC = jnp.zeros((B, 2, MLSTM_HEADS, MLSTM_DK, MLSTM_DV), jnp.float32)
            zn = jnp.zeros((B, 2, MLSTM_HEADS, MLSTM_DK), jnp.float32)
            zm = jnp.zeros((B, 2, MLSTM_HEADS), jnp.float32)
            xp, st = run_layer(xp, mods_p, l, lambda h: mlstm_mixer(h, *ml, zC, zn, zm))
            xs, _ = run_layer(xs, mods_s, l, lambda h: mlstm_mixer(
                h, *ml, state_mlstm_C[:, j], state_mlstm_n[:, j], state_mlstm_m[:, j]))
            new_C.append(st[0])
            new_n.append(st[1])
            new_m.append(st[2])
        elif kind == 1:
            fmix = lambda h: (fourier_mixer(h, fn_w_out[j], fn_b_out[j]), None)
            xp, _ = run_layer(xp, mods_p, l, fmix)
            xs, _ = run_layer(xs, mods_s, l, fmix)
        elif kind == 2:
            gmix = lambda h: (gmlp_mixer(h, gm_w_in[j], gm_b_in[j], gm_v_g[j], gm_w_s[j], gm_b_s[j], gm_w_out[j]), None)
            xp, _ = run_layer(xp, mods_p, l, gmix)
            xs, _ = run_layer(xs, mods_s, l, gmix)
        else:
            def na_ctx(h):
                q, k, v = na_qkv(h, na_w_qkv[j])
                return context_attention(q, k, v) @ na_w_out[j], (k, v)

            def na_lat(h):
                q, k, v = na_qkv(h, na_w_qkv[j])
                o = na_latent(q, k, v, cache_na_k[:, j], cache_na_v[:, j], na_rpb[j])
                return o @ na_w_out[j], None

            xp, kv = run_layer(xp, mods_p, l, na_ctx)
            xs, _ = run_layer(xs, mods_s, l, na_lat)
            new_k.append(kv[0])
            new_v.append(kv[1])
    return (rmsnorm(xp, final_g), rmsnorm(xs, final_g), jnp.stack(new_C, axis=1), jnp.stack(new_n, axis=1),
            jnp.stack(new_m, axis=1), jnp.stack(new_k, axis=1), jnp.stack(new_v, axis=1))
```

```python
import numpy as np
import ml_dtypes
import concourse.bass as bass
import concourse.mybir as mybir
from concourse.bass_utils import run_bass_kernel_spmd

F32 = mybir.dt.float32
BF16 = mybir.dt.bfloat16
AF = mybir.ActivationFunctionType
ALU = mybir.AluOpType
AX = mybir.AxisListType

L = 4
D = 1024
DFF = 2816
KC = 8
FC = 22
TP = 1024
TS = 2048
T = TP + TS
TT = 512
NT = T // TT
NPS = 4
EPS = 1e-6
NEG = -30000.0


class Sched:
    CE = ('pe', 'act', 'dve', 'pool')
    QE = ('pe', 'act', 'dve', 'pool', 'sp')

    SELF_WAIT = True

    def __init__(self, nc, n_dma_sems=32):
        self.nc = nc
        self.ops = {e: [] for e in self.QE}
        self.cnt = {e: 0 for e in self.CE}
        self.known = {e: {} for e in self.QE}
        self.last_w = {}
        self.readers = {}
        self.n_dma = n_dma_sems
        self.dma_val = [0] * n_dma_sems
        self.dma_rr = 0
        self.n_ops = 0
        self.n_waits = 0

    def op(self, eng, fn, reads=(), writes=(), dma=False):
        need = {}

        def add(tok):
            s, v = tok
            if need.get(s, 0) < v:
                need[s] = v
        for r in reads:
            t = self.last_w.get(r)
            if t:
                add(t)
        for w in writes:
            t = self.last_w.get(w)
            if t:
                add(t)
            for t in self.readers.get(w, ()):
                add(t)
        if dma:
            i = self.dma_rr
            self.dma_rr = (self.dma_rr + 1) % self.n_dma
            sname = ('dma', i)
            if self.dma_val[i] > 0 and need.get(sname, 0) < self.dma_val[i]:
                need[sname] = self.dma_val[i]
            self.dma_val[i] += 16
            tok = (sname, self.dma_val[i])
        else:
            self.cnt[eng] += 1
            tok = (eng, self.cnt[eng])
        waits = []
        for s, v in need.items():
            if s == eng and (eng == 'pe' or not self.SELF_WAIT):
                continue
            if self.known[eng].get(s, 0) < v:
                waits.append((s, v))
                self.known[eng][s] = v
        self.ops[eng].append((waits, fn, tok))
        self.n_ops += 1
        self.n_waits += len(waits)
        for r in reads:
            self.readers.setdefault(r, []).append(tok)
        for w in writes:
            self.last_w[w] = tok
            self.readers[w] = []
        return tok

    def _all_toks(self):
        toks = [(e, self.cnt[e]) for e in self.CE if self.cnt[e] > 0]
        toks += [(('dma', i), v) for i, v in enumerate(self.dma_val) if v > 0]
        return toks

    def barrier(self):
        toks = self._all_toks()
        for e in self.QE:
            waits = []
            for s, v in toks:
                if s == e:
                    continue
                if self.known[e].get(s, 0) < v:
                    waits.append((s, v))
                    self.known[e][s] = v
            if waits:
                self.ops[e].append((waits, None, None))
                self.n_waits += len(waits)
        self.last_w = {}
        self.readers = {}

    def final_wait(self, eng='sp'):
        waits = [(s, v) for s, v in self._all_toks() if s != eng and self.known[eng].get(s, 0) < v]
        self.ops[eng].append((waits, None, None))

    def emit(self):
        nc = self.nc
        from contextlib import ExitStack
        with ExitStack() as es:
            sems = {}
            for e in self.CE:
                sems[e] = es.enter_context(nc.semaphore("s_" + e))
            for i in range(self.n_dma):
                sems[('dma', i)] = es.enter_context(nc.semaphore("s_dma%d" % i))
            block = es.enter_context(nc.Block())

            def mk(ename):
                def run(eng):
                    for waits, fn, tok in self.ops[ename]:
                        for s, v in waits:
                            eng.wait_ge(sems[s], v)
                        if fn is None:
                            continue
                        ins = fn(eng)
                        if tok[0] == ename:
                            ins.then_inc(sems[tok[0]], 1)
                        else:
                            ins.then_inc(sems[tok[0]], 16)
                return run
            block.tensor(mk('pe'))
            block.scalar(mk('act'))
            block.vector(mk('dve'))
            block.gpsimd(mk('pool'))
            block.sync(mk('sp'))


class Mem:
    def __init__(self, nc, base=16512, limit=229344):
        self.nc = nc
        self.top = base
        self.limit = limit
        self.n = 0
        self.marks = []
        self.peak = base

    def alloc(self, shape, dtype, name=None):
        sz = int(np.prod(shape[1:])) * (2 if dtype == BF16 else 4)
        sz = (sz + 63) // 64 * 64
        off = self.top
        self.top += sz
        self.peak = max(self.peak, self.top)
        assert self.top <= self.limit, "SBUF overflow %d > %d" % (self.top, self.limit)
        self.n += 1
        return self.nc.alloc_sbuf_tensor_at("%s_%d" % (name or "t", self.n), list(shape), dtype, offset=off)

    def mark(self):
        self.marks.append(self.top)

    def release(self):
        self.top = self.marks.pop()


class RR:
    def __init__(self, items):
        self.items = list(items)
        self.i = 0

    def next(self):
        v = self.items[self.i]
        self.i = (self.i + 1) % len(self.items)
        return v


class _Stop(Exception):
    pass


class Builder:
    def __init__(self, dumps=(), layers=L, do_mixer=True, do_ffn=True, layer_list=None):
        self.nc = nc = bass.Bass("TRN2", target_bir_lowering=False)
        self.S = Sched(nc)
        self.M = Mem(nc)
        self.ins = {}
        self.outs = {}
        self.dumps = set(dumps)
        self.layers = layers
        self.layer_list = list(layer_list) if layer_list is not None else list(range(layers))
        self.do_mixer = do_mixer
        self.do_ffn = do_ffn
        self.ps = nc.alloc_psum_tensor("ps", [128, 8, 512], F32)
        self.psb = self.ps.bitcast(BF16)
        self.bank_a = RR([0, 1])
        self.bank_b = RR([2, 3])
        self.bank_y = RR([4, 5, 6, 7])
        self.bank_m = RR([6, 7])
        self.bank_any = RR(range(8))
        self.uid = 0

    def din(self, name, shape, dtype=F32):
        self.ins[name] = self.nc.dram_tensor(name, list(shape), dtype, kind="ExternalInput").ap()
        return self.ins[name]

    def dout(self, name, shape, dtype=F32):
        self.outs[name] = self.nc.dram_tensor(name, list(shape), dtype, kind="ExternalOutput").ap()
        return self.outs[name]

    def key(self, base):
        self.uid += 1
        return (base, self.uid)

    def dma(self, out, in_, reads=(), writes=(), q='sp'):
        return self.S.op(q, lambda e: e.dma_start(out=out, in_=in_), reads=reads, writes=writes, dma=True)

    def declare(self):
        din = self.din
        din("xT", [128, KC, T])
        din("condT", [128, KC, 2])
        din("b_adaT", [128, L, 72])
        din("norm_gT", [128, L, 3, KC])
        din("final_gT", [128, KC])
        din("cf32", [128, 9, 128])
        din("cbf16", [128, 2, 128], BF16)
        din("w_ada", [L, D, 9 * D])
        din("ffn_w1", [L, 2, D, DFF])
        din("ffn_w3", [L, 2, D, DFF])
        din("ffn_w2", [L, 2, DFF, D])
        din("ml_w_qkv", [D, 3 * D])
        din("ml_w_og", [D, D])
        din("ml_w_out", [D, D])
        din("ml_w_if", [D, 16])
        din("ml_b_if", [1, 16])
        din("ml_head_gT", [128, KC])
        din("ml_caug0", [2, 4, 256, 257])
        din("ml_m0", [1, 8])
        self.dout("caug_out", [NPS, 2, 4, 256, 257])
        self.dout("m_out", [NPS, 8])
        din("na_w_qkv", [D, 3 * D])
        din("na_w_out", [D, D])
        din("cache_k", [512, D])
        din("cache_v", [512, D])
        din("na_rexp", [16, 15, 64, 64])
        din("na_cm", [128, 256])
        din("na_rowsel", [32, 16, 128], BF16)
        din("na_rm", [32, TS], BF16)
        self.dout("k_out", [TP, D])
        self.dout("v_out", [TP, D])
        din("fn_w_out", [D, D])
        din("fn_b_outT", [128, KC])
        din("dft256", [128, 3, 2, 256], BF16)
        din("dft2048", [2, 2048, 2048], BF16)
        din("gm_w_in", [D, 2 * D])
        din("gm_w_out", [D, D])
        din("gm_b_inT", [128, 16])
        din("gm_b_in", [1, 2 * D])
        din("gm_v_g", [1, D])
        din("gm_w_sT", [128, 4, 128])
        din("gm_b_s", [1, 512])
        self.dout("yT", [128, KC, T])
        for d in sorted(self.dumps):
            if d.startswith("l") or d == "x0":
                self.dout("dbg_" + d, [128, KC, T])

    def dump_tile(self, name, tile, shape, reads, dtype=F32):
        if name not in self.dumps:
            return
        o = self.nc.dram_tensor("dbg_" + name, list(shape), dtype, kind="ExternalOutput").ap()
        self.outs["dbg_" + name] = o
        self.dma(o, tile, reads=reads)

    def dump(self, name):
        if name not in self.dumps:
            return
        o = self.outs["dbg_" + name]
        for tt in range(NT):
            sl = slice(tt * TT, (tt + 1) * TT)
            self.dma(o[:, :, sl], self.xT[:, :, sl], reads=[('x', k, tt) for k in range(KC)])

    def prologue(self):
        S, M, ins = self.S, self.M, self.ins
        self.xT = M.alloc([128, KC, T], F32, "xT")
        self.cf = M.alloc([128, 2, 128], F32, "cf")
        self.cb = M.alloc([128, 2, 128], BF16, "cb")
        self.mods = M.alloc([128, L, 72, 2], F32, "mods")
        self.modA = M.alloc([128, L, 3, 2, KC], F32, "modA")
        self.modC = M.alloc([128, L, 3, 2, KC], F32, "modC")
        self.badaT = M.alloc([128, L, 72], F32, "bada")
        self.ngT = M.alloc([128, L, 3, KC], F32, "ng")
        self.fgT = M.alloc([128, KC], F32, "fg")
        condT = M.alloc([128, KC, 2], F32, "cond")
        sc = M.alloc([128, KC, 2], BF16, "sc")
        self.ones_f = self.cf[:, 0, :]
        self.id_f = self.cf[:, 1, :]
        self.id_b = self.cb[:, 0, :]
        self.ones_b = self.cb[:, 1, :]
        self.dma(self.cf[:], ins["cf32"][:, 0:2, :], writes=['cf'])
        self.dma(self.cb[:], ins["cbf16"], writes=['cb'])
        self.dma(self.badaT[:], ins["b_adaT"], writes=['bada'])
        self.dma(self.ngT[:], ins["norm_gT"], writes=['ng'])
        self.dma(self.fgT[:], ins["final_gT"], writes=['fg'])
        self.dma(condT[:], ins["condT"], writes=['cond'])
        for tt in range(NT):
            sl = slice(tt * TT, (tt + 1) * TT)
            self.dma(self.xT[:, :, sl], ins["xT"][:, :, sl], writes=[('x', k, tt) for k in range(KC)])
        S.op('act', lambda e: e.activation(out=sc[:], in_=condT[:], func=AF.Silu), reads=['cond'], writes=['sc'])
        M.mark()
        wb = [M.alloc([128, KC, 1024], BF16, "wada") for _ in range(2)]
        ps = self.ps
        it = 0
        self.sc = sc
        for l in self.layer_list[:1]:
            wv = ins["w_ada"][l].rearrange("(k p) n -> p k n", p=128)
            for s in range(9):
                b = it % 2
                it += 1
                self.dma(wb[b][:], wv[:, :, s * 1024:(s + 1) * 1024], writes=[('wada', b)], q='pool')
                bank = self.bank_m.next()

                def mm(e, b=b, bank=bank):
                    for oc in range(8):
                        for k in range(KC):
                            ins_ = e.matmul(ps[:, bank, oc * 2:oc * 2 + 2], lhsT=wb[b][:, k, oc * 128:(oc + 1) * 128],
                                            rhs=sc[:, k, :], start=(k == 0 and oc == 0), stop=(k == KC - 1), skip_group_check=True)
                    return ins_
                S.op('pe', mm, reads=[('wada', b), 'sc'], writes=[('ps', bank)])
                for j in range(2):
                    S.op('dve', (lambda e, bank=bank, j=j, l=l, s=s: e.tensor_tensor(
                        out=self.mods[:, l, s * 8:(s + 1) * 8, j],
                        in0=ps[:, bank, 0:16].rearrange("p (c j) -> p c j", j=2)[:, :, j],
                        in1=self.badaT[:, l, s * 8:(s + 1) * 8], op=ALU.add)),
                        reads=[('ps', bank), 'bada'], writes=[('mods', l)])
        M.release()
        self.derive_mods(self.layer_list[0])

    def derive_mods(self, l):
        S = self.S
        if True:
            for i in range(3):
                for j in range(2):
                    S.op('dve', (lambda e, l=l, i=i, j=j: e.scalar_tensor_tensor(
                        out=self.modA[:, l, i, j, :], in0=self.mods[:, l, (3 * i + 1) * 8:(3 * i + 2) * 8, j], scalar=1.0,
                        in1=self.ngT[:, l, i, :], op0=ALU.add, op1=ALU.mult)),
                        reads=[('mods', l), 'ng'], writes=[('modA', l)])
                    S.op('dve', (lambda e, l=l, i=i, j=j: e.tensor_scalar(
                        out=self.modC[:, l, i, j, :], in0=self.mods[:, l, (3 * i + 2) * 8:(3 * i + 3) * 8, j],
                        scalar1=(1.0 if i == 1 else 0.5), scalar2=None, op0=ALU.mult)),
                        reads=[('mods', l)], writes=[('modC', l)])

    def shiftAP(self, l, i, j, k):
        return self.mods[:, l, 3 * i * 8 + k, j:j + 1]

    def alloc_norm_scratch(self, n=TT, share_tmp=None):
        M = self.M
        self.sq = M.alloc([128, KC, n], BF16, "sq")
        self.rstd = M.alloc([128, n], F32, "rstd")
        if share_tmp is not None:
            self.ntmp = share_tmp
        else:
            self.ntmp = [M.alloc([128, n], F32, "ntmp") for _ in range(2)]
        self.ntmp_rr = RR([0, 1])

    def xkeys(self, t0, n):
        return [('x', k, tt) for k in range(KC) for tt in range(t0 // TT, (t0 + n - 1) // TT + 1)]

    def rstd_range(self, t0, n):
        S, ps = self.S, self.ps
        sq, rstd, xT, ones_b, epsT = self.sq, self.rstd, self.xT, self.ones_b, self.epsT
        sl = slice(t0, t0 + n)
        S.op('act', lambda e: e.activation(out=sq[:, :, 0:n], in_=xT[:, :, sl], func=AF.Square), reads=self.xkeys(t0, n), writes=['sq'])
        bank = self.bank_m.next()
        self.mm(ps[:, bank, 0:n], [(ones_b, sq[:, k, 0:n]) for k in range(KC)], ['sq', 'cb'], bank)
        S.op('act', lambda e: e.activation(out=rstd[:, 0:n], in_=ps[:, bank, 0:n], func=AF.Sqrt, scale=1.0 / D, bias=epsT[:, 0:1]),
             reads=[('ps', bank), 'eps'], writes=['rstd'])
        S.op('dve', lambda e: e.reciprocal(out=rstd[:, 0:n], in_=rstd[:, 0:n]), reads=['rstd'], writes=['rstd'])

    def rstd_tile(self, tt):
        self.rstd_range(tt * TT, TT)

    def norm_mod(self, l, i, tts, hT, hoff=0):
        for tt in tts:
            self.norm_mod_range(l, i, tt * TT, TT, hT, tt * TT - hoff, [('h', k, tt) for k in range(KC)])

    def norm_mod_range(self, l, i, t0, n, hT, h0, hkeys, kset=None):
        S = self.S
        j = 0 if t0 < TP else 1
        sl = slice(t0, t0 + n)
        hsl = slice(h0, h0 + n)
        self.rstd_range(t0, n)
        ntmp, xT, modA, rstd, mods = self.ntmp, self.xT, self.modA, self.rstd, self.mods
        xk = self.xkeys(t0, n)
        for k in (range(KC) if kset is None else kset):
            tb = self.ntmp_rr.next()
            S.op('dve', (lambda e, k=k, tb=tb: e.scalar_tensor_tensor(
                out=ntmp[tb][:, 0:n], in0=xT[:, k, sl], scalar=modA[:, l, i, j, k:k + 1], in1=rstd[:, 0:n],
                op0=ALU.mult, op1=ALU.mult)), reads=[xk[kk] for kk in range(len(xk)) if xk[kk][1] == k] + ['rstd', ('modA', l)],
                writes=[('ntmp', tb)])
            S.op('act', (lambda e, k=k, tb=tb: e.activation(
                out=hT[:, k, hsl], in_=ntmp[tb][:, 0:n], func=AF.Identity, bias=mods[:, l, 3 * i * 8 + k, j:j + 1])),
                reads=[('ntmp', tb), ('mods', l)], writes=[hkeys[k]])

    def mm(self, out, pairs, reads, bank):
        pairs = list(pairs)

        def f(e):
            n = len(pairs)
            for q, (lt, r) in enumerate(pairs):
                ins = e.matmul(out, lhsT=lt, rhs=r, start=(q == 0), stop=(q == n - 1))
            return ins
        return self.S.op('pe', f, reads=reads, writes=[('ps', bank)])

    def xupdate(self, l, si, t0, n, oc, psrc, reads):
        j = 0 if t0 < TP else 1
        xT, modC = self.xT, self.modC
        sl = slice(t0, t0 + n)
        xk = [('x', oc, tt) for tt in range(t0 // TT, (t0 + n - 1) // TT + 1)]
        self.S.op('dve', lambda e: e.scalar_tensor_tensor(
            out=xT[:, oc, sl], in0=psrc, scalar=modC[:, l, si, j, oc:oc + 1], in1=xT[:, oc, sl],
            op0=ALU.mult, op1=ALU.add), reads=list(reads) + xk + [('modC', l)], writes=xk)

    G = 2

    def ffn_groups(self):
        return [(c0, min(self.G, FC - c0)) for c0 in range(0, FC, self.G)]

    def alloc_ffn(self):
        M = self.M
        G = self.G
        self.w1b = [M.alloc([128, KC, G * 128], BF16, "w1b") for _ in range(2)]
        self.w3b = [M.alloc([128, KC, G * 128], BF16, "w3b") for _ in range(2)]
        self.w2b = [M.alloc([128, G, D], BF16, "w2b") for _ in range(2)]
        self.ub = [M.alloc([128, G, TT], BF16, "ub") for _ in range(2)]
        self.slb = [M.alloc([128, TT], F32, "ntmp") for _ in range(2)]
        self.wbuf_i = 0

    def ffn_load(self, l, i, gi):
        c0, g = self.ffn_groups()[gi]
        b = self.wbuf_i % 2
        self.wbuf_i += 1
        ins = self.ins
        w1v = ins["ffn_w1"][l, i].rearrange("(k p) n -> p k n", p=128)
        w3v = ins["ffn_w3"][l, i].rearrange("(k p) n -> p k n", p=128)
        w2v = ins["ffn_w2"][l, i].rearrange("(g p) n -> p g n", p=128)
        self.dma(self.w1b[b][:, :, 0:g * 128], w1v[:, :, c0 * 128:(c0 + g) * 128], writes=[('w1', b)], q='pool')
        self.dma(self.w3b[b][:, :, 0:g * 128], w3v[:, :, c0 * 128:(c0 + g) * 128], writes=[('w3', b)], q='pool')
        self.dma(self.w2b[b][:, 0:g, :], w2v[:, c0:c0 + g, :], writes=[('w2', b)], q='pool')
        return b

    def ada_unit(self, ln, su):
        S, ps, ins, sc = self.S, self.ps, self.ins, self.sc
        wv = ins["w_ada"][ln].rearrange("(k p) n -> p k n", p=128)
        nb = len(self.aslab)
        if su == 0:
            for q in range(min(2, 36)):
                self.dma(self.aslab[q % nb][:], wv[:, :, q * 256:(q + 1) * 256], writes=[('aslab', q % nb)], q='pool')
        if su + 2 < 36:
            q = su + 2
            self.dma(self.aslab[q % nb][:], wv[:, :, q * 256:(q + 1) * 256], writes=[('aslab', q % nb)], q='pool')
        b = su % nb
        wb = self.aslab[b]
        bank = self.bank_m.next()

        def mm(e):
            for oc in range(2):
                for k in range(KC):
                    r = e.matmul(ps[:, bank, oc * 2:oc * 2 + 2], lhsT=wb[:, k, oc * 128:(oc + 1) * 128], rhs=sc[:, k, :],
                                 start=(k == 0 and oc == 0), stop=(k == KC - 1), skip_group_check=True)
            return r
        S.op('pe', mm, reads=[('aslab', b), 'sc'], writes=[('ps', bank)])
        mods, badaT = self.mods, self.badaT
        for j in range(2):
            S.op('dve', (lambda e, j=j: e.tensor_tensor(
                out=mods[:, ln, su * 2:(su + 1) * 2, j], in0=ps[:, bank, 0:4].rearrange("p (c j) -> p c j", j=2)[:, :, j],
                in1=badaT[:, ln, su * 2:(su + 1) * 2], op=ALU.add)), reads=[('ps', bank), 'bada'], writes=[('mods', ln)])

    def ffn(self, l, i, hT, first_buf, norm_i=None, ada_next=None):
        S, ps = self.S, self.ps
        groups = self.ffn_groups()
        si = 0 if i == 0 else 2
        w1b, w3b, w2b, ubuf, slb, xT, modC = self.w1b, self.w3b, self.w2b, self.ub, self.slb, self.xT, self.modC
        nxt = first_buf
        ucnt = [0]

        def U_part(b, g, tt, ub, jj):
            sl = slice(tt * TT, (tt + 1) * TT)
            ba = self.bank_a.next()
            bb = self.bank_b.next()
            hk = [('h', k, tt) for k in range(KC)]
            self.mm(ps[:, ba, :], [(w1b[b][:, k, jj * 128:(jj + 1) * 128], hT[:, k, sl]) for k in range(KC)], [('w1', b)] + hk, ba)
            self.mm(ps[:, bb, :], [(w3b[b][:, k, jj * 128:(jj + 1) * 128], hT[:, k, sl]) for k in range(KC)], [('w3', b)] + hk, bb)
            sb = self.sl_rr.next()
            S.op('act', lambda e: e.activation(out=slb[sb][:], in_=ps[:, ba, :], func=AF.Silu), reads=[('ps', ba)], writes=[('ntmp', sb)])
            S.op('dve', lambda e: e.tensor_tensor(out=ubuf[ub][:, jj, :], in0=slb[sb][:], in1=ps[:, bb, :], op=ALU.mult),
                 reads=[('ntmp', sb), ('ps', bb)], writes=[('u', ub, jj)])

        def Y_part(b, g, tt, ub, oc):
            j = 0 if tt < TP // TT else 1
            sl = slice(tt * TT, (tt + 1) * TT)
            by = self.bank_y.next()
            self.mm(ps[:, by, :], [(w2b[b][:, jj, oc * 128:(oc + 1) * 128], ubuf[ub][:, jj, :]) for jj in range(g)],
                    [('w2', b)] + [('u', ub, jj) for jj in range(g)], by)
            S.op('dve', lambda e: e.scalar_tensor_tensor(
                out=xT[:, oc, sl], in0=ps[:, by, :], scalar=modC[:, l, si, j, oc:oc + 1], in1=xT[:, oc, sl],
                op0=ALU.mult, op1=ALU.add), reads=[('ps', by), ('x', oc, tt), ('modC', l)], writes=[('x', oc, tt)])

        ada_cnt = [0]
        for gi, (c0, g) in enumerate(groups):
            b = nxt
            if gi + 1 < len(groups):
                nxt = self.ffn_load(l, i, gi + 1)
            prev = None
            for step in range(NT + 1):
                cur = None
                ub = ucnt[0] % 2
                if step < NT:
                    ucnt[0] += 1
                    cur = (step, ub)
                nparts = g if step < NT else 1
                per = (KC + nparts - 1) // nparts
                if ada_next is not None and ada_cnt[0] < 36 and (gi * (NT + 1) + step) % 2 == 0:
                    self.ada_unit(ada_next, ada_cnt[0])
                    ada_cnt[0] += 1
                if norm_i is not None and gi == 0 and step + 1 < NT:
                    self.norm_mod(l, norm_i, [step + 1], hT)
                for part in range(nparts):
                    if step < NT:
                        U_part(b, g, step, ub, part)
                    if prev is not None:
                        for oc in range(part * per, min(KC, (part + 1) * per)):
                            Y_part(b, g, prev[0], prev[1], oc)
                prev = cur
        if ada_next is not None:
            while ada_cnt[0] < 36:
                self.ada_unit(ada_next, ada_cnt[0])
                ada_cnt[0] += 1
            self.derive_mods(ada_next)

    def final(self):
        M = self.M
        M.mark()
        self.alloc_norm_scratch()
        yb = [M.alloc([128, KC, TT], F32, "yb") for _ in range(2)]
        for tt in range(NT):
            self.final_tile(tt, yb)
        M.release()

    def final_tile(self, tt, yb):
        S = self.S
        sl = slice(tt * TT, (tt + 1) * TT)
        self.rstd_tile(tt)
        b = tt % 2
        xT, fgT, rstd = self.xT, self.fgT, self.rstd
        for k in range(KC):
            S.op('dve', (lambda e, k=k: e.scalar_tensor_tensor(
                out=yb[b][:, k, :], in0=xT[:, k, sl], scalar=fgT[:, k:k + 1], in1=rstd[:],
                op0=ALU.mult, op1=ALU.mult)), reads=[('x', k, tt), 'rstd', 'fg'], writes=[('yb', b, k)])
        self.dma(self.outs["yT"][:, :, sl], yb[b][:], reads=[('yb', b, k) for k in range(KC)])

    def build(self):
        S, M = self.S, self.M
        self.declare()
        self.epsT = M.alloc([128, 1], F32, "eps")
        S.op('dve', lambda e: e.memset(self.epsT[:], EPS), writes=['eps'])
        self.prologue()
        S.barrier()
        self.dump("x0")
        self.dump_tile("mods", self.mods[:], [128, L, 72, 2], [('mods', l) for l in range(L)])
        self.dump_tile("modA", self.modA[:], [128, L, 3, 2, KC], [('modA', l) for l in range(L)])
        self.dump_tile("modC", self.modC[:], [128, L, 3, 2, KC], [('modC', l) for l in range(L)])
        for l in self.layer_list:
            for i in range(3):
                if i == 1:
                    if self.do_mixer:
                        self.mixer(l)
                    self.dump("l%d_b" % l)
                    continue
                if not self.do_ffn:
                    continue
                M.mark()
                hT = M.alloc([128, KC, T], BF16, "hT")
                self.alloc_ffn()
                self.alloc_norm_scratch(TT, share_tmp=self.slb)
                self.sl_rr = self.ntmp_rr
                li = self.layer_list.index(l)
                ada_next = self.layer_list[li + 1] if (i == 2 and li + 1 < len(self.layer_list)) else None
                if ada_next is not None:
                    self.aslab = [M.alloc([128, KC, 256], BF16, "aslab") for _ in range(3)]
                fi = 0 if i == 0 else 1
                first = self.ffn_load(l, fi, 0)
                self.norm_mod(l, i, [0], hT)
                self.ffn(l, fi, hT, first, norm_i=i, ada_next=ada_next)
                M.release()
                S.barrier()
                self.dump("l%d_%s" % (l, "a" if i == 0 else "c"))
        self.final()
        S.final_wait('sp')
        S.emit()
        return self.nc

    def mixer(self, l):
        kind = l % 4
        if kind == 2:
            self.mixer_gmlp(l)
        elif kind == 1:
            self.mixer_fourier(l)
        elif kind == 3:
            self.mixer_na(l)
        elif kind == 0:
            nm = len(self.M.marks)
            try:
                if getattr(self, "old_mlstm", False):
                    self.mixer_mlstm(l)
                else:
                    self.mixer_mlstm2(l)
            except _Stop:
                while len(self.M.marks) > nm:
                    self.M.release()
        self.S.barrier()

    def mixer_mlstm2(self, l):
        S, M, ins, ps, psb = self.S, self.M, self.ins, self.ps, self.psb
        ones_f, id_f, id_b, epsT = self.ones_f, self.id_f, self.id_b, self.epsT
        wqkv_d = ins["ml_w_qkv"].rearrange("(k p) n -> p k n", p=128)
        wog_d = ins["ml_w_og"].rearrange("(k p) n -> p k n", p=128)
        wout_d = ins["ml_w_out"].rearrange("(k p) n -> p k n", p=128)
        M.mark()
        cfm = M.alloc([128, 3, 128], F32, "mcfm")
        self.dma(cfm[:, 0:2, :], ins["cf32"][:, 2:4, :], writes=['mcfm'])
        self.dma(cfm[:, 2, :], ins["cf32"][:, 8, :], writes=['mcfm'])
        Jf = cfm[:, 2, :]
        Wif = M.alloc([128, KC, 16], BF16, "mWif")
        self.dma(Wif[:], ins["ml_w_if"].rearrange("(k p) n -> p k n", p=128), writes=['mWif'], q='pool')
        bif = M.alloc([128, 16], F32, "mbif")
        self.dma(bif[:], ins["ml_b_if"].partition_broadcast(128), writes=['mbif'])
        hgT = M.alloc([128, KC], F32, "mhg")
        self.dma(hgT[:], ins["ml_head_gT"], writes=['mhg'])
        m0b = M.alloc([128, 8], F32, "mm0")
        self.dma(m0b[:], ins["ml_m0"].partition_broadcast(128), writes=['mm0'])
        uid = [0]

        def K_(name):
            uid[0] += 1
            return (name, uid[0])

        def do_group(grp):
            M.mark()
            g0, gn = (0, TP) if grp == 0 else (TP, TS)
            NCH = gn // 128
            NCOL = NCH * 8
            nseq, cps = (NPS, 2) if grp == 0 else (1, 16)
            hT = M.alloc([128, KC, gn], BF16, "mhT")
            hk = [('mh', k, tq) for k in range(KC) for tq in range(gn // 256)]
            Uu = M.alloc([128, NCH, 8], F32, "mU")
            Wi = M.alloc([128, NCH, 8], F32, "mWi")
            Fl = M.alloc([128, NCH, 8], F32, "mFl")
            Ws = M.alloc([128, NCH, 8], F32, "mWs")
            Dc = M.alloc([128, NCH, 8], F32, "mDc")
            mfin = M.alloc([128, nseq, 8], F32, "mfin")
            M.mark()
            self.alloc_norm_scratch(256)
            for tq in range(gn // 256):
                self.norm_mod_range(l, 1, g0 + tq * 256, 256, hT, tq * 256, [('mh', k, tq) for k in range(KC)])
            S.barrier()
            M.release()
            M.mark()
            gpre = M.alloc([128, NCH, 16], F32, "mgpre")
            gi = M.alloc([128, NCH, 8], F32, "mgi")
            gl = M.alloc([128, NCH, 8], F32, "mgl")
            bb = M.alloc([128, NCH, 8], F32, "mb")
            bt = M.alloc([128, NCH, 8], F32, "mbt")
            gg = M.alloc([128, NCH, 8], F32, "mg")
            cmx = M.alloc([128, NCH, 8], F32, "mcmx")
            gmx = M.alloc([128, NCH, 8], F32, "mgmx")
            MP = M.alloc([128, NCH, 8], F32, "mMP")
            ML = M.alloc([128, NCH, 8], F32, "mML")
            Mt = M.alloc([128, NCH, 8], F32, "mMt")
            tq_ = M.alloc([128, NCH, 8], F32, "mtq")
            Rf = M.alloc([128, 128], F32, "mRf")
            Rb = M.alloc([128, 128], F32, "mRb")
            Cf = M.alloc([128, 128], F32, "mCf")
            Cb_ = M.alloc([128, 128], F32, "mCb")
            Tb = M.alloc([128, 128], F32, "mTb")
            rmax = M.alloc([128, 1], F32, "mrmax")
            dg = M.alloc([128, 128], F32, "mdg")
            for c8 in range(0, NCH, 8):
                bank = self.bank_any.next()

                def fg(e, bank=bank, c8=c8):
                    for ci in range(8):
                        ch = c8 + ci
                        for k in range(KC):
                            r = e.matmul(ps[:, bank, ci * 16:(ci + 1) * 16], lhsT=hT[:, k, ch * 128:(ch + 1) * 128], rhs=Wif[:, k, :],
                                         start=(k == 0 and ci == 0), stop=(k == KC - 1), skip_group_check=True)
                    return r
                S.op('pe', fg, reads=hk + ['mWif'], writes=[('ps', bank)])
                for ci in range(8):
                    S.op('dve', (lambda e, bank=bank, ci=ci, c8=c8: e.tensor_tensor(out=gpre[:, c8 + ci, :], in0=ps[:, bank, ci * 16:(ci + 1) * 16], in1=bif[:], op=ALU.add)),
                         reads=[('ps', bank), 'mbif'], writes=['mgpre'])
            gpv = gpre[:].rearrange("p c (d x) -> p c d x", d=2)
            v4 = lambda t: t[:].rearrange("p c (d h) -> p c d h", d=2)
            S.op('dve', lambda e: e.tensor_copy(out=v4(gi), in_=gpv[:, :, :, 0:4]), reads=['mgpre'], writes=['mgi'])
            S.op('act', lambda e: e.activation(out=v4(gl), in_=gpv[:, :, :, 4:8], func=AF.Exp, scale=-1.0), reads=['mgpre'], writes=['mgl'])
            S.op('act', lambda e: e.activation(out=gl[:], in_=gl[:], func=AF.Ln, bias=1.0), reads=['mgl'], writes=['mgl'])
            S.op('dve', lambda e: e.tensor_scalar(out=gl[:], in0=gl[:], scalar1=-1.0, scalar2=None, op0=ALU.mult), reads=['mgl'], writes=['mgl'])
            for c8 in range(0, NCH, 8):
                bank = self.bank_any.next()

                def fc(e, bank=bank, c8=c8):
                    for ci in range(8):
                        ch = c8 + ci
                        e.matmul(ps[:, bank, ci * 16:ci * 16 + 4], lhsT=cfm[:, 0, :], rhs=gl[:, ch, 0:4], start=True, stop=True)
                        e.matmul(ps[:, bank, ci * 16 + 4:ci * 16 + 8], lhsT=cfm[:, 1, :], rhs=gl[:, ch, 4:8], start=True, stop=True)
                        r = e.matmul(ps[:, bank, ci * 16 + 8:ci * 16 + 16], lhsT=ones_f, rhs=gl[:, ch, :], start=True, stop=True)
                    return r
                S.op('pe', fc, reads=['mgl', 'mcfm', 'cf'], writes=[('ps', bank)])
                pv = ps[:, bank, 0:128].rearrange("p (c x) -> p c x", x=16)
                S.op('dve', (lambda e, pv=pv, c8=c8: e.tensor_copy(out=bb[:, c8:c8 + 8, :], in_=pv[:, :, 0:8])), reads=[('ps', bank)], writes=['mb'])
                S.op('dve', (lambda e, pv=pv, c8=c8: e.tensor_copy(out=bt[:, c8:c8 + 8, :], in_=pv[:, :, 8:16])), reads=[('ps', bank)], writes=['mbt'])
            S.op('dve', lambda e: e.tensor_tensor(out=gg[:], in0=gi[:], in1=bb[:], op=ALU.subtract), reads=['mgi', 'mb'], writes=['mg'])
            ggf = gg[:].rearrange("p c k -> p (c k)")
            bankf = self.bank_any.next()
            self.mm(ps[0:NCOL, bankf, 0:128], [(ggf, id_f)], ['mg', 'cf'], bankf)
            bankb = self.bank_any.next()
            self.mm(ps[0:NCOL, bankb, 0:128], [(ggf, Jf)], ['mg', 'mcfm'], bankb)
            S.op('act', lambda e: e.activation(out=Rf[0:NCOL, :], in_=ps[0:NCOL, bankf, 0:128], func=AF.Identity), reads=[('ps', bankf)], writes=['mRf'])
            S.op('act', lambda e: e.activation(out=Rb[0:NCOL, :], in_=ps[0:NCOL, bankb, 0:128], func=AF.Identity), reads=[('ps', bankb)], writes=['mRb'])
            S.op('dve', lambda e: e.tensor_tensor_scan(out=Cf[0:NCOL, :], data0=Rf[0:NCOL, :], data1=Rf[0:NCOL, :], initial=-1e30, op0=ALU.max, op1=ALU.max),
                 reads=['mRf'], writes=['mCf'])
            S.op('dve', lambda e: e.tensor_tensor_scan(out=Cb_[0:NCOL, :], data0=Rb[0:NCOL, :], data1=Rb[0:NCOL, :], initial=-1e30, op0=ALU.max, op1=ALU.max),
                 reads=['mRb'], writes=['mCb'])
            S.op('dve', lambda e: e.tensor_reduce(out=rmax[0:NCOL, :], in_=Rf[0:NCOL, :], axis=AX.X, op=ALU.max), reads=['mRf'], writes=['mrmax'])
            bank1 = self.bank_any.next()
            self.mm(ps[:, bank1, 0:NCOL], [(Cf[0:NCOL, :], id_f[0:NCOL, 0:NCOL])], ['mCf', 'cf'], bank1)
            bank2 = self.bank_any.next()
            self.mm(ps[:, bank2, 0:NCOL], [(Cb_[0:NCOL, :], id_f[0:NCOL, 0:NCOL])], ['mCb', 'cf'], bank2)
            S.op('act', lambda e: e.activation(out=Tb[:, 0:NCOL], in_=ps[:, bank2, 0:NCOL], func=AF.Identity), reads=[('ps', bank2)], writes=['mTb'])
            bank3 = self.bank_any.next()
            self.mm(ps[:, bank3, 0:NCOL], [(Jf, Tb[:, 0:NCOL])], ['mTb', 'mcfm'], bank3)
            p1 = ps[:, bank1, 0:NCOL].rearrange("p (c d h) -> p c d h", d=2, h=4)
            p3 = ps[:, bank3, 0:NCOL].rearrange("p (c d h) -> p c d h", d=2, h=4)
            S.op('dve', lambda e: e.tensor_copy(out=v4(cmx)[:, :, 0, :], in_=p1[:, :, 0, :]), reads=[('ps', bank1)], writes=['mcmx'])
            S.op('dve', lambda e: e.tensor_copy(out=v4(cmx)[:, :, 1, :], in_=p3[:, :, 1, :]), reads=[('ps', bank3)], writes=['mcmx'])
            S.op('dve', lambda e: e.tensor_scalar(out=dg[0:NCOL, 0:NCOL], in0=id_f[0:NCOL, 0:NCOL], scalar1=rmax[0:NCOL, 0:1], scalar2=None, op0=ALU.mult),
                 reads=['mrmax', 'cf'], writes=['mdg'])
            bank4 = self.bank_any.next()
            self.mm(ps[:, bank4, 0:NCOL], [(ones_f[0:NCOL, :], dg[0:NCOL, 0:NCOL])], ['mdg', 'cf'], bank4)
            S.op('dve', lambda e: e.tensor_copy(out=gmx[:].rearrange("p c k -> p (c k)"), in_=ps[:, bank4, 0:NCOL]), reads=[('ps', bank4)], writes=['mgmx'])
            def sview(t, ci, d):
                return t[:].rearrange("p (s c) k -> p s c k", c=cps)[:, :, ci, d * 4:(d + 1) * 4]
            for d in range(2):
                order = list(range(cps)) if d == 0 else list(range(cps - 1, -1, -1))
                for oi, ci in enumerate(order):
                    if oi == 0:
                        if grp == 0:
                            S.op('dve', (lambda e, ci=ci, d=d: e.memset(sview(MP, ci, d), 0.0)), reads=[], writes=['mMP'])
                        else:
                            S.op('dve', (lambda e, ci=ci, d=d: e.tensor_copy(out=sview(MP, ci, d)[:, 0, :], in_=m0b[:, d * 4:(d + 1) * 4])), reads=['mm0'], writes=['mMP'])
                    S.op('dve', (lambda e, ci=ci, d=d: e.tensor_tensor(out=sview(ML, ci, d), in0=sview(gmx, ci, d), in1=sview(MP, ci, d), op=ALU.max)),
                         reads=['mgmx', 'mMP'], writes=['mML'])
                    if oi + 1 < len(order):
                        nx = order[oi + 1]
                        S.op('dve', (lambda e, ci=ci, d=d, nx=nx: e.tensor_tensor(out=sview(MP, nx, d), in0=sview(ML, ci, d), in1=sview(bt, ci, d), op=ALU.add)),
                             reads=['mML', 'mbt'], writes=['mMP'])
                    else:
                        S.op('dve', (lambda e, ci=ci, d=d: e.tensor_tensor(out=mfin[:, :, d * 4:(d + 1) * 4], in0=sview(ML, ci, d), in1=sview(bt, ci, d), op=ALU.add)),
                             reads=['mML', 'mbt'], writes=['mfin'])
            S.op('dve', lambda e: e.tensor_tensor(out=Mt[:], in0=cmx[:], in1=MP[:], op=ALU.max), reads=['mcmx', 'mMP'], writes=['mMt'])
            for dst, a_, b_, kk in ((Uu, ML, Mt, 'mU'), (Wi, MP, Mt, 'mWi'), (Ws, gg, ML, 'mWs'), (Dc, MP, ML, 'mDc')):
                S.op('dve', (lambda e, a_=a_, b_=b_: e.tensor_tensor(out=tq_[:], in0=a_[:], in1=b_[:], op=ALU.subtract)),
                     reads=['mML', 'mMt', 'mMP', 'mg'], writes=['mtq'])
                S.op('act', (lambda e, dst=dst: e.activation(out=dst[:], in_=tq_[:], func=AF.Exp)), reads=['mtq'], writes=[kk])
            S.op('dve', lambda e: e.tensor_tensor(out=tq_[:], in0=bb[:], in1=Mt[:], op=ALU.add), reads=['mb', 'mMt'], writes=['mtq'])
            S.op('act', lambda e: e.activation(out=Fl[:], in_=tq_[:], func=AF.Exp, scale=-1.0), reads=['mtq'], writes=['mFl'])
            if grp == 0:
                for seq in range(NPS):
                    self.dma(self.outs["m_out"][seq:seq + 1, :], mfin[0:1, seq, :], reads=['mfin'])
            S.barrier()
            M.release()
            scal_keys = ['mU', 'mWi', 'mFl', 'mWs', 'mDc']

            def do_head(hd):
                M.mark()
                wrot = [M.alloc([128, KC, 256], BF16, "mwrot")]
                Wo = M.alloc([128, 2, D], BF16, "mWo")
                qT = M.alloc([128, 2, gn], BF16, "mqT")
                kT = M.alloc([128, 2, gn], BF16, "mkT")
                ktok = M.alloc([128, NCH, 256], BF16, "mktok")
                vaug = M.alloc([128, NCH, 257], BF16, "mvaug")
                hacc = M.alloc([128, NCH, 256], BF16, "mhacc")
                chains = [(seq, d) for seq in range(nseq) for d in range(2)]
                nchn = len(chains)
                Cst = [[M.alloc([128, 257], F32, "mCst") for _ in range(2)] for _ in chains]
                Cbf = [[M.alloc([128, 257], BF16, "mCbf") for _ in range(2)] for _ in chains]
                NB = 4
                aT = [M.alloc([128, 128], BF16, "maT") for _ in range(NB)]
                num = [M.alloc([128, 257], F32, "mnum") for _ in range(NB)]
                isb = [M.alloc([128, 257], F32, "misb") for _ in range(NB)]
                kw = [M.alloc([128, 256], BF16, "mkw") for _ in range(NB)]
                sm = [M.alloc([128, 4], F32, "msm") for _ in range(NB)]
                ssq = M.alloc([128, NCH], F32, "mssq")
                junk = M.alloc([128, 256], BF16, "mjunk")
                ogT = qT
                gT = kT
                self.dma(Wo[:], wout_d[:, 2 * hd:2 * hd + 2, :], writes=['mWo'], q='pool')
                S.op('dve', lambda e: e.memset(vaug[:, :, 256:257], 1.0), writes=[K_('mv1')])

                def loadw(src):
                    self.dma(wrot[0][:], src, writes=[('mwrot', 0)], q='pool')
                    return 0
                for q_, dst, sc_ in ((0, qT, 1.0), (1, kT, 0.0625)):
                    b = loadw(wqkv_d[:, :, q_ * D + hd * 256:q_ * D + (hd + 1) * 256])
                    for kc2 in range(2):
                        for t0 in range(0, gn, 512):
                            bank = self.bank_any.next()
                            self.mm(ps[:, bank, :], [(wrot[b][:, k, kc2 * 128:(kc2 + 1) * 128], hT[:, k, t0:t0 + 512]) for k in range(KC)],
                                    [('mwrot', b)] + hk, bank)
                            S.op('act', (lambda e, dst=dst, bank=bank, kc2=kc2, t0=t0, sc_=sc_: e.activation(
                                out=dst[:, kc2, t0:t0 + 512], in_=ps[:, bank, :], func=AF.Identity, scale=sc_)),
                                reads=[('ps', bank)], writes=[('mqk', q_)])
                for q_, dst, sc_ in ((1, ktok, 0.0625), (2, vaug, 1.0)):
                    b = loadw(wqkv_d[:, :, q_ * D + hd * 256:q_ * D + (hd + 1) * 256])
                    for ch in range(0, NCH, 2):
                        bank = self.bank_any.next()

                        def ft(e, bank=bank, ch=ch, b=b):
                            for ci in range(2):
                                for k in range(KC):
                                    r = e.matmul(ps[:, bank, ci * 256:(ci + 1) * 256], lhsT=hT[:, k, (ch + ci) * 128:(ch + ci + 1) * 128],
                                                 rhs=wrot[b][:, k, :], start=(k == 0 and ci == 0), stop=(k == KC - 1), skip_group_check=True)
                            return r
                        S.op('pe', ft, reads=[('mwrot', b)] + hk, writes=[('ps', bank)])
                        S.op('act', (lambda e, dst=dst, bank=bank, ch=ch, sc_=sc_: e.activation(
                            out=dst[:, ch:ch + 2, 0:256], in_=ps[:, bank, :].rearrange("p (a b) -> p a b", b=256), func=AF.Identity, scale=sc_)),
                            reads=[('ps', bank)], writes=[('mtok', q_)])
                qkk = [('mqk', 0), ('mqk', 1)]
                tokk = [('mtok', 1), ('mtok', 2)]
                for ci_, (seq, d) in enumerate(chains):
                    for kc2 in range(2):
                        if grp == 0:
                            S.op('dve', (lambda e, t=Cst[ci_][kc2]: e.memset(t[:], 0.0)), writes=[('mCst', ci_, kc2)])
                            S.op('dve', (lambda e, t=Cbf[ci_][kc2]: e.memset(t[:], 0.0)), writes=[('mCbf', ci_)])
                        else:
                            self.dma(Cst[ci_][kc2][:], ins["ml_caug0"][d, hd, kc2 * 128:(kc2 + 1) * 128, :], writes=[('mCst', ci_, kc2)])
                            S.op('act', (lambda e, t=Cbf[ci_][kc2], s_=Cst[ci_][kc2]: e.activation(out=t[:], in_=s_[:], func=AF.Identity)),
                                 reads=[('mCst', ci_, kc2)], writes=[('mCbf', ci_)])
                hwritten = set()
                slot = [0]

                def stage1(ci_, step, lane):
                    seq, d = chains[ci_]
                    ci = step if d == 0 else cps - 1 - step
                    ch = seq * cps + ci
                    col = d * 4 + hd
                    sb = slot[0] % NB
                    slot[0] += 1
                    tsl = slice(ch * 128, (ch + 1) * 128)
                    bankA = 6 + lane
                    self.mm(ps[:, bankA, 0:128], [(kT[:, kc2, tsl], qT[:, kc2, tsl]) for kc2 in range(2)], qkk, bankA)
                    S.op('dve', lambda e: e.scalar_tensor_tensor(out=aT[sb][:], in0=ps[:, bankA, 0:128], scalar=Ws[:, ch, col:col + 1], in1=cfm[:, d, :],
                                                                 op0=ALU.mult, op1=ALU.mult),
                         reads=[('ps', bankA), 'mWs', 'mcfm'], writes=[('maT', sb)])
                    S.op('dve', lambda e: e.tensor_scalar(out=kw[sb][:], in0=ktok[:, ch, :], scalar1=Ws[:, ch, col:col + 1], scalar2=None, op0=ALU.mult),
                         reads=tokk + ['mWs'], writes=[('mkw', sb)])
                    return (ci_, ch, col, d, sb, tsl, lane)

                def stage2(st):
                    ci_, ch, col, d, sb, tsl, lane = st
                    b1, b2, b3 = 3 * lane, 3 * lane + 1, 3 * lane + 2

                    def f1(e):
                        e.matmul(ps[:, b1, 0:257], lhsT=aT[sb][:], rhs=vaug[:, ch, :], start=True, stop=True)
                        for kc2 in range(2):
                            r = e.matmul(ps[:, b1, 257 + kc2:258 + kc2], lhsT=kw[sb][:, kc2 * 128:(kc2 + 1) * 128], rhs=vaug[:, ch, 256:257], start=True, stop=True)
                        return r
                    S.op('pe', f1, reads=[('maT', sb), ('mkw', sb)] + tokk + [K_('x')], writes=[('ps', b1)])
                    self.mm(ps[:, b2, 0:257], [(qT[:, kc2, tsl], Cbf[ci_][kc2][:]) for kc2 in range(2)], qkk + [('mCbf', ci_)], b2)

                    def f3(e):
                        for kc2 in range(2):
                            r = e.matmul(ps[:, b3, kc2 * 256:(kc2 + 1) * 256], lhsT=kw[sb][:, kc2 * 128:(kc2 + 1) * 128], rhs=vaug[:, ch, 0:256], start=True, stop=True)
                        return r
                    S.op('pe', f3, reads=[('mkw', sb)] + tokk, writes=[('ps', b3)])
                    return st

                def stage3(st):
                    ci_, ch, col, d, sb, tsl, lane = st
                    b1, b2, b3 = 3 * lane, 3 * lane + 1, 3 * lane + 2
                    for kc2 in range(2):
                        S.op('dve', (lambda e, kc2=kc2: e.scalar_tensor_tensor(out=Cst[ci_][kc2][:, 0:256], in0=Cst[ci_][kc2][:, 0:256], scalar=Dc[:, ch, col:col + 1],
                                                                               in1=ps[:, b3, kc2 * 256:(kc2 + 1) * 256], op0=ALU.mult, op1=ALU.add)),
                             reads=[('ps', b3), 'mDc', ('mCst', ci_, kc2), ('mCbf', ci_)], writes=[('mCst', ci_, kc2)])
                        S.op('dve', (lambda e, kc2=kc2: e.scalar_tensor_tensor(out=Cst[ci_][kc2][:, 256:257], in0=Cst[ci_][kc2][:, 256:257], scalar=Dc[:, ch, col:col + 1],
                                                                               in1=ps[:, b1, 257 + kc2:258 + kc2], op0=ALU.mult, op1=ALU.add)),
                             reads=[('ps', b1), 'mDc', ('mCst', ci_, kc2)], writes=[('mCst', ci_, kc2)])
                    S.op('act', lambda e: e.activation(out=isb[sb][:], in_=ps[:, b2, 0:257], func=AF.Identity, scale=Wi[:, ch, col:col + 1]),
                         reads=[('ps', b2), 'mWi'], writes=[('misb', sb)])
                    for kc2 in range(2):
                        S.op('act', (lambda e, kc2=kc2: e.activation(out=Cbf[ci_][kc2][:], in_=Cst[ci_][kc2][:], func=AF.Identity)),
                             reads=[('mCst', ci_, kc2)], writes=[('mCbf', ci_)])
                    S.op('dve', lambda e: e.scalar_tensor_tensor(out=num[sb][:], in0=ps[:, b1, 0:257], scalar=Uu[:, ch, col:col + 1], in1=isb[sb][:],
                                                                 op0=ALU.mult, op1=ALU.add),
                         reads=[('ps', b1), 'mU', ('misb', sb)], writes=[('mnum', sb)])
                    S.op('dve', lambda e: e.scalar_tensor_tensor(out=sm[sb][:, 0:1], in0=num[sb][:, 256:257], scalar=-1.0, in1=num[sb][:, 256:257], op0=ALU.mult, op1=ALU.max),
                         reads=[('mnum', sb)], writes=[('msm', sb)])
                    S.op('dve', lambda e: e.tensor_tensor(out=sm[sb][:, 0:1], in0=sm[sb][:, 0:1], in1=Fl[:, ch, col:col + 1], op=ALU.max),
                         reads=[('msm', sb), 'mFl'], writes=[('msm', sb)])
                    S.op('dve', lambda e: e.reciprocal(out=sm[sb][:, 0:1], in_=sm[sb][:, 0:1]), reads=[('msm', sb)], writes=[('msm', sb)])
                    if ch not in hwritten:
                        hwritten.add(ch)
                        S.op('dve', lambda e: e.tensor_scalar(out=hacc[:, ch, :], in0=num[sb][:, 0:256], scalar1=sm[sb][:, 0:1], scalar2=None, op0=ALU.mult),
                             reads=[('mnum', sb), ('msm', sb)], writes=[('mhacc', ch)])
                    else:
                        S.op('dve', lambda e: e.scalar_tensor_tensor(out=hacc[:, ch, :], in0=num[sb][:, 0:256], scalar=sm[sb][:, 0:1], in1=hacc[:, ch, :],
                                                                     op0=ALU.mult, op1=ALU.add),
                             reads=[('mnum', sb), ('msm', sb), ('mhacc', ch)], writes=[('mhacc', ch)])

                items = [(ci_, step) for step in range(cps) for ci_ in range(nchn)]
                st1, st2 = {}, {}
                for idx in range(len(items) + 2):
                    if idx - 2 >= 0:
                        stage3(st2[idx - 2])
                    if 0 <= idx - 1 < len(items):
                        st2[idx - 1] = stage2(st1[idx - 1])
                    if idx < len(items):
                        st1[idx] = stage1(items[idx][0], items[idx][1], idx % 2)
                if grp == 0:
                    for ci_, (seq, d) in enumerate(chains):
                        for kc2 in range(2):
                            self.dma(self.outs["caug_out"][seq, d, hd, kc2 * 128:(kc2 + 1) * 128, :], Cst[ci_][kc2][:], reads=[('mCst', ci_, kc2)])
                S.barrier()
                for ch in range(NCH):
                    S.op('dve', (lambda e, ch=ch: e.scalar_tensor_tensor(out=junk[:], in0=hacc[:, ch, :], scalar=1.0, in1=hacc[:, ch, :], op0=ALU.mult, op1=ALU.mult,
                                                                         accum_out=ssq[:, ch:ch + 1])),
                         reads=[('mhacc', ch)], writes=['mjunk', ('mssq', ch)])
                S.op('act', lambda e: e.activation(out=ssq[:], in_=ssq[:], func=AF.Sqrt, scale=1.0 / 256.0, bias=epsT[:, 0:1]),
                     reads=[('mssq', ch) for ch in range(NCH)] + ['eps'], writes=['mssqr'])
                S.op('dve', lambda e: e.reciprocal(out=ssq[:], in_=ssq[:]), reads=['mssqr'], writes=['mssqr'])
                for ch in range(NCH):
                    S.op('dve', (lambda e, ch=ch: e.tensor_scalar(out=hacc[:, ch, :], in0=hacc[:, ch, :], scalar1=ssq[:, ch:ch + 1], scalar2=None, op0=ALU.mult)),
                         reads=[('mhacc', ch), 'mssqr'], writes=[('mhacc', ch)])
                b = loadw(wog_d[:, :, hd * 256:(hd + 1) * 256])
                for kc2 in range(2):
                    for t0 in range(0, gn, 512):
                        bank = self.bank_any.next()
                        self.mm(ps[:, bank, :], [(wrot[b][:, k, kc2 * 128:(kc2 + 1) * 128], hT[:, k, t0:t0 + 512]) for k in range(KC)],
                                [('mwrot', b)] + hk, bank)
                        S.op('act', (lambda e, bank=bank, kc2=kc2, t0=t0: e.activation(out=ogT[:, kc2, t0:t0 + 512], in_=ps[:, bank, :], func=AF.Sigmoid)),
                             reads=[('ps', bank)], writes=[('mog', kc2, t0)])
                for ch in range(NCH):
                    bank = self.bank_any.next()

                    def ftr(e, bank=bank, ch=ch):
                        for kc2 in range(2):
                            r = e.transpose(psb[:, bank, kc2 * 128:(kc2 + 1) * 128], hacc[:, ch, kc2 * 128:(kc2 + 1) * 128], id_b)
                        return r
                    S.op('pe', ftr, reads=[('mhacc', ch), 'cb'], writes=[('ps', bank)])
                    for kc2 in range(2):
                        S.op('dve', (lambda e, bank=bank, ch=ch, kc2=kc2: e.scalar_tensor_tensor(
                            out=gT[:, kc2, ch * 128:(ch + 1) * 128], in0=psb[:, bank, kc2 * 128:(kc2 + 1) * 128], scalar=hgT[:, 2 * hd + kc2:2 * hd + kc2 + 1],
                            in1=ogT[:, kc2, ch * 128:(ch + 1) * 128], op0=ALU.mult, op1=ALU.mult)),
                            reads=[('ps', bank), 'mhg', ('mog', kc2, (ch * 128) // 512 * 512)], writes=[('mgT', ch // 4)])
                for t0 in range(0, gn, 512):
                    for oc in range(KC):
                        bank = self.bank_any.next()
                        self.mm(ps[:, bank, :], [(Wo[:, kc2, oc * 128:(oc + 1) * 128], gT[:, kc2, t0:t0 + 512]) for kc2 in range(2)],
                                ['mWo', ('mgT', t0 // 512)], bank)
                        self.xupdate(l, 1, g0 + t0, 512, oc, ps[:, bank, :], [('ps', bank)])
                S.barrier()
                M.release()

            for hd in range(4):
                do_head(hd)
            S.barrier()
            M.release()

        do_group(0)
        do_group(1)
        M.release()

    def mixer_mlstm(self, l):
        S, M, ins, ps, psb = self.S, self.M, self.ins, self.ps, self.psb
        ones_f, id_f, id_b, epsT = self.ones_f, self.id_f, self.id_b, self.epsT
        wqkv_d = ins["ml_w_qkv"].rearrange("(k p) n -> p k n", p=128)
        wog_d = ins["ml_w_og"].rearrange("(k p) n -> p k n", p=128)
        wout_d = ins["ml_w_out"].rearrange("(k p) n -> p k n", p=128)
        M.mark()
        cfm = M.alloc([128, 6, 128], F32, "mcfm")
        self.dma(cfm[:], ins["cf32"][:, 2:8, :], writes=['mcfm'])
        Wif = M.alloc([128, KC, 16], BF16, "mWif")
        self.dma(Wif[:], ins["ml_w_if"].rearrange("(k p) n -> p k n", p=128), writes=['mWif'], q='pool')
        bif = M.alloc([128, 16], F32, "mbif")
        self.dma(bif[:], ins["ml_b_if"].partition_broadcast(128), writes=['mbif'])
        hgT = M.alloc([128, KC], F32, "mhg")
        self.dma(hgT[:], ins["ml_head_gT"], writes=['mhg'])
        m0b = M.alloc([128, 8], F32, "mm0")
        self.dma(m0b[:], ins["ml_m0"].partition_broadcast(128), writes=['mm0'])
        zero1 = M.alloc([128, 1], F32, "mzero")
        S.op('dve', lambda e: e.memset(zero1[:], 0.0), writes=['mzero'])
        uid = [0]

        def K_(name):
            uid[0] += 1
            return (name, uid[0])

        def do_group(grp):
            M.mark()
            g0, gn = (0, TP) if grp == 0 else (TP, TS)
            NCH = gn // 128
            nseq, cps = (NPS, 2) if grp == 0 else (1, 16)
            hT = M.alloc([128, KC, gn], BF16, "mhT")
            hk = [('mh', k, tq) for k in range(KC) for tq in range(gn // 256)]
            gi = M.alloc([128, NCH, 8], F32, "mgi")
            gl = M.alloc([128, NCH, 8], F32, "mgl")
            bb = M.alloc([128, NCH, 8], F32, "mb")
            bt = M.alloc([128, NCH, 8], F32, "mbt")
            gg = M.alloc([128, NCH, 8], F32, "mg")
            M.mark()
            self.alloc_norm_scratch(256)
            for tq in range(gn // 256):
                self.norm_mod_range(l, 1, g0 + tq * 256, 256, hT, tq * 256, [('mh', k, tq) for k in range(KC)])
            S.barrier()
            M.release()
            gpre = M.alloc([128, NCH, 16], F32, "mgpre")
            for c8 in range(0, NCH, 8):
                bank = self.bank_any.next()

                def fg(e, bank=bank, c8=c8):
                    for ci in range(8):
                        ch = c8 + ci
                        for k in range(KC):
                            r = e.matmul(ps[:, bank, ci * 16:(ci + 1) * 16], lhsT=hT[:, k, ch * 128:(ch + 1) * 128], rhs=Wif[:, k, :],
                                         start=(k == 0 and ci == 0), stop=(k == KC - 1), skip_group_check=True)
                    return r
                S.op('pe', fg, reads=hk + ['mWif'], writes=[('ps', bank)])
                for ci in range(8):
                    S.op('dve', (lambda e, bank=bank, ci=ci, c8=c8: e.tensor_tensor(out=gpre[:, c8 + ci, :], in0=ps[:, bank, ci * 16:(ci + 1) * 16], in1=bif[:], op=ALU.add)),
                         reads=[('ps', bank), 'mbif'], writes=['mgpre'])
            gpv = gpre[:].rearrange("p c (d x) -> p c d x", d=2)
            S.op('dve', lambda e: e.tensor_copy(out=gi[:].rearrange("p c (d h) -> p c d h", d=2), in_=gpv[:, :, :, 0:4]), reads=['mgpre'], writes=['mgi'])
            glv = gl[:].rearrange("p c (d h) -> p c d h", d=2)
            S.op('act', lambda e: e.activation(out=glv, in_=gpv[:, :, :, 4:8], func=AF.Exp, scale=-1.0), reads=['mgpre'], writes=['mgl'])
            S.op('act', lambda e: e.activation(out=gl[:], in_=gl[:], func=AF.Ln, bias=1.0), reads=['mgl'], writes=['mgl'])
            S.op('dve', lambda e: e.tensor_scalar(out=gl[:], in0=gl[:], scalar1=-1.0, scalar2=None, op0=ALU.mult), reads=['mgl'], writes=['mgl'])
            for c8 in range(0, NCH, 8):
                bank = self.bank_any.next()

                def fc(e, bank=bank, c8=c8):
                    for ci in range(8):
                        ch = c8 + ci
                        e.matmul(ps[:, bank, ci * 16:ci * 16 + 4], lhsT=cfm[:, 0, :], rhs=gl[:, ch, 0:4], start=True, stop=True)
                        e.matmul(ps[:, bank, ci * 16 + 4:ci * 16 + 8], lhsT=cfm[:, 1, :], rhs=gl[:, ch, 4:8], start=True, stop=True)
                        r = e.matmul(ps[:, bank, ci * 16 + 8:ci * 16 + 16], lhsT=ones_f, rhs=gl[:, ch, :], start=True, stop=True)
                    return r
                S.op('pe', fc, reads=['mgl', 'mcfm', 'cf'], writes=[('ps', bank)])
                pv = ps[:, bank, 0:128].rearrange("p (c x) -> p c x", x=16)
                S.op('dve', (lambda e, pv=pv, c8=c8: e.tensor_copy(out=bb[:, c8:c8 + 8, :], in_=pv[:, :, 0:8])), reads=[('ps', bank)], writes=['mb'])
                S.op('dve', (lambda e, pv=pv, c8=c8: e.tensor_copy(out=bt[:, c8:c8 + 8, :], in_=pv[:, :, 8:16])), reads=[('ps', bank)], writes=['mbt'])
            S.op('dve', lambda e: e.tensor_tensor(out=gg[:], in0=gi[:], in1=bb[:], op=ALU.subtract), reads=['mgi', 'mb'], writes=['mg'])
            stg = getattr(self, "dbg_mls", 99)
            if grp == 0:
                self.dump_tile("mWif", Wif[:], [128, KC, 16], ['mWif'], BF16)
                self.dump_tile("mhT", hT[:], [128, KC, gn], hk, BF16)
                self.dump_tile("mgpre", gpre[:], [128, NCH, 16], ['mgpre'])
                for nm_, tl_, ky_ in (("mgi", gi, 'mgi'), ("mgl", gl, 'mgl'), ("mbb", bb, 'mb'), ("mbt", bt, 'mbt'), ("mgg", gg, 'mg')):
                    self.dump_tile(nm_, tl_[:], [128, NCH, 8], [ky_])
            if stg <= 1:
                raise _Stop()

            def do_head(hd):
                M.mark()
                wrot = [M.alloc([128, KC, 256], BF16, "mwrot") for _ in range(2)]
                Wo = M.alloc([128, 2, D], BF16, "mWo")
                qT = M.alloc([128, 2, gn], BF16, "mqT")
                kT = M.alloc([128, 2, gn], BF16, "mkT")
                ktok = M.alloc([128, NCH, 256], BF16, "mktok")
                vaug = M.alloc([128, NCH, 257], BF16, "mvaug")
                hacc = M.alloc([128, NCH, 256], BF16, "mhacc")
                Cst = [M.alloc([128, 257], F32, "mCst") for _ in range(2)]
                Cbf = [M.alloc([128, 257], BF16, "mCbf") for _ in range(2)]
                mpv = [M.alloc([128, 1], F32, "mmp") for _ in range(2)]
                sm = M.alloc([128, 16], F32, "msm")
                t128 = [M.alloc([128, 128], F32, "mt128") for _ in range(4)]
                aT = M.alloc([128, 128], BF16, "maT")
                num = M.alloc([128, 257], F32, "mnum")
                isb = M.alloc([128, 257], F32, "misb")
                kw = M.alloc([128, 256], BF16, "mkw")
                hs = M.alloc([128, 256], F32, "mhs")
                hn = M.alloc([128, 256], BF16, "mhn")
                junk = M.alloc([128, 256], BF16, "mjunk")
                ogT = qT
                gT = kT
                self.dma(Wo[:], wout_d[:, 2 * hd:2 * hd + 2, :], writes=['mWo'], q='pool')
                S.op('dve', lambda e: e.memset(vaug[:, :, 256:257], 1.0), writes=[K_('mv1')])
                wi = [0]

                def loadw(src):
                    b = wi[0] % 2
                    wi[0] += 1
                    self.dma(wrot[b][:], src, writes=[('mwrot', b)], q='pool')
                    return b
                for q_, dst, sc_ in ((0, qT, 1.0), (1, kT, 0.0625)):
                    b = loadw(wqkv_d[:, :, q_ * D + hd * 256:q_ * D + (hd + 1) * 256])
                    for kc2 in range(2):
                        for t0 in range(0, gn, 512):
                            bank = self.bank_any.next()
                            self.mm(ps[:, bank, :], [(wrot[b][:, k, kc2 * 128:(kc2 + 1) * 128], hT[:, k, t0:t0 + 512]) for k in range(KC)],
                                    [('mwrot', b)] + hk, bank)
                            S.op('act', (lambda e, dst=dst, bank=bank, kc2=kc2, t0=t0, sc_=sc_: e.activation(
                                out=dst[:, kc2, t0:t0 + 512], in_=ps[:, bank, :], func=AF.Identity, scale=sc_)),
                                reads=[('ps', bank)], writes=[('mqk', q_)])
                for q_, dst, sc_ in ((1, ktok, 0.0625), (2, vaug, 1.0)):
                    b = loadw(wqkv_d[:, :, q_ * D + hd * 256:q_ * D + (hd + 1) * 256])
                    for ch in range(0, NCH, 2):
                        bank = self.bank_any.next()

                        def ft(e, bank=bank, ch=ch, b=b):
                            for ci in range(2):
                                for k in range(KC):
                                    r = e.matmul(ps[:, bank, ci * 256:(ci + 1) * 256], lhsT=hT[:, k, (ch + ci) * 128:(ch + ci + 1) * 128],
                                                 rhs=wrot[b][:, k, :], start=(k == 0 and ci == 0), stop=(k == KC - 1), skip_group_check=True)
                            return r
                        S.op('pe', ft, reads=[('mwrot', b)] + hk, writes=[('ps', bank)])
                        S.op('act', (lambda e, dst=dst, bank=bank, ch=ch, sc_=sc_: e.activation(
                            out=dst[:, ch:ch + 2, 0:256], in_=ps[:, bank, :].rearrange("p (a b) -> p a b", b=256), func=AF.Identity, scale=sc_)),
                            reads=[('ps', bank)], writes=[('mtok', q_)])
                qkk = [('mqk', 0), ('mqk', 1)]
                tokk = [('mtok', 1), ('mtok', 2)]
                if stg <= 2:
                    raise _Stop()
                icnt = [0]

                def inst(seq, d, ch, first, mprev, mnew):
                    col = d * 4 + hd
                    cg = gg[:, ch, col:col + 1]
                    cb_ = bb[:, ch, col:col + 1]
                    cbt = bt[:, ch, col:col + 1]
                    tsl = slice(ch * 128, (ch + 1) * 128)
                    kA = K_('A')
                    bankA = self.bank_any.next()
                    self.mm(ps[:, bankA, 0:128], [(kT[:, kc2, tsl], qT[:, kc2, tsl]) for kc2 in range(2)], qkk, bankA)
                    S.op('dve', lambda e: e.tensor_scalar(out=t128[0][:], in0=id_f, scalar1=cg, scalar2=None, op0=ALU.mult),
                         reads=['mg', 'cf'], writes=[('mt128', 0)])
                    bankB = self.bank_any.next()
                    self.mm(ps[:, bankB, 0:128], [(ones_f, t128[0][:])], [('mt128', 0), 'cf'], bankB)
                    S.op('dve', lambda e: e.tensor_tensor(out=t128[1][:], in0=ps[:, bankB, 0:128], in1=cfm[:, 2 + d, :], op=ALU.add),
                         reads=[('ps', bankB), 'mcfm'], writes=[('mt128', 1)])
                    S.op('dve', lambda e: e.tensor_reduce(out=sm[:, 0:1], in_=t128[1][:], axis=AX.X, op=ALU.max), reads=[('mt128', 1)], writes=[('msm', 0)])
                    S.op('dve', lambda e: e.tensor_reduce(out=sm[:, 1:2], in_=ps[:, bankB, 0:128], axis=AX.X, op=ALU.max), reads=[('ps', bankB)], writes=[('msm', 1)])
                    S.op('dve', lambda e: e.tensor_tensor(out=sm[:, 2:3], in0=sm[:, 0:1], in1=mprev, op=ALU.max), reads=[('msm', 0), 'mmp'], writes=[('msm', 2)])
                    S.op('dve', lambda e: e.tensor_tensor(out=sm[:, 3:4], in0=sm[:, 1:2], in1=mprev, op=ALU.max), reads=[('msm', 1), 'mmp'], writes=[('msm', 3)])
                    S.op('dve', lambda e: e.tensor_scalar(out=t128[2][:], in0=id_f, scalar1=sm[:, 2:3], scalar2=None, op0=ALU.mult),
                         reads=[('msm', 2), 'cf'], writes=[('mt128', 2)])
                    bankC = self.bank_any.next()
                    self.mm(ps[:, bankC, 0:128], [(ones_f, t128[2][:]), (id_f, cfm[:, 4 + d, :])], [('mt128', 2), 'cf', 'mcfm'], bankC)
                    S.op('act', lambda e: e.activation(out=t128[3][:], in_=ps[:, bankC, 0:128], func=AF.Exp, scale=-1.0, bias=cg),
                         reads=[('ps', bankC), 'mg'], writes=[('mt128', 3)])
                    S.op('dve', lambda e: e.tensor_tensor(out=aT[:], in0=t128[3][:], in1=ps[:, bankA, 0:128], op=ALU.mult),
                         reads=[('mt128', 3), ('ps', bankA)], writes=['maT'])
                    bankD = self.bank_any.next()
                    self.mm(ps[:, bankD, 0:257], [(aT[:], vaug[:, ch, :])], ['maT'] + tokk + [K_('x')], bankD)
                    bankE = self.bank_any.next()
                    self.mm(ps[:, bankE, 0:257], [(qT[:, kc2, tsl], Cbf[kc2][:]) for kc2 in range(2)], qkk + ['mCbf'], bankE)
                    S.op('act', lambda e: e.activation(out=sm[:, 4:5], in_=sm[:, 2:3], func=AF.Exp, scale=-1.0, bias=mprev),
                         reads=[('msm', 2), 'mmp'], writes=[('msm', 4)])
                    S.op('act', lambda e: e.activation(out=isb[:], in_=ps[:, bankE, 0:257], func=AF.Identity, scale=sm[:, 4:5]),
                         reads=[('ps', bankE), ('msm', 4)], writes=['misb'])
                    S.op('dve', lambda e: e.tensor_tensor(out=num[:], in0=isb[:], in1=ps[:, bankD, 0:257], op=ALU.add),
                         reads=['misb', ('ps', bankD)], writes=['mnum'])
                    S.op('dve', lambda e: e.tensor_tensor(out=sm[:, 5:6], in0=sm[:, 2:3], in1=cb_, op=ALU.add), reads=[('msm', 2), 'mb'], writes=[('msm', 5)])
                    S.op('act', lambda e: e.activation(out=sm[:, 5:6], in_=sm[:, 5:6], func=AF.Exp, scale=-1.0), reads=[('msm', 5)], writes=[('msm', 5)])
                    S.op('dve', lambda e: e.scalar_tensor_tensor(out=sm[:, 6:7], in0=num[:, 256:257], scalar=-1.0, in1=num[:, 256:257], op0=ALU.mult, op1=ALU.max),
                         reads=['mnum'], writes=[('msm', 6)])
                    S.op('dve', lambda e: e.tensor_tensor(out=sm[:, 6:7], in0=sm[:, 6:7], in1=sm[:, 5:6], op=ALU.max),
                         reads=[('msm', 6), ('msm', 5)], writes=[('msm', 6)])
                    S.op('dve', lambda e: e.reciprocal(out=sm[:, 6:7], in_=sm[:, 6:7]), reads=[('msm', 6)], writes=[('msm', 6)])
                    if d == 0:
                        S.op('dve', lambda e: e.tensor_scalar(out=hacc[:, ch, :], in0=num[:, 0:256], scalar1=sm[:, 6:7], scalar2=None, op0=ALU.mult),
                             reads=['mnum', ('msm', 6)], writes=[('mhacc', ch)])
                    else:
                        S.op('dve', lambda e: e.scalar_tensor_tensor(out=hs[:], in0=num[:, 0:256], scalar=sm[:, 6:7], in1=hacc[:, ch, :], op0=ALU.mult, op1=ALU.add),
                             reads=['mnum', ('msm', 6), ('mhacc', ch)], writes=['mhs'])
                        S.op('dve', lambda e: e.scalar_tensor_tensor(out=junk[:], in0=hs[:], scalar=1.0, in1=hs[:], op0=ALU.mult, op1=ALU.mult, accum_out=sm[:, 7:8]),
                             reads=['mhs'], writes=['mjunk', ('msm', 7)])
                        S.op('act', lambda e: e.activation(out=sm[:, 7:8], in_=sm[:, 7:8], func=AF.Sqrt, scale=1.0 / 256.0, bias=epsT[:, 0:1]),
                             reads=[('msm', 7), 'eps'], writes=[('msm', 7)])
                        S.op('dve', lambda e: e.reciprocal(out=sm[:, 7:8], in_=sm[:, 7:8]), reads=[('msm', 7)], writes=[('msm', 7)])
                        S.op('dve', lambda e: e.tensor_scalar(out=hacc[:, ch, :], in0=hs[:], scalar1=sm[:, 7:8], scalar2=None, op0=ALU.mult),
                             reads=['mhs', ('msm', 7)], writes=[('mhacc', ch)])
                    S.op('dve', lambda e: e.tensor_scalar(out=sm[:, 8:9], in0=sm[:, 3:4], scalar1=-1.0, scalar2=None, op0=ALU.mult), reads=[('msm', 3)], writes=[('msm', 8)])
                    S.op('act', lambda e: e.activation(out=sm[:, 9:10], in_=cg, func=AF.Exp, bias=sm[:, 8:9]), reads=['mg', ('msm', 8)], writes=[('msm', 9)])
                    S.op('act', lambda e: e.activation(out=sm[:, 10:11], in_=mprev, func=AF.Exp, bias=sm[:, 8:9]), reads=['mmp', ('msm', 8)], writes=[('msm', 10)])
                    S.op('dve', lambda e: e.tensor_scalar(out=kw[:], in0=ktok[:, ch, :], scalar1=sm[:, 9:10], scalar2=None, op0=ALU.mult),
                         reads=tokk + [('msm', 9)], writes=['mkw'])
                    for kc2 in range(2):
                        bankF = self.bank_any.next()
                        self.mm(ps[:, bankF, 0:257], [(kw[:, kc2 * 128:(kc2 + 1) * 128], vaug[:, ch, :])], ['mkw'] + tokk, bankF)
                        S.op('dve', (lambda e, kc2=kc2, bankF=bankF: e.scalar_tensor_tensor(out=Cst[kc2][:], in0=Cst[kc2][:], scalar=sm[:, 10:11], in1=ps[:, bankF, 0:257],
                                                                                           op0=ALU.mult, op1=ALU.add)),
                             reads=[('ps', bankF), ('msm', 10), ('mCst', kc2)], writes=[('mCst', kc2)])
                        S.op('act', (lambda e, kc2=kc2: e.activation(out=Cbf[kc2][:], in_=Cst[kc2][:], func=AF.Identity)),
                             reads=[('mCst', kc2)], writes=['mCbf'])
                    S.op('dve', lambda e: e.tensor_tensor(out=mnew, in0=sm[:, 3:4], in1=cbt, op=ALU.add), reads=[('msm', 3), 'mbt'], writes=['mmp'])
                    icnt[0] += 1
                    if stg == 3 and icnt[0] >= 1:
                        raise _Stop()

                for seq in range(nseq):
                    for d in range(2):
                        if grp == 0:
                            for kc2 in range(2):
                                S.op('dve', (lambda e, kc2=kc2: e.memset(Cst[kc2][:], 0.0)), writes=[('mCst', kc2)])
                                S.op('dve', (lambda e, kc2=kc2: e.memset(Cbf[kc2][:], 0.0)), writes=['mCbf'])
                            S.op('dve', lambda e: e.memset(mpv[0][:], 0.0), writes=['mmp'])
                        else:
                            for kc2 in range(2):
                                self.dma(Cst[kc2][:], ins["ml_caug0"][d, hd, kc2 * 128:(kc2 + 1) * 128, :], writes=[('mCst', kc2)])
                                S.op('act', (lambda e, kc2=kc2: e.activation(out=Cbf[kc2][:], in_=Cst[kc2][:], func=AF.Identity)),
                                     reads=[('mCst', kc2)], writes=['mCbf'])
                            S.op('dve', (lambda e, d=d: e.tensor_copy(out=mpv[0][:], in_=m0b[:, d * 4 + hd:d * 4 + hd + 1])), reads=['mm0'], writes=['mmp'])
                        order = list(range(cps)) if d == 0 else list(range(cps - 1, -1, -1))
                        cur = 0
                        for ci in order:
                            inst(seq, d, seq * cps + ci, ci == order[0], mpv[cur][:], mpv[1 - cur][:])
                            cur = 1 - cur
                        if grp == 0:
                            for kc2 in range(2):
                                self.dma(self.outs["caug_out"][seq, d, hd, kc2 * 128:(kc2 + 1) * 128, :], Cst[kc2][:], reads=[('mCst', kc2)])
                            self.dma(self.outs["m_out"][seq:seq + 1, d * 4 + hd:d * 4 + hd + 1], mpv[cur][0:1, 0:1], reads=['mmp'])
                if stg <= 4:
                    raise _Stop()
                S.barrier()
                b = loadw(wog_d[:, :, hd * 256:(hd + 1) * 256])
                for kc2 in range(2):
                    for t0 in range(0, gn, 512):
                        bank = self.bank_any.next()
                        self.mm(ps[:, bank, :], [(wrot[b][:, k, kc2 * 128:(kc2 + 1) * 128], hT[:, k, t0:t0 + 512]) for k in range(KC)],
                                [('mwrot', b)] + hk + qkk, bank)
                        S.op('act', (lambda e, bank=bank, kc2=kc2, t0=t0: e.activation(out=ogT[:, kc2, t0:t0 + 512], in_=ps[:, bank, :], func=AF.Sigmoid)),
                             reads=[('ps', bank)], writes=[('mog', kc2, t0)])
                if stg <= 5:
                    raise _Stop()
                for ch in range(NCH):
                    bank = self.bank_any.next()

                    def ftr(e, bank=bank, ch=ch):
                        for kc2 in range(2):
                            r = e.transpose(psb[:, bank, kc2 * 128:(kc2 + 1) * 128], hacc[:, ch, kc2 * 128:(kc2 + 1) * 128], id_b)
                        return r
                    S.op('pe', ftr, reads=[('mhacc', ch), 'cb'], writes=[('ps', bank)])
                    for kc2 in range(2):
                        S.op('dve', (lambda e, bank=bank, ch=ch, kc2=kc2: e.scalar_tensor_tensor(
                            out=gT[:, kc2, ch * 128:(ch + 1) * 128], in0=psb[:, bank, kc2 * 128:(kc2 + 1) * 128], scalar=hgT[:, 2 * hd + kc2:2 * hd + kc2 + 1],
                            in1=ogT[:, kc2, ch * 128:(ch + 1) * 128], op0=ALU.mult, op1=ALU.mult)),
                            reads=[('ps', bank), 'mhg', ('mog', kc2, (ch * 128) // 512 * 512)] + qkk, writes=[('mgT', ch // 4)])
                if stg <= 6:
                    raise _Stop()
                for t0 in range(0, gn, 512):
                    for oc in range(KC):
                        bank = self.bank_any.next()
                        self.mm(ps[:, bank, :], [(Wo[:, kc2, oc * 128:(oc + 1) * 128], gT[:, kc2, t0:t0 + 512]) for kc2 in range(2)],
                                ['mWo', ('mgT', t0 // 512)], bank)
                        self.xupdate(l, 1, g0 + t0, 512, oc, ps[:, bank, :], [('ps', bank)])
                S.barrier()
                M.release()
                if stg <= 7:
                    raise _Stop()

            for hd in range(4):
                do_head(hd)
            S.barrier()
            M.release()
            if stg <= 8:
                raise _Stop()

        if not getattr(self, "dbg_skipg0", False):
            do_group(0)
        do_group(1)
        M.release()

    def mixer_na(self, l):
        S, M, ins, ps = self.S, self.M, self.ins, self.ps
        SCALE = 0.125
        ones_b = self.ones_b
        id_b = self.id_b
        psb = self.psb
        wq_d = ins["na_w_qkv"].rearrange("(k p) n -> p k n", p=128)
        wo_d = ins["na_w_out"].rearrange("(k p) n -> p k n", p=128)
        M.mark()
        rowsel = M.alloc([32, 16, 128], BF16, "nrowsel")
        RM = M.alloc([32, TS], BF16, "nRM")
        cm = M.alloc([128, 256], F32, "ncm")
        self.dma(rowsel[:], ins["na_rowsel"], writes=['nrowsel'])
        self.dma(RM[:], ins["na_rm"], writes=['nRM'])
        self.dma(cm[:], ins["na_cm"], writes=['ncm'])
        wbuf = [[M.alloc([128, KC, 128], BF16, "nW%d" % q) for q in range(3)] + [M.alloc([128, D], BF16, "nWo")] for _ in range(2)]
        tmpf_rr = RR([0, 1])
        rec = M.alloc([128, 256], F32, "nrec")
        NQ = 256

        def load_pair_w(c, b):
            for q in range(3):
                self.dma(wbuf[b][q][:], wq_d[:, :, q * D + c * 128:q * D + (c + 1) * 128], writes=[('nW', b, q)], q='pool')
            self.dma(wbuf[b][3][:], wo_d[:, c, :], writes=[('nW', b, 3)], q='pool')

        def do_group(grp):
            M.mark()
            g0, gn = (0, TP) if grp == 0 else (TP, TS)
            hT = M.alloc([128, KC, gn], BF16, "nhT")
            hk = [('nh', k, tq) for k in range(KC) for tq in range(gn // 256)]
            self.alloc_norm_scratch(256)
            for tq in range(gn // 256):
                self.norm_mod_range(l, 1, g0 + tq * 256, 256, hT, tq * 256, [('nh', k, tq) for k in range(KC)])
            if grp == 0:
                M.mark()
                tmpf = [M.alloc([128, 512], F32, "ntf") for _ in range(2)]
                Wfull = M.alloc([128, KC, D], BF16, "nWfull")
                for q, oname in ((1, "k_out"), (2, "v_out")):
                    self.dma(Wfull[:], wq_d[:, :, q * D:(q + 1) * D], writes=['nWfull'], q='pool')
                    for tc in range(TP // 128):
                        for half in range(2):
                            bank = self.bank_any.next()
                            self.mm(ps[:, bank, :], [(hT[:, k, tc * 128:(tc + 1) * 128], Wfull[:, k, half * 512:(half + 1) * 512]) for k in range(KC)],
                                    ['nWfull'] + hk, bank)
                            tb = tmpf_rr.next()
                            S.op('act', (lambda e, bank=bank, tb=tb: e.activation(out=tmpf[tb][:], in_=ps[:, bank, :], func=AF.Identity)),
                                 reads=[('ps', bank)], writes=[('ntf', tb)])
                            self.dma(self.outs[oname][tc * 128:(tc + 1) * 128, half * 512:(half + 1) * 512], tmpf[tb][:], reads=[('ntf', tb)])
                S.barrier()
                M.release()
            qT = M.alloc([128, gn], BF16, "nqT")
            kT = M.alloc([128, gn], BF16, "nkT")
            vtok = M.alloc([128, gn // 128, 128], BF16, "nvtok")
            oT = M.alloc([128, gn], BF16, "noT")
            PT = [M.alloc([128, 10, NQ], BF16, "nPT") for _ in range(2)]
            if grp == 1:
                ckb = M.alloc([128, 4, 128], BF16, "nckb")
                cvb = M.alloc([128, 4, 128], BF16, "ncvb")
                kcT = M.alloc([128, 512], BF16, "nkcT")
                Tt = M.alloc([128, 6, 256], F32, "nTt")
                ExT = M.alloc([128, 6, 256], BF16, "nExT")
                etmp = [M.alloc([128, 256], BF16, "netmp") for _ in range(2)]
                S.op('dve', lambda e: e.memset(Tt[:], NEG), writes=['nTt'])
            load_pair_w(0, 0)
            for c in range(KC):
                b = c % 2
                if c + 1 < KC:
                    load_pair_w(c + 1, (c + 1) % 2)
                Wq, Wk, Wv, Wo = wbuf[b]
                for t0 in range(0, gn, 512):
                    for q, dst, eng in ((0, qT, 'act'), (1, kT, 'dve')):
                        bank = self.bank_any.next()
                        self.mm(ps[:, bank, :], [(wbuf[b][q][:, k, :], hT[:, k, t0:t0 + 512]) for k in range(KC)], [('nW', b, q)] + hk, bank)
                        if eng == 'act':
                            S.op('act', (lambda e, dst=dst, bank=bank, t0=t0: e.activation(out=dst[:, t0:t0 + 512], in_=ps[:, bank, :], func=AF.Identity)),
                                 reads=[('ps', bank)], writes=[('nq', t0)])
                        else:
                            S.op('dve', (lambda e, dst=dst, bank=bank, t0=t0: e.tensor_copy(out=dst[:, t0:t0 + 512], in_=ps[:, bank, :])),
                                 reads=[('ps', bank)], writes=[('nk', t0)])
                    bank = self.bank_any.next()

                    def fv(e, bank=bank, t0=t0, Wv=Wv):
                        for tc in range(4):
                            for k in range(KC):
                                r = e.matmul(ps[:, bank, tc * 128:(tc + 1) * 128], lhsT=hT[:, k, t0 + tc * 128:t0 + (tc + 1) * 128], rhs=Wv[:, k, :],
                                             start=(k == 0 and tc == 0), stop=(k == KC - 1), skip_group_check=True)
                        return r
                    S.op('pe', fv, reads=[('nW', b, 2)] + hk, writes=[('ps', bank)])
                    S.op('act', (lambda e, bank=bank, t0=t0: e.activation(out=vtok[:, t0 // 128:t0 // 128 + 4, :], in_=ps[:, bank, :].rearrange("p (a b) -> p a b", b=128),
                                                                         func=AF.Identity)),
                         reads=[('ps', bank)], writes=[('nv', t0)])
                qk_all = [('nq', t0) for t0 in range(0, gn, 512)] + [('nk', t0) for t0 in range(0, gn, 512)]
                v_all = [('nv', t0) for t0 in range(0, gn, 512)]
                if grp == 1:
                    self.dma(ckb[:], ins["cache_k"].rearrange("(a p) n -> p a n", p=128)[:, :, c * 128:(c + 1) * 128], writes=['nckb'], q='pool')
                    self.dma(cvb[:], ins["cache_v"].rearrange("(a p) n -> p a n", p=128)[:, :, c * 128:(c + 1) * 128], writes=['ncvb'], q='pool')
                    bank = self.bank_any.next()

                    def ft(e, bank=bank):
                        for tc in range(4):
                            r = e.transpose(psb[:, bank, tc * 128:(tc + 1) * 128], ckb[:, tc, :], id_b)
                        return r
                    S.op('pe', ft, reads=['nckb', 'cb'], writes=[('ps', bank)])
                    S.op('dve', (lambda e, bank=bank: e.tensor_copy(out=kcT[:], in_=psb[:, bank, 0:512])), reads=[('ps', bank)], writes=['nkcT'])
                for half in range(2):
                    h = 2 * c + half
                    p0 = half * 64
                    pr = slice(p0, p0 + 64)
                    if grp == 0:
                        def stageA(sq_, pb, pr=pr):
                            t0 = sq_ * 256
                            bank = self.bank_any.next()

                            def fs(e):
                                for kc in range(2):
                                    r = e.matmul(ps[:, bank, kc * 256:(kc + 1) * 256], lhsT=kT[pr, t0 + kc * 128:t0 + (kc + 1) * 128],
                                                 rhs=qT[pr, t0:t0 + 256], start=True, stop=True)
                                return r
                            S.op('pe', fs, reads=qk_all, writes=[('ps', bank)])
                            S.op('act', lambda e: e.activation(out=PT[pb][:, 0:2, :], in_=ps[:, bank, :].rearrange("p (a b) -> p a b", b=256),
                                                               func=AF.Exp, scale=SCALE),
                                 reads=[('ps', bank)], writes=[('nPT', pb)])

                        def stageB(sq_, pb, pr=pr, half=half):
                            t0 = sq_ * 256
                            bank2 = self.bank_any.next()

                            def fo(e):
                                for kc in range(2):
                                    e.matmul(ps[:, bank2, 0:256], lhsT=vtok[:, t0 // 128 + kc, :], rhs=PT[pb][:, kc, :], start=(kc == 0), stop=(kc == 1),
                                             skip_group_check=True)
                                for kc in range(2):
                                    r = e.matmul(ps[:, bank2, 256:512], lhsT=ones_b, rhs=PT[pb][:, kc, :], start=False, stop=(kc == 1), skip_group_check=True)
                                return r
                            S.op('pe', fo, reads=[('nPT', pb), 'cb'] + v_all, writes=[('ps', bank2)])
                            S.op('act', lambda e: e.activation(out=rec[pr, 0:256], in_=ps[pr, bank2, 256:512], func=AF.Ln),
                                 reads=[('ps', bank2)], writes=['nrec'])
                            S.op('act', lambda e: e.activation(out=rec[pr, 0:256], in_=rec[pr, 0:256], func=AF.Exp, scale=-1.0),
                                 reads=['nrec'], writes=['nrec'])
                            S.op('dve', lambda e: e.tensor_tensor(out=oT[pr, t0:t0 + 256], in0=ps[pr, bank2, 0:256], in1=rec[pr, 0:256], op=ALU.mult),
                                 reads=[('ps', bank2), 'nrec'], writes=[('no', half)])
                        nblk = NPS
                    else:
                        for d in range(6):
                            for kl in range(2):
                                lo = max(0, 2 * d + kl - 11)
                                hi = min(4, 2 * d + kl + 4)
                                if hi <= lo:
                                    continue
                                m0 = 11 - 2 * d - kl + lo
                                src = ins["na_rexp"][h, m0:m0 + hi - lo].rearrange("m k q -> k m q")
                                self.dma(Tt[kl * 64:(kl + 1) * 64, d, lo * 64:hi * 64].rearrange("p (a b) -> p a b", b=64), src, writes=['nTt'])
                        for d in range(6):
                            S.op('dve', (lambda e, d=d: e.tensor_tensor(out=Tt[:, d, :], in0=Tt[:, d, :], in1=cm[:], op=ALU.add)),
                                 reads=['nTt', 'ncm'], writes=['nTt'])
                        S.op('act', lambda e: e.activation(out=ExT[:], in_=Tt[:], func=AF.Exp), reads=['nTt'], writes=['nExT'])

                        def wins_of(jb):
                            return [(d, 2 * jb - 2 + d) for d in range(6) if 0 <= 2 * jb - 2 + d < 16]

                        def stageA(jb, pb, pr=pr):
                            q0 = jb * NQ
                            wins = wins_of(jb)
                            for wi, (d, ck) in enumerate(wins):
                                bank = self.bank_any.next()

                                def fs(e, bank=bank, ck=ck):
                                    e.matmul(ps[:, bank, 0:NQ], lhsT=kT[pr, ck * 128:(ck + 1) * 128], rhs=qT[pr, q0:q0 + NQ], start=True, stop=False)
                                    return e.matmul(ps[:, bank, 0:NQ], lhsT=rowsel[:, ck, :], rhs=RM[:, q0:q0 + NQ], start=False, stop=True)
                                S.op('pe', fs, reads=qk_all + ['nrowsel', 'nRM'], writes=[('ps', bank)])
                                tb = tmpf_rr.next()
                                S.op('act', (lambda e, bank=bank, tb=tb: e.activation(out=etmp[tb][:], in_=ps[:, bank, 0:NQ], func=AF.Exp, scale=SCALE)),
                                     reads=[('ps', bank)], writes=[('netmp', tb)])
                                S.op('dve', (lambda e, tb=tb, wi=wi, d=d: e.tensor_tensor(out=PT[pb][:, wi, :], in0=etmp[tb][:], in1=ExT[:, d, :], op=ALU.mult)),
                                     reads=[('netmp', tb), 'nExT'], writes=[('nPTs', pb, wi)])
                            for tc in range(4):
                                bank = self.bank_any.next()
                                self.mm(ps[:, bank, 0:NQ], [(kcT[pr, tc * 128:(tc + 1) * 128], qT[pr, q0:q0 + NQ])], qk_all + ['nkcT'], bank)
                                S.op('act', (lambda e, bank=bank, tc=tc, nw=len(wins): e.activation(out=PT[pb][:, nw + tc, :], in_=ps[:, bank, 0:NQ], func=AF.Exp, scale=SCALE)),
                                     reads=[('ps', bank)], writes=[('nPTs', pb, len(wins) + tc)])

                        def stageB(jb, pb, pr=pr, half=half):
                            q0 = jb * NQ
                            wins = wins_of(jb)
                            np_ = len(wins) + 4
                            bank2 = self.bank_any.next()
                            lhs = [vtok[:, ck, :] for (d, ck) in wins] + [cvb[:, tc, :] for tc in range(4)]

                            def fo(e):
                                for q in range(np_):
                                    e.matmul(ps[:, bank2, 0:NQ], lhsT=lhs[q], rhs=PT[pb][:, q, :], start=(q == 0), stop=(q == np_ - 1), skip_group_check=True)
                                for q in range(np_):
                                    r = e.matmul(ps[:, bank2, NQ:2 * NQ], lhsT=ones_b, rhs=PT[pb][:, q, :], start=False, stop=(q == np_ - 1), skip_group_check=True)
                                return r
                            S.op('pe', fo, reads=[('nPTs', pb, q) for q in range(np_)] + ['cb', 'ncvb'] + v_all, writes=[('ps', bank2)])
                            S.op('act', lambda e: e.activation(out=rec[pr, 0:NQ], in_=ps[pr, bank2, NQ:2 * NQ], func=AF.Ln),
                                 reads=[('ps', bank2)], writes=['nrec'])
                            S.op('act', lambda e: e.activation(out=rec[pr, 0:NQ], in_=rec[pr, 0:NQ], func=AF.Exp, scale=-1.0),
                                 reads=['nrec'], writes=['nrec'])
                            S.op('dve', lambda e: e.tensor_tensor(out=oT[pr, q0:q0 + NQ], in0=ps[pr, bank2, 0:NQ], in1=rec[pr, 0:NQ], op=ALU.mult),
                                 reads=[('ps', bank2), 'nrec'], writes=[('no', half)])
                        nblk = TS // NQ
                    prevb = None
                    for blk in range(nblk):
                        stageA(blk, blk % 2)
                        if prevb is not None:
                            stageB(prevb, prevb % 2)
                        prevb = blk
                    stageB(prevb, prevb % 2)
                if grp == 1 and c == 0:
                    self.dump_tile("noTs", oT[:], [128, TS], [('no', 0), ('no', 1)], BF16)
                    self.dump_tile("nvtoks", vtok[:], [128, 16, 128], v_all, BF16)
                    self.dump_tile("ncvb", cvb[:], [128, 4, 128], ['ncvb'], BF16)
                    self.dump_tile("nrec", rec[:], [128, 512], ['nrec'])
                if grp == 0 and c == 0:
                    self.dump_tile("noT", oT[:], [128, TP], [('no', 0), ('no', 1)], BF16)
                    self.dump_tile("nqT", qT[:], [128, TP], qk_all, BF16)
                    self.dump_tile("nkT", kT[:], [128, TP], qk_all, BF16)
                    self.dump_tile("nvtok", vtok[:], [128, TP // 128, 128], v_all, BF16)
                for t0 in range(0, gn, 512):
                    for oc in range(KC):
                        bank = self.bank_any.next()
                        self.mm(ps[:, bank, :], [(Wo[:, oc * 128:(oc + 1) * 128], oT[:, t0:t0 + 512])], [('nW', b, 3), ('no', 0), ('no', 1)], bank)
                        self.xupdate(l, 1, g0 + t0, 512, oc, ps[:, bank, :], [('ps', bank)])
            S.barrier()
            M.release()
        do_group(0)
        do_group(1)
        M.release()

    def mixer_fourier(self, l):
        S, M, ins, ps = self.S, self.M, self.ins, self.ps
        NTK = 256
        M.mark()
        Wout = M.alloc([128, KC, D], BF16, "fWout")
        boT = M.alloc([128, KC], F32, "fbo")
        Y1 = M.alloc([128, 16, D], BF16, "fY1")
        Y2 = M.alloc([128, 16, D], BF16, "fY2")
        tmp = [M.alloc([128, NTK], F32, "ftmp") for _ in range(2)]
        tmp_rr = RR([0, 1])
        self.dma(Wout[:], ins["fn_w_out"].rearrange("(k p) n -> p k n", p=128), writes=['fWout'], q='pool')
        self.dma(boT[:], ins["fn_b_outT"], writes=['fbo'])
        hk = [('fh', k) for k in range(KC)]
        tabs = {}

        def step1(t0, hT, ych, ykeys):
            self.norm_mod_range(l, 1, t0, NTK, hT, 0, hk)
            for tc in range(NTK // 128):
                for tab, Yd, eng in ((tabs['c'], Y1, 'act'), (tabs['s'], Y2, 'dve')):
                    for g0 in (0, 2):
                        bank = self.bank_any.next()

                        def f(e, tab=tab, g0=g0, bank=bank, tc=tc):
                            for gi in range(2):
                                g = g0 + gi
                                for kk in range(2):
                                    r = e.matmul(ps[:, bank, gi * 256:(gi + 1) * 256], lhsT=hT[:, 2 * g + kk, tc * 128:(tc + 1) * 128],
                                                 rhs=tab[:, kk, :], start=(kk == 0 and gi == 0), stop=(kk == 1), skip_group_check=True)
                            return r
                        S.op('pe', f, reads=hk[2 * g0:2 * g0 + 4] + ['fd256'], writes=[('ps', bank)])
                        c0 = g0 * 256
                        if eng == 'act':
                            S.op('act', (lambda e, Yd=Yd, bank=bank, tc=tc, c0=c0: e.activation(out=Yd[:, ych + tc, c0:c0 + 512], in_=ps[:, bank, :], func=AF.Identity)),
                                 reads=[('ps', bank)], writes=[ykeys[0]])
                        else:
                            S.op('dve', (lambda e, Yd=Yd, bank=bank, tc=tc, c0=c0: e.tensor_copy(out=Yd[:, ych + tc, c0:c0 + 512], in_=ps[:, bank, :])),
                                 reads=[('ps', bank)], writes=[ykeys[1]])

        def step23(t0, nsc, ych, CS, NS, fT, ykeys, tkeys):
            for cc in range(KC):
                bank = self.bank_any.next()
                pairs = []
                for sc in range(nsc):
                    pairs.append((Y1[:, ych + sc, cc * 128:(cc + 1) * 128], CS(sc)))
                    pairs.append((Y2[:, ych + sc, cc * 128:(cc + 1) * 128], NS(sc)))
                self.mm(ps[:, bank, 0:NTK], pairs, ykeys + tkeys, bank)
                S.op('act', (lambda e, cc=cc, bank=bank: e.activation(out=fT[:, cc, :], in_=ps[:, bank, 0:NTK], func=AF.Identity)),
                     reads=[('ps', bank)], writes=[('ffT', cc)])
            for oc in range(KC):
                bank = self.bank_any.next()
                self.mm(ps[:, bank, 0:NTK], [(Wout[:, kc, oc * 128:(oc + 1) * 128], fT[:, kc, :]) for kc in range(KC)],
                        ['fWout'] + [('ffT', cc) for cc in range(KC)], bank)
                tb = tmp_rr.next()
                S.op('act', (lambda e, oc=oc, bank=bank, tb=tb: e.activation(out=tmp[tb][:], in_=ps[:, bank, 0:NTK], func=AF.Identity,
                                                                            bias=boT[:, oc:oc + 1])),
                     reads=[('ps', bank), 'fbo'], writes=[('ftmp', tb)])
                self.xupdate(l, 1, t0, NTK, oc, tmp[tb][:], [('ftmp', tb)])

        M.mark()
        d256 = M.alloc([128, 3, 2, 256], BF16, "fd256")
        self.dma(d256[:], ins["dft256"], writes=['fd256'])
        c256, s256, ns256 = d256[:, 0], d256[:, 1], d256[:, 2]
        tabs['c'], tabs['s'] = c256, s256
        hT = M.alloc([128, KC, NTK], BF16, "fhT")
        fTp = M.alloc([128, KC, NTK], BF16, "ffTp")
        self.alloc_norm_scratch(NTK)
        for sq_ in range(NPS):
            t0 = sq_ * 256
            step1(t0, hT, 0, ['fY1p', 'fY2p'])
            step23(t0, 2, 0, (lambda sc: c256[:, sc, :]), (lambda sc: ns256[:, sc, :]), fTp, ['fY1p', 'fY2p'], ['fd256'])
        S.barrier()
        nq = TS // NTK
        for tq in range(nq):
            step1(TP + tq * NTK, hT, tq * 2, [('fY1s', tq), ('fY2s', tq)])
        ykeys = [('fY1s', tq) for tq in range(nq)] + [('fY2s', tq) for tq in range(nq)]
        S.barrier()
        M.release()
        M.mark()
        CSt = M.alloc([128, 16, NTK], BF16, "fCS")
        NSt = M.alloc([128, 16, NTK], BF16, "fNS")
        fT = M.alloc([128, KC, NTK], BF16, "ffT")
        cs_d = ins["dft2048"][0].rearrange("(k p) n -> p k n", p=128)
        ns_d = ins["dft2048"][1].rearrange("(k p) n -> p k n", p=128)
        for st in range(nq):
            self.dma(CSt[:], cs_d[:, :, st * NTK:(st + 1) * NTK], writes=['fCS'])
            self.dma(NSt[:], ns_d[:, :, st * NTK:(st + 1) * NTK], writes=['fNS'])
            step23(TP + st * NTK, 16, 0, (lambda sc: CSt[:, sc, :]), (lambda sc: NSt[:, sc, :]), fT, ykeys, ['fCS', 'fNS'])
        M.release()
        M.release()

    def mixer_gmlp(self, l):
        S, M, ins, ps = self.S, self.M, self.ins, self.ps
        NTK = 256
        NC_ = NTK // 128
        M.mark()
        Win = M.alloc([128, KC, 2 * D], BF16, "gWin")
        Wout = M.alloc([128, KC, D], BF16, "gWout")
        wsT = M.alloc([128, 4, 128], BF16, "gws")
        binT = M.alloc([128, 16], F32, "gbinT")
        binV = M.alloc([128, D], F32, "gbinV")
        vgb = M.alloc([128, D], F32, "gvgb")
        bsr = M.alloc([1, 512], F32, "gbsr")
        hT2 = [M.alloc([128, KC, NTK], BF16, "ghT") for _ in range(2)]
        uT2 = [M.alloc([128, KC, NTK], BF16, "guT") for _ in range(2)]
        vn2 = [M.alloc([128, NC_, D], BF16, "gvn") for _ in range(2)]
        vraw = M.alloc([128, D], F32, "gvraw")
        junk = M.alloc([128, 512], BF16, "gjunk")
        tmp = [M.alloc([128, 512], F32, "gtmp") for _ in range(2)]
        ssq = M.alloc([128, 2], F32, "gssq")
        rsv = M.alloc([128, 1], F32, "grsv")
        self.alloc_norm_scratch(NTK)
        epsT, ones_f = self.epsT, self.ones_f
        self.dma(Win[:], ins["gm_w_in"].rearrange("(k p) n -> p k n", p=128), writes=['gWin'], q='pool')
        self.dma(Wout[:], ins["gm_w_out"].rearrange("(k p) n -> p k n", p=128), writes=['gWout'], q='pool')
        self.dma(wsT[:], ins["gm_w_sT"], writes=['gws'], q='pool')
        self.dma(binT[:], ins["gm_b_inT"], writes=['gbinT'])
        self.dma(binV[:], ins["gm_b_in"][:, D:2 * D].partition_broadcast(128), writes=['gbinV'])
        self.dma(vgb[:], ins["gm_v_g"].partition_broadcast(128), writes=['gvgb'])
        self.dma(bsr[:], ins["gm_b_s"], writes=['gbsr'])
        tmp_rr = RR([0, 1])
        tiles = list(range(0, T, NTK))
        env = locals()
        for q, t0 in enumerate(tiles):
            self.gmlp_tile(l, t0, q % 2, env, 'A')
            if q > 0:
                self.gmlp_tile(l, tiles[q - 1], (q - 1) % 2, env, 'B')
        self.gmlp_tile(l, tiles[-1], (len(tiles) - 1) % 2, env, 'B')
        M.release()

    def gmlp_tile(self, l, t0, pb, env, stage):
        S, ps = self.S, self.ps
        NTK, NC_, Win, Wout, wsT, binT, binV, vgb, bsr = (env[k] for k in ("NTK", "NC_", "Win", "Wout", "wsT", "binT", "binV", "vgb", "bsr"))
        vraw, junk, tmp, ssq, rsv, tmp_rr, epsT, ones_f = (env[k] for k in ("vraw", "junk", "tmp", "ssq", "rsv", "tmp_rr", "epsT", "ones_f"))
        hT, uT, vn = env["hT2"][pb], env["uT2"][pb], env["vn2"][pb]
        hk = [('gh', pb, k) for k in range(KC)]
        if stage == 'A':
            self.norm_mod_range(l, 1, t0, NTK, hT, 0, hk)
            for oc in range(KC):
                bank = self.bank_any.next()
                self.mm(ps[:, bank, 0:NTK], [(Win[:, k, oc * 128:(oc + 1) * 128], hT[:, k, :]) for k in range(KC)], ['gWin'] + hk, bank)
                S.op('act', (lambda e, oc=oc, bank=bank: e.activation(out=uT[:, oc, :], in_=ps[:, bank, 0:NTK], func=AF.Gelu_apprx_tanh,
                                                                     bias=binT[:, oc:oc + 1])),
                     reads=[('ps', bank), 'gbinT'], writes=[('gu', pb, oc)])
            for tc in range(NC_):
                for half in range(2):
                    bank = self.bank_any.next()
                    self.mm(ps[:, bank, :], [(hT[:, k, tc * 128:(tc + 1) * 128], Win[:, k, D + half * 512:D + (half + 1) * 512]) for k in range(KC)],
                            ['gWin'] + hk, bank)
                    tb = tmp_rr.next()
                    S.op('dve', (lambda e, bank=bank, tb=tb, half=half: e.tensor_tensor(
                        out=tmp[tb][:], in0=ps[:, bank, :], in1=binV[:, half * 512:(half + 1) * 512], op=ALU.add)),
                        reads=[('ps', bank), 'gbinV'], writes=[('gtmp', tb)])
                    S.op('act', (lambda e, tb=tb, half=half: e.activation(out=vraw[:, half * 512:(half + 1) * 512], in_=tmp[tb][:],
                                                                         func=AF.Gelu_apprx_tanh)),
                         reads=[('gtmp', tb)], writes=[('gvraw', half)])
                    S.op('dve', (lambda e, half=half: e.scalar_tensor_tensor(
                        out=junk[:], in0=vraw[:, half * 512:(half + 1) * 512], scalar=1.0, in1=vraw[:, half * 512:(half + 1) * 512],
                        op0=ALU.mult, op1=ALU.mult, accum_out=ssq[:, half:half + 1])),
                        reads=[('gvraw', half)], writes=['gjunk', ('gssq', half)])
                S.op('dve', lambda e: e.tensor_tensor(out=rsv[:], in0=ssq[:, 0:1], in1=ssq[:, 1:2], op=ALU.add),
                     reads=[('gssq', 0), ('gssq', 1)], writes=['grsv'])
                S.op('act', lambda e: e.activation(out=rsv[:], in_=rsv[:], func=AF.Sqrt, scale=1.0 / D, bias=epsT[:, 0:1]),
                     reads=['grsv', 'eps'], writes=['grsv'])
                S.op('dve', lambda e: e.reciprocal(out=rsv[:], in_=rsv[:]), reads=['grsv'], writes=['grsv'])
                S.op('dve', (lambda e, tc=tc: e.scalar_tensor_tensor(
                    out=vn[:, tc, :], in0=vraw[:], scalar=rsv[:, 0:1], in1=vgb[:], op0=ALU.mult, op1=ALU.mult)),
                    reads=[('gvraw', 0), ('gvraw', 1), 'grsv', 'gvgb'], writes=[('gvn', pb, tc)])
        else:
            for oc in range(KC):
                g = oc // 2
                bank = self.bank_any.next()

                def f(e, oc=oc, g=g, bank=bank):
                    for tc in range(NC_):
                        e.matmul(ps[:, bank, tc * 128:(tc + 1) * 128], lhsT=vn[:, tc, oc * 128:(oc + 1) * 128], rhs=wsT[:, g, :],
                                 start=(tc == 0), stop=False, skip_group_check=True)
                        r = e.matmul(ps[:, bank, tc * 128:(tc + 1) * 128], lhsT=ones_f[0:1, :], rhs=bsr[0:1, g * 128:(g + 1) * 128],
                                     start=False, stop=True, skip_group_check=True)
                    return r
                S.op('pe', f, reads=[('gvn', pb, tc) for tc in range(NC_)] + ['gws', 'gbsr', 'cf'], writes=[('ps', bank)])
                S.op('dve', (lambda e, oc=oc, bank=bank: e.tensor_tensor(out=uT[:, oc, :], in0=uT[:, oc, :], in1=ps[:, bank, 0:NTK], op=ALU.mult)),
                     reads=[('gu', pb, oc), ('ps', bank)], writes=[('gu', pb, oc)])
            for oc in range(KC):
                bank = self.bank_any.next()
                self.mm(ps[:, bank, 0:NTK], [(Wout[:, k, oc * 128:(oc + 1) * 128], uT[:, k, :]) for k in range(KC)],
                        ['gWout'] + [('gu', pb, k) for k in range(KC)], bank)
                self.xupdate(l, 1, t0, NTK, oc, ps[:, bank, 0:NTK], [('ps', bank)])


def _consts():
    cf = np.zeros((9, 128, 128), np.float32)
    s = np.arange(128)[:, None]
    t = np.arange(128)[None, :]
    cf[0] = 1.0
    cf[1] = np.eye(128)
    cf[2] = (s <= t)
    cf[3] = (s >= t)
    cf[4] = np.where(t <= s, 0.0, NEG)
    cf[5] = np.where(t >= s, 0.0, NEG)
    cf[6] = np.where(s <= t, 0.0, -NEG)
    cf[7] = np.where(s >= t, 0.0, -NEG)
    cf[8] = np.eye(128)[::-1]
    cb = np.zeros((2, 128, 128), np.float32)
    cb[0] = np.eye(128)
    cb[1] = 1.0
    return (np.ascontiguousarray(cf.transpose(1, 0, 2)),
            np.ascontiguousarray(cb.transpose(1, 0, 2)).astype(ml_dtypes.bfloat16))


def _na_tables(rpb):
    bf = ml_dtypes.bfloat16
    dc = np.clip(np.arange(64)[:, None] - np.arange(64)[None, :] + 15, 0, 30)
    rexp = np.ascontiguousarray(rpb[:, ::-1, :][:, :, dc])
    qc = np.arange(64)
    qs = np.clip(qc - 8, 0, 48)
    kc = np.arange(64)[:, None]
    colmask = np.where((kc >= qs[None, :]) & (kc < qs[None, :] + 16), 0.0, NEG).astype(np.float32)
    cm = np.tile(colmask, (2, 4))
    rowsel = np.zeros((32, 16, 128), np.float32)
    for c in range(16):
        rowsel[2 * c, c, :64] = 1.0
        rowsel[2 * c + 1, c, 64:] = 1.0
    qr = np.arange(32)
    rs = np.clip(qr - 4, 0, 24)
    kr = np.arange(32)[:, None]
    rm = np.where((kr >= rs[None, :]) & (kr < rs[None, :] + 8), 0.0, NEG).astype(np.float32)
    rm = np.repeat(rm, 64, axis=1)
    return {"na_rexp": rexp, "na_cm": np.ascontiguousarray(cm), "na_rowsel": rowsel.astype(bf), "na_rm": np.ascontiguousarray(rm).astype(bf)}


def _fourier_tables():
    bf = ml_dtypes.bfloat16
    n = np.arange(256, dtype=np.float64)
    ang = 2.0 * np.pi * ((n[:, None] * n[None, :]) % 256) / 256.0
    c, sn = np.cos(ang) / 16.0, np.sin(ang) / 16.0
    t = np.stack([c, sn, -sn], 0).reshape(3, 2, 128, 256).transpose(2, 0, 1, 3)
    n2 = np.arange(2048, dtype=np.int64)
    ang2 = 2.0 * np.pi * ((n2[:, None] * n2[None, :]) % 2048).astype(np.float64) / 2048.0
    r = 1.0 / np.sqrt(2048.0)
    d2048 = np.stack([np.cos(ang2) * r, -np.sin(ang2) * r], 0)
    return {"dft256": np.ascontiguousarray(t).astype(bf), "dft2048": np.ascontiguousarray(d2048).astype(bf)}


def _fm(a):
    t, d = a.shape
    return np.ascontiguousarray(a.reshape(t, d // 128, 128).transpose(2, 1, 0))


def _fm_inv(a):
    p, c, t = a.shape
    return np.ascontiguousarray(a.transpose(2, 1, 0).reshape(t, c * p))


def _vecT(v):
    sh = v.shape
    n = sh[-1] // 128
    a = v.reshape(sh[:-1] + (n, 128))
    return np.ascontiguousarray(np.moveaxis(a, -1, 0))


def make_in_maps(inp, builder):
    cf, cb = _consts()
    maps = []
    f32 = lambda a: np.ascontiguousarray(np.asarray(a, dtype=np.float32))
    shared = {
        "b_adaT": _vecT(f32(inp["b_ada"])),
        "norm_gT": _vecT(f32(inp["norm_g"])),
        "final_gT": _vecT(f32(inp["final_g"])),
        "cf32": cf, "cbf16": cb,
        "w_ada": f32(inp["w_ada"]), "ffn_w1": f32(inp["ffn_w1"]), "ffn_w3": f32(inp["ffn_w3"]), "ffn_w2": f32(inp["ffn_w2"]),
    }
    shared.update(_fourier_tables())
    shared.update(_na_tables(f32(inp["na_rpb"][0])))
    shared.update({
        "ml_w_qkv": f32(inp["ml_w_qkv"][0]), "ml_w_og": f32(inp["ml_w_og"][0]), "ml_w_out": f32(inp["ml_w_out"][0]),
        "ml_w_if": np.ascontiguousarray(np.concatenate([f32(inp["ml_w_if"][0, 0]), f32(inp["ml_w_if"][0, 1])], axis=1)),
        "ml_b_if": f32(inp["ml_b_if"][0]).reshape(1, 16),
        "ml_head_gT": _vecT(f32(inp["ml_head_g"][0])),
    })
    shared.update({"na_w_qkv": f32(inp["na_w_qkv"][0]), "na_w_out": f32(inp["na_w_out"][0])})
    shared.update({
        "fn_w_out": f32(inp["fn_w_out"][0]), "fn_b_outT": _vecT(f32(inp["fn_b_out"][0])),
        "gm_w_in": f32(inp["gm_w_in"][0]), "gm_w_out": f32(inp["gm_w_out"][0]),
        "gm_b_inT": _vecT(f32(inp["gm_b_in"][0])), "gm_b_in": f32(inp["gm_b_in"]), "gm_v_g": f32(inp["gm_v_g"]),
        "gm_w_sT": np.ascontiguousarray(f32(inp["gm_w_s"][0]).transpose(2, 0, 1)),
        "gm_b_s": f32(inp["gm_b_s"][0]).reshape(1, 512),
    })
    for ci in range(8):
        b = ci // 4
        xp = f32(inp["x_prompt"][NPS * ci:NPS * (ci + 1)]).reshape(TP, D)
        xs = f32(inp["x_sample"][b])
        m = dict(shared)
        m["xT"] = _fm(np.concatenate([xp, xs], axis=0))
        cond = np.stack([f32(inp["c_ctx"]), f32(inp["c"][b])], axis=-1)
        m["condT"] = np.ascontiguousarray(cond.reshape(KC, 128, 2).transpose(1, 0, 2))
        m["ml_caug0"] = np.ascontiguousarray(np.concatenate([f32(inp["state_mlstm_C"][b, 0]), f32(inp["state_mlstm_n"][b, 0])[..., None]], axis=-1))
        m["ml_m0"] = f32(inp["state_mlstm_m"][b, 0]).reshape(1, 8)
        m["cache_k"] = f32(inp["cache_na_k"][b, 0]).reshape(512, D)
        m["cache_v"] = f32(inp["cache_na_v"][b, 0]).reshape(512, D)
        maps.append({k: v for k, v in m.items() if k in builder.ins})
    return maps


def kernel(**inp):
    B = Builder()
    nc = B.build()
    maps = make_in_maps(inp, B)
    res = run_bass_kernel_spmd(nc, maps, core_ids=list(range(8)))
    nb = 8 * NPS
    yp = np.zeros((nb, 256, D), np.float32)
    ys = np.zeros((2, 2048, D), np.float32)
    sC = np.zeros((nb, 1, 2, 4, 256, 256), np.float32)
    sn = np.zeros((nb, 1, 2, 4, 256), np.float32)
    sm = np.zeros((nb, 1, 2, 4), np.float32)
    ck = np.zeros((nb, 1, 256, 16, 64), np.float32)
    cv = np.zeros((nb, 1, 256, 16, 64), np.float32)
    for ci in range(8):
        r = res.results[ci]
        y = _fm_inv(np.asarray(r["yT"]))
        sl = slice(NPS * ci, NPS * (ci + 1))
        yp[sl] = y[:TP].reshape(NPS, 256, D)
        if ci % 4 == 0:
            ys[ci // 4] = y[TP:]
        ca = np.asarray(r["caug_out"])
        sC[sl, 0] = ca[..., :256]
        sn[sl, 0] = ca[..., 256]
        sm[sl, 0] = np.asarray(r["m_out"]).reshape(NPS, 2, 4)
        ck[sl, 0] = np.asarray(r["k_out"]).reshape(NPS, 256, 16, 64)
        cv[sl, 0] = np.asarray(r["v_out"]).reshape(NPS, 256, 16, 64)
    return yp, ys, sC, sn, sm, ck, cv
```

```python
import numpy as np
import ml_dtypes
import concourse.bass as bass
import concourse.mybir as mybir
from concourse.bass_utils import run_bass_kernel_spmd

F32 = mybir.dt.float32
BF16 = mybir.dt.bfloat16
AF = mybir.ActivationFunctionType
ALU = mybir.AluOpType
AX = mybir.AxisListType

L = 4
D = 1024
DFF = 2816
KC = 8
FC = 22
TP = 1024
TS = 2048
T = TP + TS
TT = 512
NT = T // TT
NPS = 4
EPS = 1e-6
NEG = -30000.0


class Sched:
    CE = ('pe', 'act', 'dve', 'pool')
    QE = ('pe', 'act', 'dve', 'pool', 'sp')

    SELF_WAIT = True

    def __init__(self, nc, n_dma_sems=32):
        self.nc = nc
        self.ops = {e: [] for e in self.QE}
        self.cnt = {e: 0 for e in self.CE}
        self.known = {e: {} for e in self.QE}
        self.last_w = {}
        self.readers = {}
        self.n_dma = n_dma_sems
        self.dma_val = [0] * n_dma_sems
        self.dma_rr = 0
        self.n_ops = 0
        self.n_waits = 0

    def op(self, eng, fn, reads=(), writes=(), dma=False):
        need = {}

        def add(tok):
            s, v = tok
            if need.get(s, 0) < v:
                need[s] = v
        for r in reads:
            t = self.last_w.get(r)
            if t:
                add(t)
        for w in writes:
            t = self.last_w.get(w)
            if t:
                add(t)
            for t in self.readers.get(w, ()):
                add(t)
        if dma:
            i = self.dma_rr
            self.dma_rr = (self.dma_rr + 1) % self.n_dma
            sname = ('dma', i)
            if self.dma_val[i] > 0 and need.get(sname, 0) < self.dma_val[i]:
                need[sname] = self.dma_val[i]
            self.dma_val[i] += 16
            tok = (sname, self.dma_val[i])
        else:
            self.cnt[eng] += 1
            tok = (eng, self.cnt[eng])
        waits = []
        for s, v in need.items():
            if s == eng and (eng == 'pe' or not self.SELF_WAIT):
                continue
            if self.known[eng].get(s, 0) < v:
                waits.append((s, v))
                self.known[eng][s] = v
        self.ops[eng].append((waits, fn, tok))
        self.n_ops += 1
        self.n_waits += len(waits)
        for r in reads:
            self.readers.setdefault(r, []).append(tok)
        for w in writes:
            self.last_w[w] = tok
            self.readers[w] = []
        return tok

    def _all_toks(self):
        toks = [(e, self.cnt[e]) for e in self.CE if self.cnt[e] > 0]
        toks += [(('dma', i), v) for i, v in enumerate(self.dma_val) if v > 0]
        return toks

    def barrier(self):
        toks = self._all_toks()
        for e in self.QE:
            waits = []
            for s, v in toks:
                if s == e:
                    continue
                if self.known[e].get(s, 0) < v:
                    waits.append((s, v))
                    self.known[e][s] = v
            if waits:
                self.ops[e].append((waits, None, None))
                self.n_waits += len(waits)
        self.last_w = {}
        self.readers = {}

    def final_wait(self, eng='sp'):
        waits = [(s, v) for s, v in self._all_toks() if s != eng and self.known[eng].get(s, 0) < v]
        self.ops[eng].append((waits, None, None))

    def emit(self):
        nc = self.nc
        from contextlib import ExitStack
        with ExitStack() as es:
            sems = {}
            for e in self.CE:
                sems[e] = es.enter_context(nc.semaphore("s_" + e))
            for i in range(self.n_dma):
                sems[('dma', i)] = es.enter_context(nc.semaphore("s_dma%d" % i))
            block = es.enter_context(nc.Block())

            def mk(ename):
                def run(eng):
                    for waits, fn, tok in self.ops[ename]:
                        for s, v in waits:
                            eng.wait_ge(sems[s], v)
                        if fn is None:
                            continue
                        ins = fn(eng)
                        if tok[0] == ename:
                            ins.then_inc(sems[tok[0]], 1)
                        else:
                            ins.then_inc(sems[tok[0]], 16)
                return run
            block.tensor(mk('pe'))
            block.scalar(mk('act'))
            block.vector(mk('dve'))
            block.gpsimd(mk('pool'))
            block.sync(mk('sp'))


class Mem:
    def __init__(self, nc, base=16512, limit=229344):
        self.nc = nc
        self.top = base
        self.limit = limit
        self.n = 0
        self.marks = []
        self.peak = base

    def alloc(self, shape, dtype, name=None):
        sz = int(np.prod(shape[1:])) * (2 if dtype == BF16 else 4)
        sz = (sz + 63) // 64 * 64
        off = self.top
        self.top += sz
        self.peak = max(self.peak, self.top)
        assert self.top <= self.limit, "SBUF overflow %d > %d" % (self.top, self.limit)
        self.n += 1
        return self.nc.alloc_sbuf_tensor_at("%s_%d" % (name or "t", self.n), list(shape), dtype, offset=off)

    def mark(self):
        self.marks.append(self.top)

    def release(self):
        self.top = self.marks.pop()


class RR:
    def __init__(self, items):
        self.items = list(items)
        self.i = 0

    def next(self):
        v = self.items[self.i]
        self.i = (self.i + 1) % len(self.items)
        return v


class _Stop(Exception):
    pass


class Builder:
    def __init__(self, dumps=(), layers=L, do_mixer=True, do_ffn=True, layer_list=None):
        self.nc = nc = bass.Bass("TRN2", target_bir_lowering=False)
        self.S = Sched(nc)
        self.M = Mem(nc)
        self.ins = {}
        self.outs = {}
        self.dumps = set(dumps)
        self.layers = layers
        self.layer_list = list(layer_list) if layer_list is not None else list(range(layers))
        self.do_mixer = do_mixer
        self.do_ffn = do_ffn
        self.ps = nc.alloc_psum_tensor("ps", [128, 8, 512], F32)
        self.psb = self.ps.bitcast(BF16)
        self.bank_a = RR([0, 1])
        self.bank_b = RR([2, 3])
        self.bank_y = RR([4, 5, 6, 7])
        self.bank_m = RR([6, 7])
        self.bank_any = RR(range(8))
        self.uid = 0

    def din(self, name, shape, dtype=F32):
        self.ins[name] = self.nc.dram_tensor(name, list(shape), dtype, kind="ExternalInput").ap()
        return self.ins[name]

    def dout(self, name, shape, dtype=F32):
        self.outs[name] = self.nc.dram_tensor(name, list(shape), dtype, kind="ExternalOutput").ap()
        return self.outs[name]

    def key(self, base):
        self.uid += 1
        return (base, self.uid)

    def dma(self, out, in_, reads=(), writes=(), q='sp'):
        return self.S.op(q, lambda e: e.dma_start(out=out, in_=in_), reads=reads, writes=writes, dma=True)

    def declare(self):
        din = self.din
        din("xT", [128, KC, T])
        din("condT", [128, KC, 2])
        din("b_adaT", [128, L, 72])
        din("norm_gT", [128, L, 3, KC])
        din("final_gT", [128, KC])
        din("cf32", [128, 9, 128])
        din("cbf16", [128, 2, 128], BF16)
        din("w_ada", [L, D, 9 * D])
        din("ffn_w1", [L, 2, D, DFF])
        din("ffn_w3", [L, 2, D, DFF])
        din("ffn_w2", [L, 2, DFF, D])
        din("ml_w_qkv", [D, 3 * D])
        din("ml_w_og", [D, D])
        din("ml_w_out", [D, D])
        din("ml_w_if", [D, 16])
        din("ml_b_if", [1, 16])
        din("ml_head_gT", [128, KC])
        din("ml_caug0", [2, 4, 256, 257])
        din("ml_m0", [1, 8])
        self.dout("caug_out", [NPS, 2, 4, 256, 257])
        self.dout("m_out", [NPS, 8])
        din("na_w_qkv", [D, 3 * D])
        din("na_w_out", [D, D])
        din("cache_k", [512, D])
        din("cache_v", [512, D])
        din("na_rexp", [16, 15, 64, 64])
        din("na_cm", [128, 256], BF16)
        din("na_rowsel", [32, 16, 128], BF16)
        din("na_rm", [32, TS], BF16)
        self.dout("k_out", [TP, D])
        self.dout("v_out", [TP, D])
        din("fn_w_out", [D, D])
        din("fn_b_outT", [128, KC])
        din("dft256", [128, 3, 2, 256], BF16)
        din("dft2048", [2, 2048, 2048], BF16)
        din("gm_w_in", [D, 2 * D])
        din("gm_w_out", [D, D])
        din("gm_b_inT", [128, 16])
        din("gm_b_in", [1, 2 * D])
        din("gm_v_g", [1, D])
        din("gm_w_sT", [128, 4, 128])
        din("gm_b_s", [1, 512])
        self.dout("yT", [128, KC, T])
        for d in sorted(self.dumps):
            if d.startswith("l") or d == "x0":
                self.dout("dbg_" + d, [128, KC, T])

    def dump_tile(self, name, tile, shape, reads, dtype=F32):
        if name not in self.dumps:
            return
        o = self.nc.dram_tensor("dbg_" + name, list(shape), dtype, kind="ExternalOutput").ap()
        self.outs["dbg_" + name] = o
        self.dma(o, tile, reads=reads)

    def dump(self, name):
        if name not in self.dumps:
            return
        o = self.outs["dbg_" + name]
        for tt in range(NT):
            sl = slice(tt * TT, (tt + 1) * TT)
            self.dma(o[:, :, sl], self.xT[:, :, sl], reads=[('x', k, tt) for k in range(KC)])

    def prologue(self):
        S, M, ins = self.S, self.M, self.ins
        self.xT = M.alloc([128, KC, T], F32, "xT")
        self.cf = M.alloc([128, 2, 128], F32, "cf")
        self.cb = M.alloc([128, 2, 128], BF16, "cb")
        self.mods = M.alloc([128, L, 72, 2], F32, "mods")
        self.modA = M.alloc([128, L, 3, 2, KC], F32, "modA")
        self.modC = M.alloc([128, L, 3, 2, KC], F32, "modC")
        self.badaT = M.alloc([128, L, 72], F32, "bada")
        self.ngT = M.alloc([128, L, 3, KC], F32, "ng")
        self.fgT = M.alloc([128, KC], F32, "fg")
        condT = M.alloc([128, KC, 2], F32, "cond")
        sc = M.alloc([128, KC, 2], BF16, "sc")
        self.ones_f = self.cf[:, 0, :]
        self.id_f = self.cf[:, 1, :]
        self.id_b = self.cb[:, 0, :]
        self.ones_b = self.cb[:, 1, :]
        self.dma(self.cf[:], ins["cf32"][:, 0:2, :], writes=['cf'])
        self.dma(self.cb[:], ins["cbf16"], writes=['cb'])
        self.dma(self.badaT[:], ins["b_adaT"], writes=['bada'])
        self.dma(self.ngT[:], ins["norm_gT"], writes=['ng'])
        self.dma(self.fgT[:], ins["final_gT"], writes=['fg'])
        self.dma(condT[:], ins["condT"], writes=['cond'])
        for tt in range(NT):
            sl = slice(tt * TT, (tt + 1) * TT)
            self.dma(self.xT[:, :, sl], ins["xT"][:, :, sl], writes=[('x', k, tt) for k in range(KC)])
        S.op('act', lambda e: e.activation(out=sc[:], in_=condT[:], func=AF.Silu), reads=['cond'], writes=['sc'])
        M.mark()
        wb = [M.alloc([128, KC, 1024], BF16, "wada") for _ in range(2)]
        ps = self.ps
        it = 0
        self.sc = sc
        for l in self.layer_list[:1]:
            wv = ins["w_ada"][l].rearrange("(k p) n -> p k n", p=128)
            for s in range(9):
                b = it % 2
                it += 1
                self.dma(wb[b][:], wv[:, :, s * 1024:(s + 1) * 1024], writes=[('wada', b)], q='pool')
                bank = self.bank_m.next()

                def mm(e, b=b, bank=bank):
                    for oc in range(8):
                        for k in range(KC):
                            ins_ = e.matmul(ps[:, bank, oc * 2:oc * 2 + 2], lhsT=wb[b][:, k, oc * 128:(oc + 1) * 128],
                                            rhs=sc[:, k, :], start=(k == 0 and oc == 0), stop=(k == KC - 1), skip_group_check=True)
                    return ins_
                S.op('pe', mm, reads=[('wada', b), 'sc'], writes=[('ps', bank)])
                for j in range(2):
                    S.op('dve', (lambda e, bank=bank, j=j, l=l, s=s: e.tensor_tensor(
                        out=self.mods[:, l, s * 8:(s + 1) * 8, j],
                        in0=ps[:, bank, 0:16].rearrange("p (c j) -> p c j", j=2)[:, :, j],
                        in1=self.badaT[:, l, s * 8:(s + 1) * 8], op=ALU.add)),
                        reads=[('ps', bank), 'bada'], writes=[('mods', l)])
        M.release()
        self.derive_mods(self.layer_list[0])

    def derive_mods(self, l):
        S = self.S
        if True:
            for i in range(3):
                for j in range(2):
                    S.op('dve', (lambda e, l=l, i=i, j=j: e.scalar_tensor_tensor(
                        out=self.modA[:, l, i, j, :], in0=self.mods[:, l, (3 * i + 1) * 8:(3 * i + 2) * 8, j], scalar=1.0,
                        in1=self.ngT[:, l, i, :], op0=ALU.add, op1=ALU.mult)),
                        reads=[('mods', l), 'ng'], writes=[('modA', l)])
                    S.op('dve', (lambda e, l=l, i=i, j=j: e.tensor_scalar(
                        out=self.modC[:, l, i, j, :], in0=self.mods[:, l, (3 * i + 2) * 8:(3 * i + 3) * 8, j],
                        scalar1=(1.0 if i == 1 else 0.5), scalar2=None, op0=ALU.mult)),
                        reads=[('mods', l)], writes=[('modC', l)])

    def shiftAP(self, l, i, j, k):
        return self.mods[:, l, 3 * i * 8 + k, j:j + 1]

    def alloc_norm_scratch(self, n=TT, share_tmp=None):
        M = self.M
        self.sq = M.alloc([128, KC, n], BF16, "sq")
        self.rstd = M.alloc([128, n], F32, "rstd")
        if share_tmp is not None:
            self.ntmp = share_tmp
        else:
            self.ntmp = [M.alloc([128, n], F32, "ntmp") for _ in range(2)]
        self.ntmp_rr = RR([0, 1])

    def xkeys(self, t0, n):
        return [('x', k, tt) for k in range(KC) for tt in range(t0 // TT, (t0 + n - 1) // TT + 1)]

    def rstd_range(self, t0, n):
        S, ps = self.S, self.ps
        sq, rstd, xT, ones_b, epsT = self.sq, self.rstd, self.xT, self.ones_b, self.epsT
        sl = slice(t0, t0 + n)
        S.op('act', lambda e: e.activation(out=sq[:, :, 0:n], in_=xT[:, :, sl], func=AF.Square), reads=self.xkeys(t0, n), writes=['sq'])
        bank = self.bank_m.next()
        self.mm(ps[:, bank, 0:n], [(ones_b, sq[:, k, 0:n]) for k in range(KC)], ['sq', 'cb'], bank)
        S.op('act', lambda e: e.activation(out=rstd[:, 0:n], in_=ps[:, bank, 0:n], func=AF.Sqrt, scale=1.0 / D, bias=epsT[:, 0:1]),
             reads=[('ps', bank), 'eps'], writes=['rstd'])
        S.op('dve', lambda e: e.reciprocal(out=rstd[:, 0:n], in_=rstd[:, 0:n]), reads=['rstd'], writes=['rstd'])

    def rstd_tile(self, tt):
        self.rstd_range(tt * TT, TT)

    def norm_mod(self, l, i, tts, hT, hoff=0):
        for tt in tts:
            self.norm_mod_range(l, i, tt * TT, TT, hT, tt * TT - hoff, [('h', k, tt) for k in range(KC)])

    def norm_mod_range(self, l, i, t0, n, hT, h0, hkeys, kset=None):
        S = self.S
        j = 0 if t0 < TP else 1
        sl = slice(t0, t0 + n)
        hsl = slice(h0, h0 + n)
        self.rstd_range(t0, n)
        ntmp, xT, modA, rstd, mods = self.ntmp, self.xT, self.modA, self.rstd, self.mods
        xk = self.xkeys(t0, n)
        for k in (range(KC) if kset is None else kset):
            tb = self.ntmp_rr.next()
            S.op('dve', (lambda e, k=k, tb=tb: e.scalar_tensor_tensor(
                out=ntmp[tb][:, 0:n], in0=xT[:, k, sl], scalar=modA[:, l, i, j, k:k + 1], in1=rstd[:, 0:n],
                op0=ALU.mult, op1=ALU.mult)), reads=[xk[kk] for kk in range(len(xk)) if xk[kk][1] == k] + ['rstd', ('modA', l)],
                writes=[('ntmp', tb)])
            S.op('act', (lambda e, k=k, tb=tb: e.activation(
                out=hT[:, k, hsl], in_=ntmp[tb][:, 0:n], func=AF.Identity, bias=mods[:, l, 3 * i * 8 + k, j:j + 1])),
                reads=[('ntmp', tb), ('mods', l)], writes=[hkeys[k]])

    def mm(self, out, pairs, reads, bank):
        pairs = list(pairs)

        def f(e):
            n = len(pairs)
            for q, (lt, r) in enumerate(pairs):
                ins = e.matmul(out, lhsT=lt, rhs=r, start=(q == 0), stop=(q == n - 1))
            return ins
        return self.S.op('pe', f, reads=reads, writes=[('ps', bank)])

    def xupdate(self, l, si, t0, n, oc, psrc, reads):
        j = 0 if t0 < TP else 1
        xT, modC = self.xT, self.modC
        sl = slice(t0, t0 + n)
        xk = [('x', oc, tt) for tt in range(t0 // TT, (t0 + n - 1) // TT + 1)]
        self.S.op('dve', lambda e: e.scalar_tensor_tensor(
            out=xT[:, oc, sl], in0=psrc, scalar=modC[:, l, si, j, oc:oc + 1], in1=xT[:, oc, sl],
            op0=ALU.mult, op1=ALU.add), reads=list(reads) + xk + [('modC', l)], writes=xk)

    G = 2

    def ffn_groups(self):
        return [(c0, min(self.G, FC - c0)) for c0 in range(0, FC, self.G)]

    def alloc_ffn(self):
        M = self.M
        G = self.G
        self.w1b = [M.alloc([128, KC, G * 128], BF16, "w1b") for _ in range(2)]
        self.w3b = [M.alloc([128, KC, G * 128], BF16, "w3b") for _ in range(2)]
        self.w2b = [M.alloc([128, G, D], BF16, "w2b") for _ in range(2)]
        self.ub = [M.alloc([128, G, TT], BF16, "ub") for _ in range(2)]
        self.slb = [M.alloc([128, TT], F32, "ntmp") for _ in range(2)]
        self.wbuf_i = 0

    def ffn_load(self, l, i, gi):
        c0, g = self.ffn_groups()[gi]
        b = self.wbuf_i % 2
        self.wbuf_i += 1
        ins = self.ins
        w1v = ins["ffn_w1"][l, i].rearrange("(k p) n -> p k n", p=128)
        w3v = ins["ffn_w3"][l, i].rearrange("(k p) n -> p k n", p=128)
        w2v = ins["ffn_w2"][l, i].rearrange("(g p) n -> p g n", p=128)
        self.dma(self.w1b[b][:, :, 0:g * 128], w1v[:, :, c0 * 128:(c0 + g) * 128], writes=[('w1', b)], q='pool')
        self.dma(self.w3b[b][:, :, 0:g * 128], w3v[:, :, c0 * 128:(c0 + g) * 128], writes=[('w3', b)], q='pool')
        self.dma(self.w2b[b][:, 0:g, :], w2v[:, c0:c0 + g, :], writes=[('w2', b)], q='pool')
        return b

    def ada_unit(self, ln, su):
        S, ps, ins, sc = self.S, self.ps, self.ins, self.sc
        wv = ins["w_ada"][ln].rearrange("(k p) n -> p k n", p=128)
        nb = len(self.aslab)
        if su == 0:
            for q in range(min(2, 36)):
                self.dma(self.aslab[q % nb][:], wv[:, :, q * 256:(q + 1) * 256], writes=[('aslab', q % nb)], q='pool')
        if su + 2 < 36:
            q = su + 2
            self.dma(self.aslab[q % nb][:], wv[:, :, q * 256:(q + 1) * 256], writes=[('aslab', q % nb)], q='pool')
        b = su % nb
        wb = self.aslab[b]
        bank = self.bank_m.next()

        def mm(e):
            for oc in range(2):
                for k in range(KC):
                    r = e.matmul(ps[:, bank, oc * 2:oc * 2 + 2], lhsT=wb[:, k, oc * 128:(oc + 1) * 128], rhs=sc[:, k, :],
                                 start=(k == 0 and oc == 0), stop=(k == KC - 1), skip_group_check=True)
            return r
        S.op('pe', mm, reads=[('aslab', b), 'sc'], writes=[('ps', bank)])
        mods, badaT = self.mods, self.badaT
        for j in range(2):
            S.op('dve', (lambda e, j=j: e.tensor_tensor(
                out=mods[:, ln, su * 2:(su + 1) * 2, j], in0=ps[:, bank, 0:4].rearrange("p (c j) -> p c j", j=2)[:, :, j],
                in1=badaT[:, ln, su * 2:(su + 1) * 2], op=ALU.add)), reads=[('ps', bank), 'bada'], writes=[('mods', ln)])

    def ffn(self, l, i, hT, first_buf, norm_i=None, ada_next=None):
        S, ps = self.S, self.ps
        groups = self.ffn_groups()
        si = 0 if i == 0 else 2
        w1b, w3b, w2b, ubuf, slb, xT, modC = self.w1b, self.w3b, self.w2b, self.ub, self.slb, self.xT, self.modC
        nxt = first_buf
        ucnt = [0]

        def U_part(b, g, tt, ub, jj):
            sl = slice(tt * TT, (tt + 1) * TT)
            ba = self.bank_a.next()
            bb = self.bank_b.next()
            hk = [('h', k, tt) for k in range(KC)]
            self.mm(ps[:, ba, :], [(w1b[b][:, k, jj * 128:(jj + 1) * 128], hT[:, k, sl]) for k in range(KC)], [('w1', b)] + hk, ba)
            self.mm(ps[:, bb, :], [(w3b[b][:, k, jj * 128:(jj + 1) * 128], hT[:, k, sl]) for k in range(KC)], [('w3', b)] + hk, bb)
            sb = self.sl_rr.next()
            S.op('act', lambda e: e.activation(out=slb[sb][:], in_=ps[:, ba, :], func=AF.Silu), reads=[('ps', ba)], writes=[('ntmp', sb)])
            S.op('dve', lambda e: e.tensor_tensor(out=ubuf[ub][:, jj, :], in0=slb[sb][:], in1=ps[:, bb, :], op=ALU.mult),
                 reads=[('ntmp', sb), ('ps', bb)], writes=[('u', ub, jj)])

        def Y_part(b, g, tt, ub, oc):
            j = 0 if tt < TP // TT else 1
            sl = slice(tt * TT, (tt + 1) * TT)
            by = self.bank_y.next()
            self.mm(ps[:, by, :], [(w2b[b][:, jj, oc * 128:(oc + 1) * 128], ubuf[ub][:, jj, :]) for jj in range(g)],
                    [('w2', b)] + [('u', ub, jj) for jj in range(g)], by)
            S.op('dve', lambda e: e.scalar_tensor_tensor(
                out=xT[:, oc, sl], in0=ps[:, by, :], scalar=modC[:, l, si, j, oc:oc + 1], in1=xT[:, oc, sl],
                op0=ALU.mult, op1=ALU.add), reads=[('ps', by), ('x', oc, tt), ('modC', l)], writes=[('x', oc, tt)])

        ada_cnt = [0]
        for gi, (c0, g) in enumerate(groups):
            b = nxt
            if gi + 1 < len(groups):
                nxt = self.ffn_load(l, i, gi + 1)
            prev = None
            for step in range(NT + 1):
                cur = None
                ub = ucnt[0] % 2
                if step < NT:
                    ucnt[0] += 1
                    cur = (step, ub)
                nparts = g if step < NT else 1
                per = (KC + nparts - 1) // nparts
                if ada_next is not None and ada_cnt[0] < 36 and (gi * (NT + 1) + step) % 2 == 0:
                    self.ada_unit(ada_next, ada_cnt[0])
                    ada_cnt[0] += 1
                if norm_i is not None and gi == 0 and step + 1 < NT:
                    self.norm_mod(l, norm_i, [step + 1], hT)
                for part in range(nparts):
                    if step < NT:
                        U_part(b, g, step, ub, part)
                    if prev is not None:
                        for oc in range(part * per, min(KC, (part + 1) * per)):
                            Y_part(b, g, prev[0], prev[1], oc)
                prev = cur
        if ada_next is not None:
            while ada_cnt[0] < 36:
                self.ada_unit(ada_next, ada_cnt[0])
                ada_cnt[0] += 1
            self.derive_mods(ada_next)

    def final(self):
        M = self.M
        M.mark()
        self.alloc_norm_scratch()
        yb = [M.alloc([128, KC, TT], F32, "yb") for _ in range(2)]
        for tt in range(NT):
            self.final_tile(tt, yb)
        M.release()

    def final_tile(self, tt, yb):
        S = self.S
        sl = slice(tt * TT, (tt + 1) * TT)
        self.rstd_tile(tt)
        b = tt % 2
        xT, fgT, rstd = self.xT, self.fgT, self.rstd
        for k in range(KC):
            S.op('dve', (lambda e, k=k: e.scalar_tensor_tensor(
                out=yb[b][:, k, :], in0=xT[:, k, sl], scalar=fgT[:, k:k + 1], in1=rstd[:],
                op0=ALU.mult, op1=ALU.mult)), reads=[('x', k, tt), 'rstd', 'fg'], writes=[('yb', b, k)])
        self.dma(self.outs["yT"][:, :, sl], yb[b][:], reads=[('yb', b, k) for k in range(KC)])

    def build(self):
        S, M = self.S, self.M
        self.declare()
        self.epsT = M.alloc([128, 1], F32, "eps")
        S.op('dve', lambda e: e.memset(self.epsT[:], EPS), writes=['eps'])
        self.prologue()
        S.barrier()
        self.dump("x0")
        self.dump_tile("mods", self.mods[:], [128, L, 72, 2], [('mods', l) for l in range(L)])
        self.dump_tile("modA", self.modA[:], [128, L, 3, 2, KC], [('modA', l) for l in range(L)])
        self.dump_tile("modC", self.modC[:], [128, L, 3, 2, KC], [('modC', l) for l in range(L)])
        for l in self.layer_list:
            for i in range(3):
                if i == 1:
                    if self.do_mixer:
                        self.mixer(l)
                    self.dump("l%d_b" % l)
                    continue
                if not self.do_ffn:
                    continue
                M.mark()
                hT = M.alloc([128, KC, T], BF16, "hT")
                self.alloc_ffn()
                self.alloc_norm_scratch(TT, share_tmp=self.slb)
                self.sl_rr = self.ntmp_rr
                li = self.layer_list.index(l)
                ada_next = self.layer_list[li + 1] if (i == 2 and li + 1 < len(self.layer_list)) else None
                if ada_next is not None:
                    self.aslab = [M.alloc([128, KC, 256], BF16, "aslab") for _ in range(3)]
                fi = 0 if i == 0 else 1
                first = self.ffn_load(l, fi, 0)
                self.norm_mod(l, i, [0], hT)
                self.ffn(l, fi, hT, first, norm_i=i, ada_next=ada_next)
                M.release()
                S.barrier()
                self.dump("l%d_%s" % (l, "a" if i == 0 else "c"))
        self.final()
        S.final_wait('sp')
        S.emit()
        return self.nc

    def mixer(self, l):
        kind = l % 4
        if kind == 2:
            self.mixer_gmlp(l)
        elif kind == 1:
            self.mixer_fourier(l)
        elif kind == 3:
            self.mixer_na(l)
        elif kind == 0:
            nm = len(self.M.marks)
            try:
                if getattr(self, "old_mlstm", False):
                    self.mixer_mlstm(l)
                else:
                    self.mixer_mlstm2(l)
            except _Stop:
                while len(self.M.marks) > nm:
                    self.M.release()
        self.S.barrier()

    def mixer_mlstm2(self, l):
        S, M, ins, ps, psb = self.S, self.M, self.ins, self.ps, self.psb
        ones_f, id_f, id_b, epsT = self.ones_f, self.id_f, self.id_b, self.epsT
        wqkv_d = ins["ml_w_qkv"].rearrange("(k p) n -> p k n", p=128)
        wog_d = ins["ml_w_og"].rearrange("(k p) n -> p k n", p=128)
        wout_d = ins["ml_w_out"].rearrange("(k p) n -> p k n", p=128)
        M.mark()
        cfm = M.alloc([128, 3, 128], F32, "mcfm")
        self.dma(cfm[:, 0:2, :], ins["cf32"][:, 2:4, :], writes=['mcfm'])
        self.dma(cfm[:, 2, :], ins["cf32"][:, 8, :], writes=['mcfm'])
        Jf = cfm[:, 2, :]
        Wif = M.alloc([128, KC, 16], BF16, "mWif")
        self.dma(Wif[:], ins["ml_w_if"].rearrange("(k p) n -> p k n", p=128), writes=['mWif'], q='pool')
        bif = M.alloc([128, 16], F32, "mbif")
        self.dma(bif[:], ins["ml_b_if"].partition_broadcast(128), writes=['mbif'])
        hgT = M.alloc([128, KC], F32, "mhg")
        self.dma(hgT[:], ins["ml_head_gT"], writes=['mhg'])
        m0b = M.alloc([128, 8], F32, "mm0")
        self.dma(m0b[:], ins["ml_m0"].partition_broadcast(128), writes=['mm0'])
        uid = [0]

        def K_(name):
            uid[0] += 1
            return (name, uid[0])

        def do_group(grp):
            M.mark()
            g0, gn = (0, TP) if grp == 0 else (TP, TS)
            NCH = gn // 128
            NCOL = NCH * 8
            nseq, cps = (NPS, 2) if grp == 0 else (1, 16)
            hT = M.alloc([128, KC, gn], BF16, "mhT")
            hk = [('mh', k, tq) for k in range(KC) for tq in range(gn // 256)]
            Uu = M.alloc([128, NCH, 8], F32, "mU")
            Wi = M.alloc([128, NCH, 8], F32, "mWi")
            Fl = M.alloc([128, NCH, 8], F32, "mFl")
            Ws = M.alloc([128, NCH, 8], F32, "mWs")
            Dc = M.alloc([128, NCH, 8], F32, "mDc")
            mfin = M.alloc([128, nseq, 8], F32, "mfin")
            M.mark()
            self.alloc_norm_scratch(256)
            for tq in range(gn // 256):
                self.norm_mod_range(l, 1, g0 + tq * 256, 256, hT, tq * 256, [('mh', k, tq) for k in range(KC)])
            S.barrier()
            M.release()
            M.mark()
            gpre = M.alloc([128, NCH, 16], F32, "mgpre")
            gi = M.alloc([128, NCH, 8], F32, "mgi")
            gl = M.alloc([128, NCH, 8], F32, "mgl")
            bb = M.alloc([128, NCH, 8], F32, "mb")
            bt = M.alloc([128, NCH, 8], F32, "mbt")
            gg = M.alloc([128, NCH, 8], F32, "mg")
            cmx = M.alloc([128, NCH, 8], F32, "mcmx")
            gmx = M.alloc([128, NCH, 8], F32, "mgmx")
            MP = M.alloc([128, NCH, 8], F32, "mMP")
            ML = M.alloc([128, NCH, 8], F32, "mML")
            Mt = M.alloc([128, NCH, 8], F32, "mMt")
            tq_ = M.alloc([128, NCH, 8], F32, "mtq")
            Rf = M.alloc([128, 128], F32, "mRf")
            Rb = M.alloc([128, 128], F32, "mRb")
            Cf = M.alloc([128, 128], F32, "mCf")
            Cb_ = M.alloc([128, 128], F32, "mCb")
            Tb = M.alloc([128, 128], F32, "mTb")
            rmax = M.alloc([128, 1], F32, "mrmax")
            dg = M.alloc([128, 128], F32, "mdg")
            for c8 in range(0, NCH, 8):
                bank = self.bank_any.next()

                def fg(e, bank=bank, c8=c8):
                    for ci in range(8):
                        ch = c8 + ci
                        for k in range(KC):
                            r = e.matmul(ps[:, bank, ci * 16:(ci + 1) * 16], lhsT=hT[:, k, ch * 128:(ch + 1) * 128], rhs=Wif[:, k, :],
                                         start=(k == 0 and ci == 0), stop=(k == KC - 1), skip_group_check=True)
                    return r
                S.op('pe', fg, reads=hk + ['mWif'], writes=[('ps', bank)])
                for ci in range(8):
                    S.op('dve', (lambda e, bank=bank, ci=ci, c8=c8: e.tensor_tensor(out=gpre[:, c8 + ci, :], in0=ps[:, bank, ci * 16:(ci + 1) * 16], in1=bif[:], op=ALU.add)),
                         reads=[('ps', bank), 'mbif'], writes=['mgpre'])
            gpv = gpre[:].rearrange("p c (d x) -> p c d x", d=2)
            v4 = lambda t: t[:].rearrange("p c (d h) -> p c d h", d=2)
            S.op('dve', lambda e: e.tensor_copy(out=v4(gi), in_=gpv[:, :, :, 0:4]), reads=['mgpre'], writes=['mgi'])
            S.op('act', lambda e: e.activation(out=v4(gl), in_=gpv[:, :, :, 4:8], func=AF.Exp, scale=-1.0), reads=['mgpre'], writes=['mgl'])
            S.op('act', lambda e: e.activation(out=gl[:], in_=gl[:], func=AF.Ln, bias=1.0), reads=['mgl'], writes=['mgl'])
            S.op('dve', lambda e: e.tensor_scalar(out=gl[:], in0=gl[:], scalar1=-1.0, scalar2=None, op0=ALU.mult), reads=['mgl'], writes=['mgl'])
            for c8 in range(0, NCH, 8):
                bank = self.bank_any.next()

                def fc(e, bank=bank, c8=c8):
                    for ci in range(8):
                        ch = c8 + ci
                        e.matmul(ps[:, bank, ci * 16:ci * 16 + 4], lhsT=cfm[:, 0, :], rhs=gl[:, ch, 0:4], start=True, stop=True)
                        e.matmul(ps[:, bank, ci * 16 + 4:ci * 16 + 8], lhsT=cfm[:, 1, :], rhs=gl[:, ch, 4:8], start=True, stop=True)
                        r = e.matmul(ps[:, bank, ci * 16 + 8:ci * 16 + 16], lhsT=ones_f, rhs=gl[:, ch, :], start=True, stop=True)
                    return r
                S.op('pe', fc, reads=['mgl', 'mcfm', 'cf'], writes=[('ps', bank)])
                pv = ps[:, bank, 0:128].rearrange("p (c x) -> p c x", x=16)
                S.op('dve', (lambda e, pv=pv, c8=c8: e.tensor_copy(out=bb[:, c8:c8 + 8, :], in_=pv[:, :, 0:8])), reads=[('ps', bank)], writes=['mb'])
                S.op('dve', (lambda e, pv=pv, c8=c8: e.tensor_copy(out=bt[:, c8:c8 + 8, :], in_=pv[:, :, 8:16])), reads=[('ps', bank)], writes=['mbt'])
            S.op('dve', lambda e: e.tensor_tensor(out=gg[:], in0=gi[:], in1=bb[:], op=ALU.subtract), reads=['mgi', 'mb'], writes=['mg'])
            ggf = gg[:].rearrange("p c k -> p (c k)")
            bankf = self.bank_any.next()
            self.mm(ps[0:NCOL, bankf, 0:128], [(ggf, id_f)], ['mg', 'cf'], bankf)
            bankb = self.bank_any.next()
            self.mm(ps[0:NCOL, bankb, 0:128], [(ggf, Jf)], ['mg', 'mcfm'], bankb)
            S.op('act', lambda e: e.activation(out=Rf[0:NCOL, :], in_=ps[0:NCOL, bankf, 0:128], func=AF.Identity), reads=[('ps', bankf)], writes=['mRf'])
            S.op('act', lambda e: e.activation(out=Rb[0:NCOL, :], in_=ps[0:NCOL, bankb, 0:128], func=AF.Identity), reads=[('ps', bankb)], writes=['mRb'])
            S.op('dve', lambda e: e.tensor_tensor_scan(out=Cf[0:NCOL, :], data0=Rf[0:NCOL, :], data1=Rf[0:NCOL, :], initial=-1e30, op0=ALU.max, op1=ALU.max),
                 reads=['mRf'], writes=['mCf'])
            S.op('dve', lambda e: e.tensor_tensor_scan(out=Cb_[0:NCOL, :], data0=Rb[0:NCOL, :], data1=Rb[0:NCOL, :], initial=-1e30, op0=ALU.max, op1=ALU.max),
                 reads=['mRb'], writes=['mCb'])
            S.op('dve', lambda e: e.tensor_reduce(out=rmax[0:NCOL, :], in_=Rf[0:NCOL, :], axis=AX.X, op=ALU.max), reads=['mRf'], writes=['mrmax'])
            bank1 = self.bank_any.next()
            self.mm(ps[:, bank1, 0:NCOL], [(Cf[0:NCOL, :], id_f[0:NCOL, 0:NCOL])], ['mCf', 'cf'], bank1)
            bank2 = self.bank_any.next()
            self.mm(ps[:, bank2, 0:NCOL], [(Cb_[0:NCOL, :], id_f[0:NCOL, 0:NCOL])], ['mCb', 'cf'], bank2)
            S.op('act', lambda e: e.activation(out=Tb[:, 0:NCOL], in_=ps[:, bank2, 0:NCOL], func=AF.Identity), reads=[('ps', bank2)], writes=['mTb'])
            bank3 = self.bank_any.next()
            self.mm(ps[:, bank3, 0:NCOL], [(Jf, Tb[:, 0:NCOL])], ['mTb', 'mcfm'], bank3)
            p1 = ps[:, bank1, 0:NCOL].rearrange("p (c d h) -> p c d h", d=2, h=4)
            p3 = ps[:, bank3, 0:NCOL].rearrange("p (c d h) -> p c d h", d=2, h=4)
            S.op('dve', lambda e: e.tensor_copy(out=v4(cmx)[:, :, 0, :], in_=p1[:, :, 0, :]), reads=[('ps', bank1)], writes=['mcmx'])
            S.op('dve', lambda e: e.tensor_copy(out=v4(cmx)[:, :, 1, :], in_=p3[:, :, 1, :]), reads=[('ps', bank3)], writes=['mcmx'])
            S.op('dve', lambda e: e.tensor_scalar(out=dg[0:NCOL, 0:NCOL], in0=id_f[0:NCOL, 0:NCOL], scalar1=rmax[0:NCOL, 0:1], scalar2=None, op0=ALU.mult),
                 reads=['mrmax', 'cf'], writes=['mdg'])
            bank4 = self.bank_any.next()
            self.mm(ps[:, bank4, 0:NCOL], [(ones_f[0:NCOL, :], dg[0:NCOL, 0:NCOL])], ['mdg', 'cf'], bank4)
            S.op('dve', lambda e: e.tensor_copy(out=gmx[:].rearrange("p c k -> p (c k)"), in_=ps[:, bank4, 0:NCOL]), reads=[('ps', bank4)], writes=['mgmx'])
            def sview(t, ci, d):
                return t[:].rearrange("p (s c) k -> p s c k", c=cps)[:, :, ci, d * 4:(d + 1) * 4]
            for d in range(2):
                order = list(range(cps)) if d == 0 else list(range(cps - 1, -1, -1))
                for oi, ci in enumerate(order):
                    if oi == 0:
                        if grp == 0:
                            S.op('dve', (lambda e, ci=ci, d=d: e.memset(sview(MP, ci, d), 0.0)), reads=[], writes=['mMP'])
                        else:
                            S.op('dve', (lambda e, ci=ci, d=d: e.tensor_copy(out=sview(MP, ci, d)[:, 0, :], in_=m0b[:, d * 4:(d + 1) * 4])), reads=['mm0'], writes=['mMP'])
                    S.op('dve', (lambda e, ci=ci, d=d: e.tensor_tensor(out=sview(ML, ci, d), in0=sview(gmx, ci, d), in1=sview(MP, ci, d), op=ALU.max)),
                         reads=['mgmx', 'mMP'], writes=['mML'])
                    if oi + 1 < len(order):
                        nx = order[oi + 1]
                        S.op('dve', (lambda e, ci=ci, d=d, nx=nx: e.tensor_tensor(out=sview(MP, nx, d), in0=sview(ML, ci, d), in1=sview(bt, ci, d), op=ALU.add)),
                             reads=['mML', 'mbt'], writes=['mMP'])
                    else:
                        S.op('dve', (lambda e, ci=ci, d=d: e.tensor_tensor(out=mfin[:, :, d * 4:(d + 1) * 4], in0=sview(ML, ci, d), in1=sview(bt, ci, d), op=ALU.add)),
                             reads=['mML', 'mbt'], writes=['mfin'])
            S.op('dve', lambda e: e.tensor_tensor(out=Mt[:], in0=cmx[:], in1=MP[:], op=ALU.max), reads=['mcmx', 'mMP'], writes=['mMt'])
            for dst, a_, b_, kk in ((Uu, ML, Mt, 'mU'), (Wi, MP, Mt, 'mWi'), (Ws, gg, ML, 'mWs'), (Dc, MP, ML, 'mDc')):
                S.op('dve', (lambda e, a_=a_, b_=b_: e.tensor_tensor(out=tq_[:], in0=a_[:], in1=b_[:], op=ALU.subtract)),
                     reads=['mML', 'mMt', 'mMP', 'mg'], writes=['mtq'])
                S.op('act', (lambda e, dst=dst: e.activation(out=dst[:], in_=tq_[:], func=AF.Exp)), reads=['mtq'], writes=[kk])
            S.op('dve', lambda e: e.tensor_tensor(out=tq_[:], in0=bb[:], in1=Mt[:], op=ALU.add), reads=['mb', 'mMt'], writes=['mtq'])
            S.op('act', lambda e: e.activation(out=Fl[:], in_=tq_[:], func=AF.Exp, scale=-1.0), reads=['mtq'], writes=['mFl'])
            if grp == 0:
                for seq in range(NPS):
                    self.dma(self.outs["m_out"][seq:seq + 1, :], mfin[0:1, seq, :], reads=['mfin'])
            S.barrier()
            M.release()
            scal_keys = ['mU', 'mWi', 'mFl', 'mWs', 'mDc']

            def do_head(hd):
                M.mark()
                wrot = [M.alloc([128, KC, 256], BF16, "mwrot")]
                Wo = M.alloc([128, 2, D], BF16, "mWo")
                qT = M.alloc([128, 2, gn], BF16, "mqT")
                kT = M.alloc([128, 2, gn], BF16, "mkT")
                ktok = M.alloc([128, NCH, 256], BF16, "mktok")
                vaug = M.alloc([128, NCH, 257], BF16, "mvaug")
                hacc = M.alloc([128, NCH, 256], BF16, "mhacc")
                chains = [(seq, d) for seq in range(nseq) for d in range(2)]
                nchn = len(chains)
                Cst = [[M.alloc([128, 257], F32, "mCst") for _ in range(2)] for _ in chains]
                Cbf = [[M.alloc([128, 257], BF16, "mCbf") for _ in range(2)] for _ in chains]
                NB = 4
                aT = [M.alloc([128, 128], BF16, "maT") for _ in range(NB)]
                num = [M.alloc([128, 257], F32, "mnum") for _ in range(NB)]
                isb = [M.alloc([128, 257], F32, "misb") for _ in range(NB)]
                kw = [M.alloc([128, 256], BF16, "mkw") for _ in range(NB)]
                sm = [M.alloc([128, 4], F32, "msm") for _ in range(NB)]
                ssq = M.alloc([128, NCH], F32, "mssq")
                junk = M.alloc([128, 256], BF16, "mjunk")
                ogT = qT
                gT = kT
                self.dma(Wo[:], wout_d[:, 2 * hd:2 * hd + 2, :], writes=['mWo'], q='pool')
                S.op('dve', lambda e: e.memset(vaug[:, :, 256:257], 1.0), writes=[K_('mv1')])

                def loadw(src):
                    self.dma(wrot[0][:], src, writes=[('mwrot', 0)], q='pool')
                    return 0
                for q_, dst, sc_ in ((0, qT, 1.0), (1, kT, 0.0625)):
                    b = loadw(wqkv_d[:, :, q_ * D + hd * 256:q_ * D + (hd + 1) * 256])
                    for kc2 in range(2):
                        for t0 in range(0, gn, 512):
                            bank = self.bank_any.next()
                            self.mm(ps[:, bank, :], [(wrot[b][:, k, kc2 * 128:(kc2 + 1) * 128], hT[:, k, t0:t0 + 512]) for k in range(KC)],
                                    [('mwrot', b)] + hk, bank)
                            S.op('act', (lambda e, dst=dst, bank=bank, kc2=kc2, t0=t0, sc_=sc_: e.activation(
                                out=dst[:, kc2, t0:t0 + 512], in_=ps[:, bank, :], func=AF.Identity, scale=sc_)),
                                reads=[('ps', bank)], writes=[('mqk', q_)])
                for q_, dst, sc_ in ((1, ktok, 0.0625), (2, vaug, 1.0)):
                    b = loadw(wqkv_d[:, :, q_ * D + hd * 256:q_ * D + (hd + 1) * 256])
                    for ch in range(0, NCH, 2):
                        bank = self.bank_any.next()

                        def ft(e, bank=bank, ch=ch, b=b):
                            for ci in range(2):
                                for k in range(KC):
                                    r = e.matmul(ps[:, bank, ci * 256:(ci + 1) * 256], lhsT=hT[:, k, (ch + ci) * 128:(ch + ci + 1) * 128],
                                                 rhs=wrot[b][:, k, :], start=(k == 0 and ci == 0), stop=(k == KC - 1), skip_group_check=True)
                            return r
                        S.op('pe', ft, reads=[('mwrot', b)] + hk, writes=[('ps', bank)])
                        S.op('act', (lambda e, dst=dst, bank=bank, ch=ch, sc_=sc_: e.activation(
                            out=dst[:, ch:ch + 2, 0:256], in_=ps[:, bank, :].rearrange("p (a b) -> p a b", b=256), func=AF.Identity, scale=sc_)),
                            reads=[('ps', bank)], writes=[('mtok', q_)])
                qkk = [('mqk', 0), ('mqk', 1)]
                tokk = [('mtok', 1), ('mtok', 2)]
                for ci_, (seq, d) in enumerate(chains):
                    for kc2 in range(2):
                        if grp == 0:
                            S.op('dve', (lambda e, t=Cst[ci_][kc2]: e.memset(t[:], 0.0)), writes=[('mCst', ci_, kc2)])
                            S.op('dve', (lambda e, t=Cbf[ci_][kc2]: e.memset(t[:], 0.0)), writes=[('mCbf', ci_)])
                        else:
                            self.dma(Cst[ci_][kc2][:], ins["ml_caug0"][d, hd, kc2 * 128:(kc2 + 1) * 128, :], writes=[('mCst', ci_, kc2)])
                            S.op('act', (lambda e, t=Cbf[ci_][kc2], s_=Cst[ci_][kc2]: e.activation(out=t[:], in_=s_[:], func=AF.Identity)),
                                 reads=[('mCst', ci_, kc2)], writes=[('mCbf', ci_)])
                hwritten = set()
                slot = [0]

                def stage1(ci_, step, lane):
                    seq, d = chains[ci_]
                    ci = step if d == 0 else cps - 1 - step
                    ch = seq * cps + ci
                    col = d * 4 + hd
                    sb = slot[0] % NB
                    slot[0] += 1
                    tsl = slice(ch * 128, (ch + 1) * 128)
                    bankA = 6 + lane
                    self.mm(ps[:, bankA, 0:128], [(kT[:, kc2, tsl], qT[:, kc2, tsl]) for kc2 in range(2)], qkk, bankA)
                    S.op('dve', lambda e: e.scalar_tensor_tensor(out=aT[sb][:], in0=ps[:, bankA, 0:128], scalar=Ws[:, ch, col:col + 1], in1=cfm[:, d, :],
                                                                 op0=ALU.mult, op1=ALU.mult),
                         reads=[('ps', bankA), 'mWs', 'mcfm'], writes=[('maT', sb)])
                    S.op('dve', lambda e: e.tensor_scalar(out=kw[sb][:], in0=ktok[:, ch, :], scalar1=Ws[:, ch, col:col + 1], scalar2=None, op0=ALU.mult),
                         reads=tokk + ['mWs'], writes=[('mkw', sb)])
                    return (ci_, ch, col, d, sb, tsl, lane)

                def stage2(st):
                    ci_, ch, col, d, sb, tsl, lane = st
                    b1, b2, b3 = 3 * lane, 3 * lane + 1, 3 * lane + 2

                    def f1(e):
                        e.matmul(ps[:, b1, 0:257], lhsT=aT[sb][:], rhs=vaug[:, ch, :], start=True, stop=True)
                        for kc2 in range(2):
                            r = e.matmul(ps[:, b1, 257 + kc2:258 + kc2], lhsT=kw[sb][:, kc2 * 128:(kc2 + 1) * 128], rhs=vaug[:, ch, 256:257], start=True, stop=True)
                        return r
                    S.op('pe', f1, reads=[('maT', sb), ('mkw', sb)] + tokk + [K_('x')], writes=[('ps', b1)])
                    self.mm(ps[:, b2, 0:257], [(qT[:, kc2, tsl], Cbf[ci_][kc2][:]) for kc2 in range(2)], qkk + [('mCbf', ci_)], b2)

                    def f3(e):
                        for kc2 in range(2):
                            r = e.matmul(ps[:, b3, kc2 * 256:(kc2 + 1) * 256], lhsT=kw[sb][:, kc2 * 128:(kc2 + 1) * 128], rhs=vaug[:, ch, 0:256], start=True, stop=True)
                        return r
                    S.op('pe', f3, reads=[('mkw', sb)] + tokk, writes=[('ps', b3)])
                    return st

                def stage3(st):
                    ci_, ch, col, d, sb, tsl, lane = st
                    b1, b2, b3 = 3 * lane, 3 * lane + 1, 3 * lane + 2
                    for kc2 in range(2):
                        S.op('dve', (lambda e, kc2=kc2: e.scalar_tensor_tensor(out=Cst[ci_][kc2][:, 0:256], in0=Cst[ci_][kc2][:, 0:256], scalar=Dc[:, ch, col:col + 1],
                                                                               in1=ps[:, b3, kc2 * 256:(kc2 + 1) * 256], op0=ALU.mult, op1=ALU.add)),
                             reads=[('ps', b3), 'mDc', ('mCst', ci_, kc2), ('mCbf', ci_)], writes=[('mCst', ci_, kc2)])
                        S.op('dve', (lambda e, kc2=kc2: e.scalar_tensor_tensor(out=Cst[ci_][kc2][:, 256:257], in0=Cst[ci_][kc2][:, 256:257], scalar=Dc[:, ch, col:col + 1],
                                                                               in1=ps[:, b1, 257 + kc2:258 + kc2], op0=ALU.mult, op1=ALU.add)),
                             reads=[('ps', b1), 'mDc', ('mCst', ci_, kc2)], writes=[('mCst', ci_, kc2)])
                    S.op('act', lambda e: e.activation(out=isb[sb][:], in_=ps[:, b2, 0:257], func=AF.Identity, scale=Wi[:, ch, col:col + 1]),
                         reads=[('ps', b2), 'mWi'], writes=[('misb', sb)])
                    for kc2 in range(2):
                        S.op('act', (lambda e, kc2=kc2: e.activation(out=Cbf[ci_][kc2][:], in_=Cst[ci_][kc2][:], func=AF.Identity)),
                             reads=[('mCst', ci_, kc2)], writes=[('mCbf', ci_)])
                    S.op('dve', lambda e: e.scalar_tensor_tensor(out=num[sb][:], in0=ps[:, b1, 0:257], scalar=Uu[:, ch, col:col + 1], in1=isb[sb][:],
                                                                 op0=ALU.mult, op1=ALU.add),
                         reads=[('ps', b1), 'mU', ('misb', sb)], writes=[('mnum', sb)])
                    S.op('dve', lambda e: e.scalar_tensor_tensor(out=sm[sb][:, 0:1], in0=num[sb][:, 256:257], scalar=-1.0, in1=num[sb][:, 256:257], op0=ALU.mult, op1=ALU.max),
                         reads=[('mnum', sb)], writes=[('msm', sb)])
                    S.op('dve', lambda e: e.tensor_tensor(out=sm[sb][:, 0:1], in0=sm[sb][:, 0:1], in1=Fl[:, ch, col:col + 1], op=ALU.max),
                         reads=[('msm', sb), 'mFl'], writes=[('msm', sb)])
                    S.op('dve', lambda e: e.reciprocal(out=sm[sb][:, 0:1], in_=sm[sb][:, 0:1]), reads=[('msm', sb)], writes=[('msm', sb)])
                    if ch not in hwritten:
                        hwritten.add(ch)
                        S.op('dve', lambda e: e.tensor_scalar(out=hacc[:, ch, :], in0=num[sb][:, 0:256], scalar1=sm[sb][:, 0:1], scalar2=None, op0=ALU.mult),
                             reads=[('mnum', sb), ('msm', sb)], writes=[('mhacc', ch)])
                    else:
                        S.op('dve', lambda e: e.scalar_tensor_tensor(out=hacc[:, ch, :], in0=num[sb][:, 0:256], scalar=sm[sb][:, 0:1], in1=hacc[:, ch, :],
                                                                     op0=ALU.mult, op1=ALU.add),
                             reads=[('mnum', sb), ('msm', sb), ('mhacc', ch)], writes=[('mhacc', ch)])

                items = [(ci_, step) for step in range(cps) for ci_ in range(nchn)]
                st1, st2 = {}, {}
                for idx in range(len(items) + 2):
                    if idx - 2 >= 0:
                        stage3(st2[idx - 2])
                    if 0 <= idx - 1 < len(items):
                        st2[idx - 1] = stage2(st1[idx - 1])
                    if idx < len(items):
                        st1[idx] = stage1(items[idx][0], items[idx][1], idx % 2)
                if grp == 0:
                    for ci_, (seq, d) in enumerate(chains):
                        for kc2 in range(2):
                            self.dma(self.outs["caug_out"][seq, d, hd, kc2 * 128:(kc2 + 1) * 128, :], Cst[ci_][kc2][:], reads=[('mCst', ci_, kc2)])
                S.barrier()
                for ch in range(NCH):
                    S.op('dve', (lambda e, ch=ch: e.scalar_tensor_tensor(out=junk[:], in0=hacc[:, ch, :], scalar=1.0, in1=hacc[:, ch, :], op0=ALU.mult, op1=ALU.mult,
                                                                         accum_out=ssq[:, ch:ch + 1])),
                         reads=[('mhacc', ch)], writes=['mjunk', ('mssq', ch)])
                S.op('act', lambda e: e.activation(out=ssq[:], in_=ssq[:], func=AF.Sqrt, scale=1.0 / 256.0, bias=epsT[:, 0:1]),
                     reads=[('mssq', ch) for ch in range(NCH)] + ['eps'], writes=['mssqr'])
                S.op('dve', lambda e: e.reciprocal(out=ssq[:], in_=ssq[:]), reads=['mssqr'], writes=['mssqr'])
                for ch in range(NCH):
                    S.op('dve', (lambda e, ch=ch: e.tensor_scalar(out=hacc[:, ch, :], in0=hacc[:, ch, :], scalar1=ssq[:, ch:ch + 1], scalar2=None, op0=ALU.mult)),
                         reads=[('mhacc', ch), 'mssqr'], writes=[('mhacc', ch)])
                b = loadw(wog_d[:, :, hd * 256:(hd + 1) * 256])
                for kc2 in range(2):
                    for t0 in range(0, gn, 512):
                        bank = self.bank_any.next()
                        self.mm(ps[:, bank, :], [(wrot[b][:, k, kc2 * 128:(kc2 + 1) * 128], hT[:, k, t0:t0 + 512]) for k in range(KC)],
                                [('mwrot', b)] + hk, bank)
                        S.op('act', (lambda e, bank=bank, kc2=kc2, t0=t0: e.activation(out=ogT[:, kc2, t0:t0 + 512], in_=ps[:, bank, :], func=AF.Sigmoid)),
                             reads=[('ps', bank)], writes=[('mog', kc2, t0)])
                for ch in range(NCH):
                    bank = self.bank_any.next()

                    def ftr(e, bank=bank, ch=ch):
                        for kc2 in range(2):
                            r = e.transpose(psb[:, bank, kc2 * 128:(kc2 + 1) * 128], hacc[:, ch, kc2 * 128:(kc2 + 1) * 128], id_b)
                        return r
                    S.op('pe', ftr, reads=[('mhacc', ch), 'cb'], writes=[('ps', bank)])
                    for kc2 in range(2):
                        S.op('dve', (lambda e, bank=bank, ch=ch, kc2=kc2: e.scalar_tensor_tensor(
                            out=gT[:, kc2, ch * 128:(ch + 1) * 128], in0=psb[:, bank, kc2 * 128:(kc2 + 1) * 128], scalar=hgT[:, 2 * hd + kc2:2 * hd + kc2 + 1],
                            in1=ogT[:, kc2, ch * 128:(ch + 1) * 128], op0=ALU.mult, op1=ALU.mult)),
                            reads=[('ps', bank), 'mhg', ('mog', kc2, (ch * 128) // 512 * 512)], writes=[('mgT', ch // 4)])
                for t0 in range(0, gn, 512):
                    for oc in range(KC):
                        bank = self.bank_any.next()
                        self.mm(ps[:, bank, :], [(Wo[:, kc2, oc * 128:(oc + 1) * 128], gT[:, kc2, t0:t0 + 512]) for kc2 in range(2)],
                                ['mWo', ('mgT', t0 // 512)], bank)
                        self.xupdate(l, 1, g0 + t0, 512, oc, ps[:, bank, :], [('ps', bank)])
                S.barrier()
                M.release()

            for hd in range(4):
                do_head(hd)
            S.barrier()
            M.release()

        do_group(0)
        do_group(1)
        M.release()

    def mixer_mlstm(self, l):
        S, M, ins, ps, psb = self.S, self.M, self.ins, self.ps, self.psb
        ones_f, id_f, id_b, epsT = self.ones_f, self.id_f, self.id_b, self.epsT
        wqkv_d = ins["ml_w_qkv"].rearrange("(k p) n -> p k n", p=128)
        wog_d = ins["ml_w_og"].rearrange("(k p) n -> p k n", p=128)
        wout_d = ins["ml_w_out"].rearrange("(k p) n -> p k n", p=128)
        M.mark()
        cfm = M.alloc([128, 6, 128], F32, "mcfm")
        self.dma(cfm[:], ins["cf32"][:, 2:8, :], writes=['mcfm'])
        Wif = M.alloc([128, KC, 16], BF16, "mWif")
        self.dma(Wif[:], ins["ml_w_if"].rearrange("(k p) n -> p k n", p=128), writes=['mWif'], q='pool')
        bif = M.alloc([128, 16], F32, "mbif")
        self.dma(bif[:], ins["ml_b_if"].partition_broadcast(128), writes=['mbif'])
        hgT = M.alloc([128, KC], F32, "mhg")
        self.dma(hgT[:], ins["ml_head_gT"], writes=['mhg'])
        m0b = M.alloc([128, 8], F32, "mm0")
        self.dma(m0b[:], ins["ml_m0"].partition_broadcast(128), writes=['mm0'])
        zero1 = M.alloc([128, 1], F32, "mzero")
        S.op('dve', lambda e: e.memset(zero1[:], 0.0), writes=['mzero'])
        uid = [0]

        def K_(name):
            uid[0] += 1
            return (name, uid[0])

        def do_group(grp):
            M.mark()
            g0, gn = (0, TP) if grp == 0 else (TP, TS)
            NCH = gn // 128
            nseq, cps = (NPS, 2) if grp == 0 else (1, 16)
            hT = M.alloc([128, KC, gn], BF16, "mhT")
            hk = [('mh', k, tq) for k in range(KC) for tq in range(gn // 256)]
            gi = M.alloc([128, NCH, 8], F32, "mgi")
            gl = M.alloc([128, NCH, 8], F32, "mgl")
            bb = M.alloc([128, NCH, 8], F32, "mb")
            bt = M.alloc([128, NCH, 8], F32, "mbt")
            gg = M.alloc([128, NCH, 8], F32, "mg")
            M.mark()
            self.alloc_norm_scratch(256)
            for tq in range(gn // 256):
                self.norm_mod_range(l, 1, g0 + tq * 256, 256, hT, tq * 256, [('mh', k, tq) for k in range(KC)])
            S.barrier()
            M.release()
            gpre = M.alloc([128, NCH, 16], F32, "mgpre")
            for c8 in range(0, NCH, 8):
                bank = self.bank_any.next()

                def fg(e, bank=bank, c8=c8):
                    for ci in range(8):
                        ch = c8 + ci
                        for k in range(KC):
                            r = e.matmul(ps[:, bank, ci * 16:(ci + 1) * 16], lhsT=hT[:, k, ch * 128:(ch + 1) * 128], rhs=Wif[:, k, :],
                                         start=(k == 0 and ci == 0), stop=(k == KC - 1), skip_group_check=True)
                    return r
                S.op('pe', fg, reads=hk + ['mWif'], writes=[('ps', bank)])
                for ci in range(8):
                    S.op('dve', (lambda e, bank=bank, ci=ci, c8=c8: e.tensor_tensor(out=gpre[:, c8 + ci, :], in0=ps[:, bank, ci * 16:(ci + 1) * 16], in1=bif[:], op=ALU.add)),
                         reads=[('ps', bank), 'mbif'], writes=['mgpre'])
            gpv = gpre[:].rearrange("p c (d x) -> p c d x", d=2)
            S.op('dve', lambda e: e.tensor_copy(out=gi[:].rearrange("p c (d h) -> p c d h", d=2), in_=gpv[:, :, :, 0:4]), reads=['mgpre'], writes=['mgi'])
            glv = gl[:].rearrange("p c (d h) -> p c d h", d=2)
            S.op('act', lambda e: e.activation(out=glv, in_=gpv[:, :, :, 4:8], func=AF.Exp, scale=-1.0), reads=['mgpre'], writes=['mgl'])
            S.op('act', lambda e: e.activation(out=gl[:], in_=gl[:], func=AF.Ln, bias=1.0), reads=['mgl'], writes=['mgl'])
            S.op('dve', lambda e: e.tensor_scalar(out=gl[:], in0=gl[:], scalar1=-1.0, scalar2=None, op0=ALU.mult), reads=['mgl'], writes=['mgl'])
            for c8 in range(0, NCH, 8):
                bank = self.bank_any.next()

                def fc(e, bank=bank, c8=c8):
                    for ci in range(8):
                        ch = c8 + ci
                        e.matmul(ps[:, bank, ci * 16:ci * 16 + 4], lhsT=cfm[:, 0, :], rhs=gl[:, ch, 0:4], start=True, stop=True)
                        e.matmul(ps[:, bank, ci * 16 + 4:ci * 16 + 8], lhsT=cfm[:, 1, :], rhs=gl[:, ch, 4:8], start=True, stop=True)
                        r = e.matmul(ps[:, bank, ci * 16 + 8:ci * 16 + 16], lhsT=ones_f, rhs=gl[:, ch, :], start=True, stop=True)
                    return r
                S.op('pe', fc, reads=['mgl', 'mcfm', 'cf'], writes=[('ps', bank)])
                pv = ps[:, bank, 0:128].rearrange("p (c x) -> p c x", x=16)
                S.op('dve', (lambda e, pv=pv, c8=c8: e.tensor_copy(out=bb[:, c8:c8 + 8, :], in_=pv[:, :, 0:8])), reads=[('ps', bank)], writes=['mb'])
                S.op('dve', (lambda e, pv=pv, c8=c8: e.tensor_copy(out=bt[:, c8:c8 + 8, :], in_=pv[:, :, 8:16])), reads=[('ps', bank)], writes=['mbt'])
            S.op('dve', lambda e: e.tensor_tensor(out=gg[:], in0=gi[:], in1=bb[:], op=ALU.subtract), reads=['mgi', 'mb'], writes=['mg'])
            stg = getattr(self, "dbg_mls", 99)
            if grp == 0:
                self.dump_tile("mWif", Wif[:], [128, KC, 16], ['mWif'], BF16)
                self.dump_tile("mhT", hT[:], [128, KC, gn], hk, BF16)
                self.dump_tile("mgpre", gpre[:], [128, NCH, 16], ['mgpre'])
                for nm_, tl_, ky_ in (("mgi", gi, 'mgi'), ("mgl", gl, 'mgl'), ("mbb", bb, 'mb'), ("mbt", bt, 'mbt'), ("mgg", gg, 'mg')):
                    self.dump_tile(nm_, tl_[:], [128, NCH, 8], [ky_])
            if stg <= 1:
                raise _Stop()

            def do_head(hd):
                M.mark()
                wrot = [M.alloc([128, KC, 256], BF16, "mwrot") for _ in range(2)]
                Wo = M.alloc([128, 2, D], BF16, "mWo")
                qT = M.alloc([128, 2, gn], BF16, "mqT")
                kT = M.alloc([128, 2, gn], BF16, "mkT")
                ktok = M.alloc([128, NCH, 256], BF16, "mktok")
                vaug = M.alloc([128, NCH, 257], BF16, "mvaug")
                hacc = M.alloc([128, NCH, 256], BF16, "mhacc")
                Cst = [M.alloc([128, 257], F32, "mCst") for _ in range(2)]
                Cbf = [M.alloc([128, 257], BF16, "mCbf") for _ in range(2)]
                mpv = [M.alloc([128, 1], F32, "mmp") for _ in range(2)]
                sm = M.alloc([128, 16], F32, "msm")
                t128 = [M.alloc([128, 128], F32, "mt128") for _ in range(4)]
                aT = M.alloc([128, 128], BF16, "maT")
                num = M.alloc([128, 257], F32, "mnum")
                isb = M.alloc([128, 257], F32, "misb")
                kw = M.alloc([128, 256], BF16, "mkw")
                hs = M.alloc([128, 256], F32, "mhs")
                hn = M.alloc([128, 256], BF16, "mhn")
                junk = M.alloc([128, 256], BF16, "mjunk")
                ogT = qT
                gT = kT
                self.dma(Wo[:], wout_d[:, 2 * hd:2 * hd + 2, :], writes=['mWo'], q='pool')
                S.op('dve', lambda e: e.memset(vaug[:, :, 256:257], 1.0), writes=[K_('mv1')])
                wi = [0]

                def loadw(src):
                    b = wi[0] % 2
                    wi[0] += 1
                    self.dma(wrot[b][:], src, writes=[('mwrot', b)], q='pool')
                    return b
                for q_, dst, sc_ in ((0, qT, 1.0), (1, kT, 0.0625)):
                    b = loadw(wqkv_d[:, :, q_ * D + hd * 256:q_ * D + (hd + 1) * 256])
                    for kc2 in range(2):
                        for t0 in range(0, gn, 512):
                            bank = self.bank_any.next()
                            self.mm(ps[:, bank, :], [(wrot[b][:, k, kc2 * 128:(kc2 + 1) * 128], hT[:, k, t0:t0 + 512]) for k in range(KC)],
                                    [('mwrot', b)] + hk, bank)
                            S.op('act', (lambda e, dst=dst, bank=bank, kc2=kc2, t0=t0, sc_=sc_: e.activation(
                                out=dst[:, kc2, t0:t0 + 512], in_=ps[:, bank, :], func=AF.Identity, scale=sc_)),
                                reads=[('ps', bank)], writes=[('mqk', q_)])
                for q_, dst, sc_ in ((1, ktok, 0.0625), (2, vaug, 1.0)):
                    b = loadw(wqkv_d[:, :, q_ * D + hd * 256:q_ * D + (hd + 1) * 256])
                    for ch in range(0, NCH, 2):
                        bank = self.bank_any.next()

                        def ft(e, bank=bank, ch=ch, b=b):
                            for ci in range(2):
                                for k in range(KC):
                                    r = e.matmul(ps[:, bank, ci * 256:(ci + 1) * 256], lhsT=hT[:, k, (ch + ci) * 128:(ch + ci + 1) * 128],
                                                 rhs=wrot[b][:, k, :], start=(k == 0 and ci == 0), stop=(k == KC - 1), skip_group_check=True)
                            return r
                        S.op('pe', ft, reads=[('mwrot', b)] + hk, writes=[('ps', bank)])
                        S.op('act', (lambda e, dst=dst, bank=bank, ch=ch, sc_=sc_: e.activation(
                            out=dst[:, ch:ch + 2, 0:256], in_=ps[:, bank, :].rearrange("p (a b) -> p a b", b=256), func=AF.Identity, scale=sc_)),
                            reads=[('ps', bank)], writes=[('mtok', q_)])
                qkk = [('mqk', 0), ('mqk', 1)]
                tokk = [('mtok', 1), ('mtok', 2)]
                if stg <= 2:
                    raise _Stop()
                icnt = [0]

                def inst(seq, d, ch, first, mprev, mnew):
                    col = d * 4 + hd
                    cg = gg[:, ch, col:col + 1]
                    cb_ = bb[:, ch, col:col + 1]
                    cbt = bt[:, ch, col:col + 1]
                    tsl = slice(ch * 128, (ch + 1) * 128)
                    kA = K_('A')
                    bankA = self.bank_any.next()
                    self.mm(ps[:, bankA, 0:128], [(kT[:, kc2, tsl], qT[:, kc2, tsl]) for kc2 in range(2)], qkk, bankA)
                    S.op('dve', lambda e: e.tensor_scalar(out=t128[0][:], in0=id_f, scalar1=cg, scalar2=None, op0=ALU.mult),
                         reads=['mg', 'cf'], writes=[('mt128', 0)])
                    bankB = self.bank_any.next()
                    self.mm(ps[:, bankB, 0:128], [(ones_f, t128[0][:])], [('mt128', 0), 'cf'], bankB)
                    S.op('dve', lambda e: e.tensor_tensor(out=t128[1][:], in0=ps[:, bankB, 0:128], in1=cfm[:, 2 + d, :], op=ALU.add),
                         reads=[('ps', bankB), 'mcfm'], writes=[('mt128', 1)])
                    S.op('dve', lambda e: e.tensor_reduce(out=sm[:, 0:1], in_=t128[1][:], axis=AX.X, op=ALU.max), reads=[('mt128', 1)], writes=[('msm', 0)])
                    S.op('dve', lambda e: e.tensor_reduce(out=sm[:, 1:2], in_=ps[:, bankB, 0:128], axis=AX.X, op=ALU.max), reads=[('ps', bankB)], writes=[('msm', 1)])
                    S.op('dve', lambda e: e.tensor_tensor(out=sm[:, 2:3], in0=sm[:, 0:1], in1=mprev, op=ALU.max), reads=[('msm', 0), 'mmp'], writes=[('msm', 2)])
                    S.op('dve', lambda e: e.tensor_tensor(out=sm[:, 3:4], in0=sm[:, 1:2], in1=mprev, op=ALU.max), reads=[('msm', 1), 'mmp'], writes=[('msm', 3)])
                    S.op('dve', lambda e: e.tensor_scalar(out=t128[2][:], in0=id_f, scalar1=sm[:, 2:3], scalar2=None, op0=ALU.mult),
                         reads=[('msm', 2), 'cf'], writes=[('mt128', 2)])
                    bankC = self.bank_any.next()
                    self.mm(ps[:, bankC, 0:128], [(ones_f, t128[2][:]), (id_f, cfm[:, 4 + d, :])], [('mt128', 2), 'cf', 'mcfm'], bankC)
                    S.op('act', lambda e: e.activation(out=t128[3][:], in_=ps[:, bankC, 0:128], func=AF.Exp, scale=-1.0, bias=cg),
                         reads=[('ps', bankC), 'mg'], writes=[('mt128', 3)])
                    S.op('dve', lambda e: e.tensor_tensor(out=aT[:], in0=t128[3][:], in1=ps[:, bankA, 0:128], op=ALU.mult),
                         reads=[('mt128', 3), ('ps', bankA)], writes=['maT'])
                    bankD = self.bank_any.next()
                    self.mm(ps[:, bankD, 0:257], [(aT[:], vaug[:, ch, :])], ['maT'] + tokk + [K_('x')], bankD)
                    bankE = self.bank_any.next()
                    self.mm(ps[:, bankE, 0:257], [(qT[:, kc2, tsl], Cbf[kc2][:]) for kc2 in range(2)], qkk + ['mCbf'], bankE)
                    S.op('act', lambda e: e.activation(out=sm[:, 4:5], in_=sm[:, 2:3], func=AF.Exp, scale=-1.0, bias=mprev),
                         reads=[('msm', 2), 'mmp'], writes=[('msm', 4)])
                    S.op('act', lambda e: e.activation(out=isb[:], in_=ps[:, bankE, 0:257], func=AF.Identity, scale=sm[:, 4:5]),
                         reads=[('ps', bankE), ('msm', 4)], writes=['misb'])
                    S.op('dve', lambda e: e.tensor_tensor(out=num[:], in0=isb[:], in1=ps[:, bankD, 0:257], op=ALU.add),
                         reads=['misb', ('ps', bankD)], writes=['mnum'])
                    S.op('dve', lambda e: e.tensor_tensor(out=sm[:, 5:6], in0=sm[:, 2:3], in1=cb_, op=ALU.add), reads=[('msm', 2), 'mb'], writes=[('msm', 5)])
                    S.op('act', lambda e: e.activation(out=sm[:, 5:6], in_=sm[:, 5:6], func=AF.Exp, scale=-1.0), reads=[('msm', 5)], writes=[('msm', 5)])
                    S.op('dve', lambda e: e.scalar_tensor_tensor(out=sm[:, 6:7], in0=num[:, 256:257], scalar=-1.0, in1=num[:, 256:257], op0=ALU.mult, op1=ALU.max),
                         reads=['mnum'], writes=[('msm', 6)])
                    S.op('dve', lambda e: e.tensor_tensor(out=sm[:, 6:7], in0=sm[:, 6:7], in1=sm[:, 5:6], op=ALU.max),
                         reads=[('msm', 6), ('msm', 5)], writes=[('msm', 6)])
                    S.op('dve', lambda e: e.reciprocal(out=sm[:, 6:7], in_=sm[:, 6:7]), reads=[('msm', 6)], writes=[('msm', 6)])
                    if d == 0:
                        S.op('dve', lambda e: e.tensor_scalar(out=hacc[:, ch, :], in0=num[:, 0:256], scalar1=sm[:, 6:7], scalar2=None, op0=ALU.mult),
                             reads=['mnum', ('msm', 6)], writes=[('mhacc', ch)])
                    else:
                        S.op('dve', lambda e: e.scalar_tensor_tensor(out=hs[:], in0=num[:, 0:256], scalar=sm[:, 6:7], in1=hacc[:, ch, :], op0=ALU.mult, op1=ALU.add),
                             reads=['mnum', ('msm', 6), ('mhacc', ch)], writes=['mhs'])
                        S.op('dve', lambda e: e.scalar_tensor_tensor(out=junk[:], in0=hs[:], scalar=1.0, in1=hs[:], op0=ALU.mult, op1=ALU.mult, accum_out=sm[:, 7:8]),
                             reads=['mhs'], writes=['mjunk', ('msm', 7)])
                        S.op('act', lambda e: e.activation(out=sm[:, 7:8], in_=sm[:, 7:8], func=AF.Sqrt, scale=1.0 / 256.0, bias=epsT[:, 0:1]),
                             reads=[('msm', 7), 'eps'], writes=[('msm', 7)])
                        S.op('dve', lambda e: e.reciprocal(out=sm[:, 7:8], in_=sm[:, 7:8]), reads=[('msm', 7)], writes=[('msm', 7)])
                        S.op('dve', lambda e: e.tensor_scalar(out=hacc[:, ch, :], in0=hs[:], scalar1=sm[:, 7:8], scalar2=None, op0=ALU.mult),
                             reads=['mhs', ('msm', 7)], writes=[('mhacc', ch)])
                    S.op('dve', lambda e: e.tensor_scalar(out=sm[:, 8:9], in0=sm[:, 3:4], scalar1=-1.0, scalar2=None, op0=ALU.mult), reads=[('msm', 3)], writes=[('msm', 8)])
                    S.op('act', lambda e: e.activation(out=sm[:, 9:10], in_=cg, func=AF.Exp, bias=sm[:, 8:9]), reads=['mg', ('msm', 8)], writes=[('msm', 9)])
                    S.op('act', lambda e: e.activation(out=sm[:, 10:11], in_=mprev, func=AF.Exp, bias=sm[:, 8:9]), reads=['mmp', ('msm', 8)], writes=[('msm', 10)])
                    S.op('dve', lambda e: e.tensor_scalar(out=kw[:], in0=ktok[:, ch, :], scalar1=sm[:, 9:10], scalar2=None, op0=ALU.mult),
                         reads=tokk + [('msm', 9)], writes=['mkw'])
                    for kc2 in range(2):
                        bankF = self.bank_any.next()
                        self.mm(ps[:, bankF, 0:257], [(kw[:, kc2 * 128:(kc2 + 1) * 128], vaug[:, ch, :])], ['mkw'] + tokk, bankF)
                        S.op('dve', (lambda e, kc2=kc2, bankF=bankF: e.scalar_tensor_tensor(out=Cst[kc2][:], in0=Cst[kc2][:], scalar=sm[:, 10:11], in1=ps[:, bankF, 0:257],
                                                                                           op0=ALU.mult, op1=ALU.add)),
                             reads=[('ps', bankF), ('msm', 10), ('mCst', kc2)], writes=[('mCst', kc2)])
                        S.op('act', (lambda e, kc2=kc2: e.activation(out=Cbf[kc2][:], in_=Cst[kc2][:], func=AF.Identity)),
                             reads=[('mCst', kc2)], writes=['mCbf'])
                    S.op('dve', lambda e: e.tensor_tensor(out=mnew, in0=sm[:, 3:4], in1=cbt, op=ALU.add), reads=[('msm', 3), 'mbt'], writes=['mmp'])
                    icnt[0] += 1
                    if stg == 3 and icnt[0] >= 1:
                        raise _Stop()

                for seq in range(nseq):
                    for d in range(2):
                        if grp == 0:
                            for kc2 in range(2):
                                S.op('dve', (lambda e, kc2=kc2: e.memset(Cst[kc2][:], 0.0)), writes=[('mCst', kc2)])
                                S.op('dve', (lambda e, kc2=kc2: e.memset(Cbf[kc2][:], 0.0)), writes=['mCbf'])
                            S.op('dve', lambda e: e.memset(mpv[0][:], 0.0), writes=['mmp'])
                        else:
                            for kc2 in range(2):
                                self.dma(Cst[kc2][:], ins["ml_caug0"][d, hd, kc2 * 128:(kc2 + 1) * 128, :], writes=[('mCst', kc2)])
                                S.op('act', (lambda e, kc2=kc2: e.activation(out=Cbf[kc2][:], in_=Cst[kc2][:], func=AF.Identity)),
                                     reads=[('mCst', kc2)], writes=['mCbf'])
                            S.op('dve', (lambda e, d=d: e.tensor_copy(out=mpv[0][:], in_=m0b[:, d * 4 + hd:d * 4 + hd + 1])), reads=['mm0'], writes=['mmp'])
                        order = list(range(cps)) if d == 0 else list(range(cps - 1, -1, -1))
                        cur = 0
                        for ci in order:
                            inst(seq, d, seq * cps + ci, ci == order[0], mpv[cur][:], mpv[1 - cur][:])
                            cur = 1 - cur
                        if grp == 0:
                            for kc2 in range(2):
                                self.dma(self.outs["caug_out"][seq, d, hd, kc2 * 128:(kc2 + 1) * 128, :], Cst[kc2][:], reads=[('mCst', kc2)])
                            self.dma(self.outs["m_out"][seq:seq + 1, d * 4 + hd:d * 4 + hd + 1], mpv[cur][0:1, 0:1], reads=['mmp'])
                if stg <= 4:
                    raise _Stop()
                S.barrier()
                b = loadw(wog_d[:, :, hd * 256:(hd + 1) * 256])
                for kc2 in range(2):
                    for t0 in range(0, gn, 512):
                        bank = self.bank_any.next()
                        self.mm(ps[:, bank, :], [(wrot[b][:, k, kc2 * 128:(kc2 + 1) * 128], hT[:, k, t0:t0 + 512]) for k in range(KC)],
                                [('mwrot', b)] + hk + qkk, bank)
                        S.op('act', (lambda e, bank=bank, kc2=kc2, t0=t0: e.activation(out=ogT[:, kc2, t0:t0 + 512], in_=ps[:, bank, :], func=AF.Sigmoid)),
                             reads=[('ps', bank)], writes=[('mog', kc2, t0)])
                if stg <= 5:
                    raise _Stop()
                for ch in range(NCH):
                    bank = self.bank_any.next()

                    def ftr(e, bank=bank, ch=ch):
                        for kc2 in range(2):
                            r = e.transpose(psb[:, bank, kc2 * 128:(kc2 + 1) * 128], hacc[:, ch, kc2 * 128:(kc2 + 1) * 128], id_b)
                        return r
                    S.op('pe', ftr, reads=[('mhacc', ch), 'cb'], writes=[('ps', bank)])
                    for kc2 in range(2):
                        S.op('dve', (lambda e, bank=bank, ch=ch, kc2=kc2: e.scalar_tensor_tensor(
                            out=gT[:, kc2, ch * 128:(ch + 1) * 128], in0=psb[:, bank, kc2 * 128:(kc2 + 1) * 128], scalar=hgT[:, 2 * hd + kc2:2 * hd + kc2 + 1],
                            in1=ogT[:, kc2, ch * 128:(ch + 1) * 128], op0=ALU.mult, op1=ALU.mult)),
                            reads=[('ps', bank), 'mhg', ('mog', kc2, (ch * 128) // 512 * 512)] + qkk, writes=[('mgT', ch // 4)])
                if stg <= 6:
                    raise _Stop()
                for t0 in range(0, gn, 512):
                    for oc in range(KC):
                        bank = self.bank_any.next()
                        self.mm(ps[:, bank, :], [(Wo[:, kc2, oc * 128:(oc + 1) * 128], gT[:, kc2, t0:t0 + 512]) for kc2 in range(2)],
                                ['mWo', ('mgT', t0 // 512)], bank)
                        self.xupdate(l, 1, g0 + t0, 512, oc, ps[:, bank, :], [('ps', bank)])
                S.barrier()
                M.release()
                if stg <= 7:
                    raise _Stop()

            for hd in range(4):
                do_head(hd)
            S.barrier()
            M.release()
            if stg <= 8:
                raise _Stop()

        if not getattr(self, "dbg_skipg0", False):
            do_group(0)
        do_group(1)
        M.release()

    def mixer_na(self, l):
        S, M, ins, ps = self.S, self.M, self.ins, self.ps
        SCALE = 0.125
        ones_b = self.ones_b
        id_b = self.id_b
        psb = self.psb
        wq_d = ins["na_w_qkv"].rearrange("(k p) n -> p k n", p=128)
        wo_d = ins["na_w_out"].rearrange("(k p) n -> p k n", p=128)
        M.mark()
        rowsel = M.alloc([32, 16, 128], BF16, "nrowsel")
        RM = M.alloc([32, TS], BF16, "nRM")
        cm = M.alloc([128, 256], BF16, "ncm")
        self.dma(rowsel[:], ins["na_rowsel"], writes=['nrowsel'])
        self.dma(RM[:], ins["na_rm"], writes=['nRM'])
        self.dma(cm[:], ins["na_cm"], writes=['ncm'])
        wbuf = [[M.alloc([128, KC, 128], BF16, "nW%d" % q) for q in range(3)] + [M.alloc([128, D], BF16, "nWo")] for _ in range(2)]
        tmpf_rr = RR([0, 1])
        rec = M.alloc([128, 256], F32, "nrec")
        NQ = 256

        def load_pair_w(c, b):
            for q in range(3):
                self.dma(wbuf[b][q][:], wq_d[:, :, q * D + c * 128:q * D + (c + 1) * 128], writes=[('nW', b, q)], q='pool')
            self.dma(wbuf[b][3][:], wo_d[:, c, :], writes=[('nW', b, 3)], q='pool')

        def do_group(grp):
            M.mark()
            g0, gn = (0, TP) if grp == 0 else (TP, TS)
            hT = M.alloc([128, KC, gn], BF16, "nhT")
            hk = [('nh', k, tq) for k in range(KC) for tq in range(gn // 256)]
            self.alloc_norm_scratch(256)
            for tq in range(gn // 256):
                self.norm_mod_range(l, 1, g0 + tq * 256, 256, hT, tq * 256, [('nh', k, tq) for k in range(KC)])
            if grp == 0:
                M.mark()
                tmpf = [M.alloc([128, 512], F32, "ntf") for _ in range(2)]
                Wfull = M.alloc([128, KC, D], BF16, "nWfull")
                for q, oname in ((1, "k_out"), (2, "v_out")):
                    self.dma(Wfull[:], wq_d[:, :, q * D:(q + 1) * D], writes=['nWfull'], q='pool')
                    for tc in range(TP // 128):
                        for half in range(2):
                            bank = self.bank_any.next()
                            self.mm(ps[:, bank, :], [(hT[:, k, tc * 128:(tc + 1) * 128], Wfull[:, k, half * 512:(half + 1) * 512]) for k in range(KC)],
                                    ['nWfull'] + hk, bank)
                            tb = tmpf_rr.next()
                            S.op('act', (lambda e, bank=bank, tb=tb: e.activation(out=tmpf[tb][:], in_=ps[:, bank, :], func=AF.Identity)),
                                 reads=[('ps', bank)], writes=[('ntf', tb)])
                            self.dma(self.outs[oname][tc * 128:(tc + 1) * 128, half * 512:(half + 1) * 512], tmpf[tb][:], reads=[('ntf', tb)])
                S.barrier()
                M.release()
            qT = M.alloc([128, gn], BF16, "nqT")
            kT = M.alloc([128, gn], BF16, "nkT")
            vtok = M.alloc([128, gn // 128, 128], BF16, "nvtok")
            oT = M.alloc([128, gn], BF16, "noT")
            PT = [M.alloc([128, 10, NQ], BF16, "nPT") for _ in range(2)]
            if grp == 1:
                ckb = M.alloc([128, 4, 128], BF16, "nckb")
                cvb = M.alloc([128, 4, 128], BF16, "ncvb")
                kcT = M.alloc([128, 512], BF16, "nkcT")
                Tt = M.alloc([128, 6, 256], F32, "nTt")
                ExT = M.alloc([128, 6, 256], BF16, "nExT")
                etmp = [M.alloc([128, 512], BF16, "netmp") for _ in range(2)]
                S.op('dve', lambda e: e.memset(Tt[:], NEG), writes=['nTt'])
            load_pair_w(0, 0)
            for c in range(KC):
                b = c % 2
                if c + 1 < KC:
                    load_pair_w(c + 1, (c + 1) % 2)
                Wq, Wk, Wv, Wo = wbuf[b]
                for t0 in range(0, gn, 512):
                    for q, dst, eng in ((0, qT, 'act'), (1, kT, 'dve')):
                        bank = self.bank_any.next()
                        self.mm(ps[:, bank, :], [(wbuf[b][q][:, k, :], hT[:, k, t0:t0 + 512]) for k in range(KC)], [('nW', b, q)] + hk, bank)
                        if eng == 'act':
                            S.op('act', (lambda e, dst=dst, bank=bank, t0=t0: e.activation(out=dst[:, t0:t0 + 512], in_=ps[:, bank, :], func=AF.Identity)),
                                 reads=[('ps', bank)], writes=[('nq', t0)])
                        else:
                            S.op('dve', (lambda e, dst=dst, bank=bank, t0=t0: e.tensor_copy(out=dst[:, t0:t0 + 512], in_=ps[:, bank, :])),
                                 reads=[('ps', bank)], writes=[('nk', t0)])
                    bank = self.bank_any.next()

                    def fv(e, bank=bank, t0=t0, Wv=Wv):
                        for tc in range(4):
                            for k in range(KC):
                                r = e.matmul(ps[:, bank, tc * 128:(tc + 1) * 128], lhsT=hT[:, k, t0 + tc * 128:t0 + (tc + 1) * 128], rhs=Wv[:, k, :],
                                             start=(k == 0 and tc == 0), stop=(k == KC - 1), skip_group_check=True)
                        return r
                    S.op('pe', fv, reads=[('nW', b, 2)] + hk, writes=[('ps', bank)])
                    S.op('act', (lambda e, bank=bank, t0=t0: e.activation(out=vtok[:, t0 // 128:t0 // 128 + 4, :], in_=ps[:, bank, :].rearrange("p (a b) -> p a b", b=128),
                                                                         func=AF.Identity)),
                         reads=[('ps', bank)], writes=[('nv', t0)])
                qk_all = [('nq', t0) for t0 in range(0, gn, 512)] + [('nk', t0) for t0 in range(0, gn, 512)]
                v_all = [('nv', t0) for t0 in range(0, gn, 512)]
                if grp == 1:
                    self.dma(ckb[:], ins["cache_k"].rearrange("(a p) n -> p a n", p=128)[:, :, c * 128:(c + 1) * 128], writes=['nckb'], q='pool')
                    self.dma(cvb[:], ins["cache_v"].rearrange("(a p) n -> p a n", p=128)[:, :, c * 128:(c + 1) * 128], writes=['ncvb'], q='pool')
                    bank = self.bank_any.next()

                    def ft(e, bank=bank):
                        for tc in range(4):
                            r = e.transpose(psb[:, bank, tc * 128:(tc + 1) * 128], ckb[:, tc, :], id_b)
                        return r
                    S.op('pe', ft, reads=['nckb', 'cb'], writes=[('ps', bank)])
                    S.op('dve', (lambda e, bank=bank: e.tensor_copy(out=kcT[:], in_=psb[:, bank, 0:512])), reads=[('ps', bank)], writes=['nkcT'])
                for half in range(2):
                    h = 2 * c + half
                    p0 = half * 64
                    pr = slice(p0, p0 + 64)
                    if grp == 0:
                        def stageA(sq_, pb, pr=pr):
                            t0 = sq_ * 256
                            bank = self.bank_any.next()

                            def fs(e):
                                for kc in range(2):
                                    r = e.matmul(ps[:, bank, kc * 256:(kc + 1) * 256], lhsT=kT[pr, t0 + kc * 128:t0 + (kc + 1) * 128],
                                                 rhs=qT[pr, t0:t0 + 256], start=True, stop=True)
                                return r
                            S.op('pe', fs, reads=qk_all, writes=[('ps', bank)])
                            S.op('act', lambda e: e.activation(out=PT[pb][:, 0:2, :], in_=ps[:, bank, :].rearrange("p (a b) -> p a b", b=256),
                                                               func=AF.Exp, scale=SCALE),
                                 reads=[('ps', bank)], writes=[('nPT', pb)])

                        def stageB(sq_, pb, pr=pr, half=half):
                            t0 = sq_ * 256
                            bank2 = self.bank_any.next()

                            def fo(e):
                                for kc in range(2):
                                    e.matmul(ps[:, bank2, 0:256], lhsT=vtok[:, t0 // 128 + kc, :], rhs=PT[pb][:, kc, :], start=(kc == 0), stop=(kc == 1),
                                             skip_group_check=True)
                                for kc in range(2):
                                    r = e.matmul(ps[:, bank2, 256:512], lhsT=ones_b, rhs=PT[pb][:, kc, :], start=False, stop=(kc == 1), skip_group_check=True)
                                return r
                            S.op('pe', fo, reads=[('nPT', pb), 'cb'] + v_all, writes=[('ps', bank2)])
                            S.op('act', lambda e: e.activation(out=rec[pr, 0:256], in_=ps[pr, bank2, 256:512], func=AF.Ln),
                                 reads=[('ps', bank2)], writes=['nrec'])
                            S.op('act', lambda e: e.activation(out=rec[pr, 0:256], in_=rec[pr, 0:256], func=AF.Exp, scale=-1.0),
                                 reads=['nrec'], writes=['nrec'])
                            S.op('dve', lambda e: e.tensor_tensor(out=oT[pr, t0:t0 + 256], in0=ps[pr, bank2, 0:256], in1=rec[pr, 0:256], op=ALU.mult),
                                 reads=[('ps', bank2), 'nrec'], writes=[('no', half)])
                        nblk = NPS
                    else:
                        for d in range(6):
                            for kl in range(2):
                                lo = max(0, 2 * d + kl - 11)
                                hi = min(4, 2 * d + kl + 4)
                                if hi <= lo:
                                    continue
                                m0 = 11 - 2 * d - kl + lo
                                src = ins["na_rexp"][h, m0:m0 + hi - lo].rearrange("m k q -> k m q")
                                self.dma(Tt[kl * 64:(kl + 1) * 64, d, lo * 64:hi * 64].rearrange("p (a b) -> p a b", b=64), src, writes=['nTt'])
                        for d in range(6):
                            S.op('dve', (lambda e, d=d: e.tensor_tensor(out=Tt[:, d, :], in0=Tt[:, d, :], in1=cm[:], op=ALU.add)),
                                 reads=['nTt', 'ncm'], writes=['nTt'])
                        S.op('act', lambda e: e.activation(out=ExT[:], in_=Tt[:], func=AF.Exp), reads=['nTt'], writes=['nExT'])

                        def wins_of(jb):
                            return [(d, 2 * jb - 2 + d) for d in range(6) if 0 <= 2 * jb - 2 + d < 16]

                        def stageA(jb, pb, pr=pr):
                            q0 = jb * NQ
                            wins = wins_of(jb)
                            for wp in range(0, len(wins), 2):
                                (d0, ck0), (d1, ck1) = wins[wp], wins[wp + 1]
                                bank = self.bank_any.next()

                                def fs(e, bank=bank, ck0=ck0, ck1=ck1):
                                    e.matmul(ps[:, bank, 0:NQ], lhsT=kT[pr, ck0 * 128:(ck0 + 1) * 128], rhs=qT[pr, q0:q0 + NQ], start=True, stop=False,
                                             skip_group_check=True)
                                    e.matmul(ps[:, bank, 0:NQ], lhsT=rowsel[:, ck0, :], rhs=RM[:, q0:q0 + NQ], start=False, stop=True, skip_group_check=True)
                                    e.matmul(ps[:, bank, NQ:2 * NQ], lhsT=kT[pr, ck1 * 128:(ck1 + 1) * 128], rhs=qT[pr, q0:q0 + NQ], start=False, stop=False,
                                             skip_group_check=True)
                                    return e.matmul(ps[:, bank, NQ:2 * NQ], lhsT=rowsel[:, ck1, :], rhs=RM[:, q0:q0 + NQ], start=False, stop=True,
                                                    skip_group_check=True)
                                S.op('pe', fs, reads=qk_all + ['nrowsel', 'nRM'], writes=[('ps', bank)])
                                tb = tmpf_rr.next()
                                S.op('act', (lambda e, bank=bank, tb=tb: e.activation(out=etmp[tb][:], in_=ps[:, bank, 0:2 * NQ], func=AF.Exp, scale=SCALE)),
                                     reads=[('ps', bank)], writes=[('netmp', tb)])
                                S.op('dve', (lambda e, tb=tb, wp=wp, d0=d0: e.tensor_tensor(
                                    out=PT[pb][:, wp:wp + 2, :], in0=etmp[tb][:].rearrange("p (a b) -> p a b", b=NQ), in1=ExT[:, d0:d0 + 2, :], op=ALU.mult)),
                                    reads=[('netmp', tb), 'nExT'], writes=[('nPTs', pb, wp), ('nPTs', pb, wp + 1)])
                            nw = len(wins)
                            for t2 in range(2):
                                bank = self.bank_any.next()

                                def fc(e, bank=bank, t2=t2):
                                    for q in range(2):
                                        tc = 2 * t2 + q
                                        r = e.matmul(ps[:, bank, q * NQ:(q + 1) * NQ], lhsT=kcT[pr, tc * 128:(tc + 1) * 128], rhs=qT[pr, q0:q0 + NQ],
                                                     start=True, stop=True)
                                    return r
                                S.op('pe', fc, reads=qk_all + ['nkcT'], writes=[('ps', bank)])
                                S.op('act', (lambda e, bank=bank, t2=t2, nw=nw: e.activation(
                                    out=PT[pb][:, nw + 2 * t2:nw + 2 * t2 + 2, :], in_=ps[:, bank, 0:2 * NQ].rearrange("p (a b) -> p a b", b=NQ), func=AF.Exp, scale=SCALE)),
                                    reads=[('ps', bank)], writes=[('nPTs', pb, nw + 2 * t2), ('nPTs', pb, nw + 2 * t2 + 1)])

                        def stageB(jb, pb, pr=pr, half=half):
                            q0 = jb * NQ
                            wins = wins_of(jb)
                            np_ = len(wins) + 4
                            bank2 = self.bank_any.next()
                            lhs = [vtok[:, ck, :] for (d, ck) in wins] + [cvb[:, tc, :] for tc in range(4)]

                            def fo(e):
                                for q in range(np_):
                                    e.matmul(ps[:, bank2, 0:NQ], lhsT=lhs[q], rhs=PT[pb][:, q, :], start=(q == 0), stop=(q == np_ - 1), skip_group_check=True)
                                for q in range(np_):
                                    r = e.matmul(ps[:, bank2, NQ:2 * NQ], lhsT=ones_b, rhs=PT[pb][:, q, :], start=False, stop=(q == np_ - 1), skip_group_check=True)
                                return r
                            S.op('pe', fo, reads=[('nPTs', pb, q) for q in range(np_)] + ['cb', 'ncvb'] + v_all, writes=[('ps', bank2)])
                            S.op('act', lambda e: e.activation(out=rec[pr, 0:NQ], in_=ps[pr, bank2, NQ:2 * NQ], func=AF.Ln),
                                 reads=[('ps', bank2)], writes=['nrec'])
                            S.op('act', lambda e: e.activation(out=rec[pr, 0:NQ], in_=rec[pr, 0:NQ], func=AF.Exp, scale=-1.0),
                                 reads=['nrec'], writes=['nrec'])
                            S.op('dve', lambda e: e.tensor_tensor(out=oT[pr, q0:q0 + NQ], in0=ps[pr, bank2, 0:NQ], in1=rec[pr, 0:NQ], op=ALU.mult),
                                 reads=[('ps', bank2), 'nrec'], writes=[('no', half)])
                        nblk = TS // NQ
                    prevb = None
                    for blk in range(nblk):
                        stageA(blk, blk % 2)
                        if prevb is not None:
                            stageB(prevb, prevb % 2)
                        prevb = blk
                    stageB(prevb, prevb % 2)
                if grp == 1 and c == 0:
                    self.dump_tile("noTs", oT[:], [128, TS], [('no', 0), ('no', 1)], BF16)
                    self.dump_tile("nvtoks", vtok[:], [128, 16, 128], v_all, BF16)
                    self.dump_tile("ncvb", cvb[:], [128, 4, 128], ['ncvb'], BF16)
                    self.dump_tile("nrec", rec[:], [128, 512], ['nrec'])
                if grp == 0 and c == 0:
                    self.dump_tile("noT", oT[:], [128, TP], [('no', 0), ('no', 1)], BF16)
                    self.dump_tile("nqT", qT[:], [128, TP], qk_all, BF16)
                    self.dump_tile("nkT", kT[:], [128, TP], qk_all, BF16)
                    self.dump_tile("nvtok", vtok[:], [128, TP // 128, 128], v_all, BF16)
                for t0 in range(0, gn, 512):
                    for oc in range(KC):
                        bank = self.bank_any.next()
                        self.mm(ps[:, bank, :], [(Wo[:, oc * 128:(oc + 1) * 128], oT[:, t0:t0 + 512])], [('nW', b, 3), ('no', 0), ('no', 1)], bank)
                        self.xupdate(l, 1, g0 + t0, 512, oc, ps[:, bank, :], [('ps', bank)])
            S.barrier()
            M.release()
        do_group(0)
        do_group(1)
        M.release()

    def mixer_fourier(self, l):
        S, M, ins, ps = self.S, self.M, self.ins, self.ps
        NTK = 256
        M.mark()
        Wout = M.alloc([128, KC, D], BF16, "fWout")
        boT = M.alloc([128, KC], F32, "fbo")
        Y1 = M.alloc([128, 16, D], BF16, "fY1")
        Y2 = M.alloc([128, 16, D], BF16, "fY2")
        tmp = [M.alloc([128, NTK], F32, "ftmp") for _ in range(2)]
        tmp_rr = RR([0, 1])
        self.dma(Wout[:], ins["fn_w_out"].rearrange("(k p) n -> p k n", p=128), writes=['fWout'], q='pool')
        self.dma(boT[:], ins["fn_b_outT"], writes=['fbo'])
        hk = [('fh', k) for k in range(KC)]
        tabs = {}

        def step1(t0, hT, ych, ykeys):
            self.norm_mod_range(l, 1, t0, NTK, hT, 0, hk)
            for tc in range(NTK // 128):
                for tab, Yd, eng in ((tabs['c'], Y1, 'act'), (tabs['s'], Y2, 'dve')):
                    for g0 in (0, 2):
                        bank = self.bank_any.next()

                        def f(e, tab=tab, g0=g0, bank=bank, tc=tc):
                            for gi in range(2):
                                g = g0 + gi
                                for kk in range(2):
                                    r = e.matmul(ps[:, bank, gi * 256:(gi + 1) * 256], lhsT=hT[:, 2 * g + kk, tc * 128:(tc + 1) * 128],
                                                 rhs=tab[:, kk, :], start=(kk == 0 and gi == 0), stop=(kk == 1), skip_group_check=True)
                            return r
                        S.op('pe', f, reads=hk[2 * g0:2 * g0 + 4] + ['fd256'], writes=[('ps', bank)])
                        c0 = g0 * 256
                        if eng == 'act':
                            S.op('act', (lambda e, Yd=Yd, bank=bank, tc=tc, c0=c0: e.activation(out=Yd[:, ych + tc, c0:c0 + 512], in_=ps[:, bank, :], func=AF.Identity)),
                                 reads=[('ps', bank)], writes=[ykeys[0]])
                        else:
                            S.op('dve', (lambda e, Yd=Yd, bank=bank, tc=tc, c0=c0: e.tensor_copy(out=Yd[:, ych + tc, c0:c0 + 512], in_=ps[:, bank, :])),
                                 reads=[('ps', bank)], writes=[ykeys[1]])

        def step23(t0, nsc, ych, CS, NS, fT, ykeys, tkeys):
            for cc in range(KC):
                bank = self.bank_any.next()
                pairs = []
                for sc in range(nsc):
                    pairs.append((Y1[:, ych + sc, cc * 128:(cc + 1) * 128], CS(sc)))
                    pairs.append((Y2[:, ych + sc, cc * 128:(cc + 1) * 128], NS(sc)))
                self.mm(ps[:, bank, 0:NTK], pairs, ykeys + tkeys, bank)
                S.op('act', (lambda e, cc=cc, bank=bank: e.activation(out=fT[:, cc, :], in_=ps[:, bank, 0:NTK], func=AF.Identity)),
                     reads=[('ps', bank)], writes=[('ffT', cc)])
            for oc in range(KC):
                bank = self.bank_any.next()
                self.mm(ps[:, bank, 0:NTK], [(Wout[:, kc, oc * 128:(oc + 1) * 128], fT[:, kc, :]) for kc in range(KC)],
                        ['fWout'] + [('ffT', cc) for cc in range(KC)], bank)
                tb = tmp_rr.next()
                S.op('act', (lambda e, oc=oc, bank=bank, tb=tb: e.activation(out=tmp[tb][:], in_=ps[:, bank, 0:NTK], func=AF.Identity,
                                                                            bias=boT[:, oc:oc + 1])),
                     reads=[('ps', bank), 'fbo'], writes=[('ftmp', tb)])
                self.xupdate(l, 1, t0, NTK, oc, tmp[tb][:], [('ftmp', tb)])

        M.mark()
        d256 = M.alloc([128, 3, 2, 256], BF16, "fd256")
        self.dma(d256[:], ins["dft256"], writes=['fd256'])
        c256, s256, ns256 = d256[:, 0], d256[:, 1], d256[:, 2]
        tabs['c'], tabs['s'] = c256, s256
        hT = M.alloc([128, KC, NTK], BF16, "fhT")
        fTp = M.alloc([128, KC, NTK], BF16, "ffTp")
        self.alloc_norm_scratch(NTK)
        for sq_ in range(NPS):
            t0 = sq_ * 256
            step1(t0, hT, 0, ['fY1p', 'fY2p'])
            step23(t0, 2, 0, (lambda sc: c256[:, sc, :]), (lambda sc: ns256[:, sc, :]), fTp, ['fY1p', 'fY2p'], ['fd256'])
        S.barrier()
        nq = TS // NTK
        for tq in range(nq):
            step1(TP + tq * NTK, hT, tq * 2, [('fY1s', tq), ('fY2s', tq)])
        ykeys = [('fY1s', tq) for tq in range(nq)] + [('fY2s', tq) for tq in range(nq)]
        S.barrier()
        M.release()
        M.mark()
        CSt = M.alloc([128, 16, NTK], BF16, "fCS")
        NSt = M.alloc([128, 16, NTK], BF16, "fNS")
        fT = M.alloc([128, KC, NTK], BF16, "ffT")
        cs_d = ins["dft2048"][0].rearrange("(k p) n -> p k n", p=128)
        ns_d = ins["dft2048"][1].rearrange("(k p) n -> p k n", p=128)
        for st in range(nq):
            self.dma(CSt[:], cs_d[:, :, st * NTK:(st + 1) * NTK], writes=['fCS'])
            self.dma(NSt[:], ns_d[:, :, st * NTK:(st + 1) * NTK], writes=['fNS'])
            step23(TP + st * NTK, 16, 0, (lambda sc: CSt[:, sc, :]), (lambda sc: NSt[:, sc, :]), fT, ykeys, ['fCS', 'fNS'])
        M.release()
        M.release()

    def mixer_gmlp(self, l):
        S, M, ins, ps = self.S, self.M, self.ins, self.ps
        NTK = 256
        NC_ = NTK // 128
        M.mark()
        Win = M.alloc([128, KC, 2 * D], BF16, "gWin")
        Wout = M.alloc([128, KC, D], BF16, "gWout")
        wsT = M.alloc([128, 4, 128], BF16, "gws")
        binT = M.alloc([128, 16], F32, "gbinT")
        binV = M.alloc([128, D], F32, "gbinV")
        vgb = M.alloc([128, D], F32, "gvgb")
        bsr = M.alloc([1, 512], F32, "gbsr")
        hT2 = [M.alloc([128, KC, NTK], BF16, "ghT") for _ in range(2)]
        uT2 = [M.alloc([128, KC, NTK], BF16, "guT") for _ in range(2)]
        vn2 = [M.alloc([128, NC_, D], BF16, "gvn") for _ in range(2)]
        vraw = M.alloc([128, D], F32, "gvraw")
        junk = M.alloc([128, 512], BF16, "gjunk")
        tmp = [M.alloc([128, 512], F32, "gtmp") for _ in range(2)]
        ssq = M.alloc([128, 2], F32, "gssq")
        rsv = M.alloc([128, 1], F32, "grsv")
        self.alloc_norm_scratch(NTK)
        epsT, ones_f = self.epsT, self.ones_f
        self.dma(Win[:], ins["gm_w_in"].rearrange("(k p) n -> p k n", p=128), writes=['gWin'], q='pool')
        self.dma(Wout[:], ins["gm_w_out"].rearrange("(k p) n -> p k n", p=128), writes=['gWout'], q='pool')
        self.dma(wsT[:], ins["gm_w_sT"], writes=['gws'], q='pool')
        self.dma(binT[:], ins["gm_b_inT"], writes=['gbinT'])
        self.dma(binV[:], ins["gm_b_in"][:, D:2 * D].partition_broadcast(128), writes=['gbinV'])
        self.dma(vgb[:], ins["gm_v_g"].partition_broadcast(128), writes=['gvgb'])
        self.dma(bsr[:], ins["gm_b_s"], writes=['gbsr'])
        tmp_rr = RR([0, 1])
        tiles = list(range(0, T, NTK))
        env = locals()
        for q, t0 in enumerate(tiles):
            self.gmlp_tile(l, t0, q % 2, env, 'A')
            if q > 0:
                self.gmlp_tile(l, tiles[q - 1], (q - 1) % 2, env, 'B')
        self.gmlp_tile(l, tiles[-1], (len(tiles) - 1) % 2, env, 'B')
        M.release()

    def gmlp_tile(self, l, t0, pb, env, stage):
        S, ps = self.S, self.ps
        NTK, NC_, Win, Wout, wsT, binT, binV, vgb, bsr = (env[k] for k in ("NTK", "NC_", "Win", "Wout", "wsT", "binT", "binV", "vgb", "bsr"))
        vraw, junk, tmp, ssq, rsv, tmp_rr, epsT, ones_f = (env[k] for k in ("vraw", "junk", "tmp", "ssq", "rsv", "tmp_rr", "epsT", "ones_f"))
        hT, uT, vn = env["hT2"][pb], env["uT2"][pb], env["vn2"][pb]
        hk = [('gh', pb, k) for k in range(KC)]
        if stage == 'A':
            self.norm_mod_range(l, 1, t0, NTK, hT, 0, hk)
            for oc in range(KC):
                bank = self.bank_any.next()
                self.mm(ps[:, bank, 0:NTK], [(Win[:, k, oc * 128:(oc + 1) * 128], hT[:, k, :]) for k in range(KC)], ['gWin'] + hk, bank)
                S.op('act', (lambda e, oc=oc, bank=bank: e.activation(out=uT[:, oc, :], in_=ps[:, bank, 0:NTK], func=AF.Gelu_apprx_tanh,
                                                                     bias=binT[:, oc:oc + 1])),
                     reads=[('ps', bank), 'gbinT'], writes=[('gu', pb, oc)])
            for tc in range(NC_):
                for half in range(2):
                    bank = self.bank_any.next()
                    self.mm(ps[:, bank, :], [(hT[:, k, tc * 128:(tc + 1) * 128], Win[:, k, D + half * 512:D + (half + 1) * 512]) for k in range(KC)],
                            ['gWin'] + hk, bank)
                    tb = tmp_rr.next()
                    S.op('dve', (lambda e, bank=bank, tb=tb, half=half: e.tensor_tensor(
                        out=tmp[tb][:], in0=ps[:, bank, :], in1=binV[:, half * 512:(half + 1) * 512], op=ALU.add)),
                        reads=[('ps', bank), 'gbinV'], writes=[('gtmp', tb)])
                    S.op('act', (lambda e, tb=tb, half=half: e.activation(out=vraw[:, half * 512:(half + 1) * 512], in_=tmp[tb][:],
                                                                         func=AF.Gelu_apprx_tanh)),
                         reads=[('gtmp', tb)], writes=[('gvraw', half)])
                    S.op('dve', (lambda e, half=half: e.scalar_tensor_tensor(
                        out=junk[:], in0=vraw[:, half * 512:(half + 1) * 512], scalar=1.0, in1=vraw[:, half * 512:(half + 1) * 512],
                        op0=ALU.mult, op1=ALU.mult, accum_out=ssq[:, half:half + 1])),
                        reads=[('gvraw', half)], writes=['gjunk', ('gssq', half)])
                S.op('dve', lambda e: e.tensor_tensor(out=rsv[:], in0=ssq[:, 0:1], in1=ssq[:, 1:2], op=ALU.add),
                     reads=[('gssq', 0), ('gssq', 1)], writes=['grsv'])
                S.op('act', lambda e: e.activation(out=rsv[:], in_=rsv[:], func=AF.Sqrt, scale=1.0 / D, bias=epsT[:, 0:1]),
                     reads=['grsv', 'eps'], writes=['grsv'])
                S.op('dve', lambda e: e.reciprocal(out=rsv[:], in_=rsv[:]), reads=['grsv'], writes=['grsv'])
                S.op('dve', (lambda e, tc=tc: e.scalar_tensor_tensor(
                    out=vn[:, tc, :], in0=vraw[:], scalar=rsv[:, 0:1], in1=vgb[:], op0=ALU.mult, op1=ALU.mult)),
                    reads=[('gvraw', 0), ('gvraw', 1), 'grsv', 'gvgb'], writes=[('gvn', pb, tc)])
        else:
            for oc in range(KC):
                g = oc // 2
                bank = self.bank_any.next()

                def f(e, oc=oc, g=g, bank=bank):
                    for tc in range(NC_):
                        e.matmul(ps[:, bank, tc * 128:(tc + 1) * 128], lhsT=vn[:, tc, oc * 128:(oc + 1) * 128], rhs=wsT[:, g, :],
                                 start=(tc == 0), stop=False, skip_group_check=True)
                        r = e.matmul(ps[:, bank, tc * 128:(tc + 1) * 128], lhsT=ones_f[0:1, :], rhs=bsr[0:1, g * 128:(g + 1) * 128],
                                     start=False, stop=True, skip_group_check=True)
                    return r
                S.op('pe', f, reads=[('gvn', pb, tc) for tc in range(NC_)] + ['gws', 'gbsr', 'cf'], writes=[('ps', bank)])
                S.op('dve', (lambda e, oc=oc, bank=bank: e.tensor_tensor(out=uT[:, oc, :], in0=uT[:, oc, :], in1=ps[:, bank, 0:NTK], op=ALU.mult)),
                     reads=[('gu', pb, oc), ('ps', bank)], writes=[('gu', pb, oc)])
            for oc in range(KC):
                bank = self.bank_any.next()
                self.mm(ps[:, bank, 0:NTK], [(Wout[:, k, oc * 128:(oc + 1) * 128], uT[:, k, :]) for k in range(KC)],
                        ['gWout'] + [('gu', pb, k) for k in range(KC)], bank)
                self.xupdate(l, 1, t0, NTK, oc, ps[:, bank, 0:NTK], [('ps', bank)])


def _consts():
    cf = np.zeros((9, 128, 128), np.float32)
    s = np.arange(128)[:, None]
    t = np.arange(128)[None, :]
    cf[0] = 1.0
    cf[1] = np.eye(128)
    cf[2] = (s <= t)
    cf[3] = (s >= t)
    cf[4] = np.where(t <= s, 0.0, NEG)
    cf[5] = np.where(t >= s, 0.0, NEG)
    cf[6] = np.where(s <= t, 0.0, -NEG)
    cf[7] = np.where(s >= t, 0.0, -NEG)
    cf[8] = np.eye(128)[::-1]
    cb = np.zeros((2, 128, 128), np.float32)
    cb[0] = np.eye(128)
    cb[1] = 1.0
    return (np.ascontiguousarray(cf.transpose(1, 0, 2)),
            np.ascontiguousarray(cb.transpose(1, 0, 2)).astype(ml_dtypes.bfloat16))


def _na_tables(rpb):
    bf = ml_dtypes.bfloat16
    dc = np.clip(np.arange(64)[:, None] - np.arange(64)[None, :] + 15, 0, 30)
    rexp = np.ascontiguousarray(rpb[:, ::-1, :][:, :, dc])
    qc = np.arange(64)
    qs = np.clip(qc - 8, 0, 48)
    kc = np.arange(64)[:, None]
    colmask = np.where((kc >= qs[None, :]) & (kc < qs[None, :] + 16), 0.0, NEG).astype(np.float32)
    cm = np.tile(colmask, (2, 4))
    rowsel = np.zeros((32, 16, 128), np.float32)
    for c in range(16):
        rowsel[2 * c, c, :64] = 1.0
        rowsel[2 * c + 1, c, 64:] = 1.0
    qr = np.arange(32)
    rs = np.clip(qr - 4, 0, 24)
    kr = np.arange(32)[:, None]
    rm = np.where((kr >= rs[None, :]) & (kr < rs[None, :] + 8), 0.0, NEG).astype(np.float32)
    rm = np.repeat(rm, 64, axis=1)
    return {"na_rexp": rexp, "na_cm": np.ascontiguousarray(cm).astype(bf), "na_rowsel": rowsel.astype(bf), "na_rm": np.ascontiguousarray(rm).astype(bf)}


def _fourier_tables():
    bf = ml_dtypes.bfloat16
    n = np.arange(256, dtype=np.float64)
    ang = 2.0 * np.pi * ((n[:, None] * n[None, :]) % 256) / 256.0
    c, sn = np.cos(ang) / 16.0, np.sin(ang) / 16.0
    t = np.stack([c, sn, -sn], 0).reshape(3, 2, 128, 256).transpose(2, 0, 1, 3)
    n2 = np.arange(2048, dtype=np.int64)
    ang2 = 2.0 * np.pi * ((n2[:, None] * n2[None, :]) % 2048).astype(np.float64) / 2048.0
    r = 1.0 / np.sqrt(2048.0)
    d2048 = np.stack([np.cos(ang2) * r, -np.sin(ang2) * r], 0)
    return {"dft256": np.ascontiguousarray(t).astype(bf), "dft2048": np.ascontiguousarray(d2048).astype(bf)}


def _fm(a):
    t, d = a.shape
    return np.ascontiguousarray(a.reshape(t, d // 128, 128).transpose(2, 1, 0))


def _fm_inv(a):
    p, c, t = a.shape
    return np.ascontiguousarray(a.transpose(2, 1, 0).reshape(t, c * p))


def _vecT(v):
    sh = v.shape
    n = sh[-1] // 128
    a = v.reshape(sh[:-1] + (n, 128))
    return np.ascontiguousarray(np.moveaxis(a, -1, 0))


def make_in_maps(inp, builder):
    cf, cb = _consts()
    maps = []
    f32 = lambda a: np.ascontiguousarray(np.asarray(a, dtype=np.float32))
    shared = {
        "b_adaT": _vecT(f32(inp["b_ada"])),
        "norm_gT": _vecT(f32(inp["norm_g"])),
        "final_gT": _vecT(f32(inp["final_g"])),
        "cf32": cf, "cbf16": cb,
        "w_ada": f32(inp["w_ada"]), "ffn_w1": f32(inp["ffn_w1"]), "ffn_w3": f32(inp["ffn_w3"]), "ffn_w2": f32(inp["ffn_w2"]),
    }
    shared.update(_fourier_tables())
    shared.update(_na_tables(f32(inp["na_rpb"][0])))
    shared.update({
        "ml_w_qkv": f32(inp["ml_w_qkv"][0]), "ml_w_og": f32(inp["ml_w_og"][0]), "ml_w_out": f32(inp["ml_w_out"][0]),
        "ml_w_if": np.ascontiguousarray(np.concatenate([f32(inp["ml_w_if"][0, 0]), f32(inp["ml_w_if"][0, 1])], axis=1)),
        "ml_b_if": f32(inp["ml_b_if"][0]).reshape(1, 16),
        "ml_head_gT": _vecT(f32(inp["ml_head_g"][0])),
    })
    shared.update({"na_w_qkv": f32(inp["na_w_qkv"][0]), "na_w_out": f32(inp["na_w_out"][0])})
    shared.update({
        "fn_w_out": f32(inp["fn_w_out"][0]), "fn_b_outT": _vecT(f32(inp["fn_b_out"][0])),
        "gm_w_in": f32(inp["gm_w_in"][0]), "gm_w_out": f32(inp["gm_w_out"][0]),
        "gm_b_inT": _vecT(f32(inp["gm_b_in"][0])), "gm_b_in": f32(inp["gm_b_in"]), "gm_v_g": f32(inp["gm_v_g"]),
        "gm_w_sT": np.ascontiguousarray(f32(inp["gm_w_s"][0]).transpose(2, 0, 1)),
        "gm_b_s": f32(inp["gm_b_s"][0]).reshape(1, 512),
    })
    for ci in range(8):
        b = ci // 4
        xp = f32(inp["x_prompt"][NPS * ci:NPS * (ci + 1)]).reshape(TP, D)
        xs = f32(inp["x_sample"][b])
        m = dict(shared)
        m["xT"] = _fm(np.concatenate([xp, xs], axis=0))
        cond = np.stack([f32(inp["c_ctx"]), f32(inp["c"][b])], axis=-1)
        m["condT"] = np.ascontiguousarray(cond.reshape(KC, 128, 2).transpose(1, 0, 2))
        m["ml_caug0"] = np.ascontiguousarray(np.concatenate([f32(inp["state_mlstm_C"][b, 0]), f32(inp["state_mlstm_n"][b, 0])[..., None]], axis=-1))
        m["ml_m0"] = f32(inp["state_mlstm_m"][b, 0]).reshape(1, 8)
        m["cache_k"] = f32(inp["cache_na_k"][b, 0]).reshape(512, D)
        m["cache_v"] = f32(inp["cache_na_v"][b, 0]).reshape(512, D)
        maps.append({k: v for k, v in m.items() if k in builder.ins})
    return maps


def kernel(**inp):
    B = Builder()
    nc = B.build()
    maps = make_in_maps(inp, B)
    res = run_bass_kernel_spmd(nc, maps, core_ids=list(range(8)))
    nb = 8 * NPS
    yp = np.zeros((nb, 256, D), np.float32)
    ys = np.zeros((2, 2048, D), np.float32)
    sC = np.zeros((nb, 1, 2, 4, 256, 256), np.float32)
    sn = np.zeros((nb, 1, 2, 4, 256), np.float32)
    sm = np.zeros((nb, 1, 2, 4), np.float32)
    ck = np.zeros((nb, 1, 256, 16, 64), np.float32)
    cv = np.zeros((nb, 1, 256, 16, 64), np.float32)
    for ci in range(8):
        r = res.results[ci]
        y = _fm_inv(np.asarray(r["yT"]))
        sl = slice(NPS * ci, NPS * (ci + 1))
        yp[sl] = y[:TP].reshape(NPS, 256, D)
        if ci % 4 == 0:
            ys[ci // 4] = y[TP:]
        ca = np.asarray(r["caug_out"])
        sC[sl, 0] = ca[..., :256]
        sn[sl, 0] = ca[..., 256]
        sm[sl, 0] = np.asarray(r["m_out"]).reshape(NPS, 2, 4)
        ck[sl, 0] = np.asarray(r["k_out"]).reshape(NPS, 256, 16, 64)
        cv[sl, 0] = np.asarray(r["v_out"]).reshape(NPS, 256, 16, 64)
    return yp, ys, sC, sn, sm, ck, cv
```

```python
import numpy as np
import ml_dtypes
import concourse.bass as bass
import concourse.mybir as mybir
from concourse.bass_utils import run_bass_kernel_spmd

F32 = mybir.dt.float32
BF16 = mybir.dt.bfloat16
AF = mybir.ActivationFunctionType
ALU = mybir.AluOpType
AX = mybir.AxisListType

L = 4
D = 1024
DFF = 2816
KC = 8
FC = 22
TP = 1024
TS = 2048
T = TP + TS
TT = 512
NT = T // TT
NPS = 4
EPS = 1e-6
NEG = -30000.0


class Sched:
    CE = ('pe', 'act', 'dve', 'pool')
    QE = ('pe', 'act', 'dve', 'pool', 'sp')

    SELF_WAIT = True

    def __init__(self, nc, n_dma_sems=32):
        self.nc = nc
        self.ops = {e: [] for e in self.QE}
        self.cnt = {e: 0 for e in self.CE}
        self.known = {e: {} for e in self.QE}
        self.last_w = {}
        self.readers = {}
        self.n_dma = n_dma_sems
        self.dma_val = [0] * n_dma_sems
        self.dma_rr = 0
        self.n_ops = 0
        self.n_waits = 0

    def op(self, eng, fn, reads=(), writes=(), dma=False):
        need = {}

        def add(tok):
            s, v = tok
            if need.get(s, 0) < v:
                need[s] = v
        for r in reads:
            t = self.last_w.get(r)
            if t:
                add(t)
        for w in writes:
            t = self.last_w.get(w)
            if t:
                add(t)
            for t in self.readers.get(w, ()):
                add(t)
        if dma:
            i = self.dma_rr
            self.dma_rr = (self.dma_rr + 1) % self.n_dma
            sname = ('dma', i)
            if self.dma_val[i] > 0 and need.get(sname, 0) < self.dma_val[i]:
                need[sname] = self.dma_val[i]
            self.dma_val[i] += 16
            tok = (sname, self.dma_val[i])
        else:
            self.cnt[eng] += 1
            tok = (eng, self.cnt[eng])
        waits = []
        for s, v in need.items():
            if s == eng and (eng == 'pe' or not self.SELF_WAIT):
                continue
            if self.known[eng].get(s, 0) < v:
                waits.append((s, v))
                self.known[eng][s] = v
        self.ops[eng].append((waits, fn, tok))
        self.n_ops += 1
        self.n_waits += len(waits)
        for r in reads:
            self.readers.setdefault(r, []).append(tok)
        for w in writes:
            self.last_w[w] = tok
            self.readers[w] = []
        return tok

    def _all_toks(self):
        toks = [(e, self.cnt[e]) for e in self.CE if self.cnt[e] > 0]
        toks += [(('dma', i), v) for i, v in enumerate(self.dma_val) if v > 0]
        return toks

    def barrier(self):
        toks = self._all_toks()
        for e in self.QE:
            waits = []
            for s, v in toks:
                if s == e:
                    continue
                if self.known[e].get(s, 0) < v:
                    waits.append((s, v))
                    self.known[e][s] = v
            if waits:
                self.ops[e].append((waits, None, None))
                self.n_waits += len(waits)
        self.last_w = {}
        self.readers = {}

    def final_wait(self, eng='sp'):
        waits = [(s, v) for s, v in self._all_toks() if s != eng and self.known[eng].get(s, 0) < v]
        self.ops[eng].append((waits, None, None))

    def emit(self):
        nc = self.nc
        from contextlib import ExitStack
        with ExitStack() as es:
            sems = {}
            for e in self.CE:
                sems[e] = es.enter_context(nc.semaphore("s_" + e))
            for i in range(self.n_dma):
                sems[('dma', i)] = es.enter_context(nc.semaphore("s_dma%d" % i))
            block = es.enter_context(nc.Block())

            def mk(ename):
                def run(eng):
                    for waits, fn, tok in self.ops[ename]:
                        for s, v in waits:
                            eng.wait_ge(sems[s], v)
                        if fn is None:
                            continue
                        ins = fn(eng)
                        if tok[0] == ename:
                            ins.then_inc(sems[tok[0]], 1)
                        else:
                            ins.then_inc(sems[tok[0]], 16)
                return run
            block.tensor(mk('pe'))
            block.scalar(mk('act'))
            block.vector(mk('dve'))
            block.gpsimd(mk('pool'))
            block.sync(mk('sp'))


class Mem:
    def __init__(self, nc, base=16512, limit=229344):
        self.nc = nc
        self.top = base
        self.limit = limit
        self.n = 0
        self.marks = []
        self.peak = base

    def alloc(self, shape, dtype, name=None):
        sz = int(np.prod(shape[1:])) * (2 if dtype == BF16 else 4)
        sz = (sz + 63) // 64 * 64
        off = self.top
        self.top += sz
        self.peak = max(self.peak, self.top)
        assert self.top <= self.limit, "SBUF overflow %d > %d" % (self.top, self.limit)
        self.n += 1
        return self.nc.alloc_sbuf_tensor_at("%s_%d" % (name or "t", self.n), list(shape), dtype, offset=off)

    def mark(self):
        self.marks.append(self.top)

    def release(self):
        self.top = self.marks.pop()


class RR:
    def __init__(self, items):
        self.items = list(items)
        self.i = 0

    def next(self):
        v = self.items[self.i]
        self.i = (self.i + 1) % len(self.items)
        return v


class _Stop(Exception):
    pass


class Builder:
    def __init__(self, dumps=(), layers=L, do_mixer=True, do_ffn=True, layer_list=None):
        self.nc = nc = bass.Bass("TRN2", target_bir_lowering=False)
        self.S = Sched(nc)
        self.M = Mem(nc)
        self.ins = {}
        self.outs = {}
        self.dumps = set(dumps)
        self.layers = layers
        self.layer_list = list(layer_list) if layer_list is not None else list(range(layers))
        self.do_mixer = do_mixer
        self.do_ffn = do_ffn
        self.ps = nc.alloc_psum_tensor("ps", [128, 8, 512], F32)
        self.psb = self.ps.bitcast(BF16)
        self.bank_a = RR([0, 1])
        self.bank_b = RR([2, 3])
        self.bank_y = RR([4, 5, 6, 7])
        self.bank_m = RR([6, 7])
        self.bank_any = RR(range(8))
        self.uid = 0

    def din(self, name, shape, dtype=F32):
        self.ins[name] = self.nc.dram_tensor(name, list(shape), dtype, kind="ExternalInput").ap()
        return self.ins[name]

    def dout(self, name, shape, dtype=F32):
        self.outs[name] = self.nc.dram_tensor(name, list(shape), dtype, kind="ExternalOutput").ap()
        return self.outs[name]

    def key(self, base):
        self.uid += 1
        return (base, self.uid)

    def dma(self, out, in_, reads=(), writes=(), q='sp'):
        return self.S.op(q, lambda e: e.dma_start(out=out, in_=in_), reads=reads, writes=writes, dma=True)

    def declare(self):
        din = self.din
        din("xT", [128, KC, T])
        din("condT", [128, KC, 2])
        din("b_adaT", [128, L, 72])
        din("norm_gT", [128, L, 3, KC])
        din("final_gT", [128, KC])
        din("cf32", [128, 9, 128])
        din("cbf16", [128, 2, 128], BF16)
        din("w_ada", [L, D, 9 * D])
        din("ffn_w1", [L, 2, D, DFF])
        din("ffn_w3", [L, 2, D, DFF])
        din("ffn_w2", [L, 2, DFF, D])
        din("ml_w_qkv", [D, 3 * D])
        din("ml_w_og", [D, D])
        din("ml_w_out", [D, D])
        din("ml_w_if", [D, 16])
        din("ml_b_if", [1, 16])
        din("ml_head_gT", [128, KC])
        din("ml_caug0", [2, 4, 256, 257])
        din("ml_m0", [1, 8])
        self.dout("caug_out", [NPS, 2, 4, 256, 257])
        self.dout("m_out", [NPS, 8])
        din("na_w_qkv", [D, 3 * D])
        din("na_w_out", [D, D])
        din("cache_k", [512, D])
        din("cache_v", [512, D])
        din("na_rexp", [16, 15, 64, 64])
        din("na_cm", [128, 256], BF16)
        din("na_rowsel", [32, 16, 128], BF16)
        din("na_rm", [32, TS], BF16)
        self.dout("k_out", [TP, D])
        self.dout("v_out", [TP, D])
        din("fn_w_out", [D, D])
        din("fn_b_outT", [128, KC])
        din("dft256", [128, 3, 2, 256], BF16)
        din("dft2048", [2, 2048, 2048], BF16)
        din("gm_w_in", [D, 2 * D])
        din("gm_w_out", [D, D])
        din("gm_b_inT", [128, 16])
        din("gm_b_in", [1, 2 * D])
        din("gm_v_g", [1, D])
        din("gm_w_sT", [128, 4, 128])
        din("gm_b_s", [1, 512])
        self.dout("yT", [128, KC, T])
        for d in sorted(self.dumps):
            if d.startswith("l") or d == "x0":
                self.dout("dbg_" + d, [128, KC, T])

    def dump_tile(self, name, tile, shape, reads, dtype=F32):
        if name not in self.dumps:
            return
        o = self.nc.dram_tensor("dbg_" + name, list(shape), dtype, kind="ExternalOutput").ap()
        self.outs["dbg_" + name] = o
        self.dma(o, tile, reads=reads)

    def dump(self, name):
        if name not in self.dumps:
            return
        o = self.outs["dbg_" + name]
        for tt in range(NT):
            sl = slice(tt * TT, (tt + 1) * TT)
            self.dma(o[:, :, sl], self.xT[:, :, sl], reads=[('x', k, tt) for k in range(KC)])

    def prologue(self):
        S, M, ins = self.S, self.M, self.ins
        self.xT = M.alloc([128, KC, T], F32, "xT")
        self.cf = M.alloc([128, 2, 128], F32, "cf")
        self.cb = M.alloc([128, 2, 128], BF16, "cb")
        self.mods = M.alloc([128, L, 72, 2], F32, "mods")
        self.modA = M.alloc([128, L, 3, 2, KC], F32, "modA")
        self.modC = M.alloc([128, L, 3, 2, KC], F32, "modC")
        self.badaT = M.alloc([128, L, 72], F32, "bada")
        self.ngT = M.alloc([128, L, 3, KC], F32, "ng")
        self.fgT = M.alloc([128, KC], F32, "fg")
        condT = M.alloc([128, KC, 2], F32, "cond")
        sc = M.alloc([128, KC, 2], BF16, "sc")
        self.ones_f = self.cf[:, 0, :]
        self.id_f = self.cf[:, 1, :]
        self.id_b = self.cb[:, 0, :]
        self.ones_b = self.cb[:, 1, :]
        self.dma(self.cf[:], ins["cf32"][:, 0:2, :], writes=['cf'])
        self.dma(self.cb[:], ins["cbf16"], writes=['cb'])
        self.dma(self.badaT[:], ins["b_adaT"], writes=['bada'])
        self.dma(self.ngT[:], ins["norm_gT"], writes=['ng'])
        self.dma(self.fgT[:], ins["final_gT"], writes=['fg'])
        self.dma(condT[:], ins["condT"], writes=['cond'])
        for tt in range(NT):
            sl = slice(tt * TT, (tt + 1) * TT)
            self.dma(self.xT[:, :, sl], ins["xT"][:, :, sl], writes=[('x', k, tt) for k in range(KC)])
        S.op('act', lambda e: e.activation(out=sc[:], in_=condT[:], func=AF.Silu), reads=['cond'], writes=['sc'])
        M.mark()
        wb = [M.alloc([128, KC, 1024], BF16, "wada") for _ in range(2)]
        ps = self.ps
        it = 0
        self.sc = sc
        for l in self.layer_list[:1]:
            wv = ins["w_ada"][l].rearrange("(k p) n -> p k n", p=128)
            for s in range(9):
                b = it % 2
                it += 1
                self.dma(wb[b][:], wv[:, :, s * 1024:(s + 1) * 1024], writes=[('wada', b)], q='pool')
                bank = self.bank_m.next()

                def mm(e, b=b, bank=bank):
                    for oc in range(8):
                        for k in range(KC):
                            ins_ = e.matmul(ps[:, bank, oc * 2:oc * 2 + 2], lhsT=wb[b][:, k, oc * 128:(oc + 1) * 128],
                                            rhs=sc[:, k, :], start=(k == 0 and oc == 0), stop=(k == KC - 1), skip_group_check=True)
                    return ins_
                S.op('pe', mm, reads=[('wada', b), 'sc'], writes=[('ps', bank)])
                for j in range(2):
                    S.op('dve', (lambda e, bank=bank, j=j, l=l, s=s: e.tensor_tensor(
                        out=self.mods[:, l, s * 8:(s + 1) * 8, j],
                        in0=ps[:, bank, 0:16].rearrange("p (c j) -> p c j", j=2)[:, :, j],
                        in1=self.badaT[:, l, s * 8:(s + 1) * 8], op=ALU.add)),
                        reads=[('ps', bank), 'bada'], writes=[('mods', l)])
        M.release()
        self.derive_mods(self.layer_list[0])

    def derive_mods(self, l):
        S = self.S
        if True:
            for i in range(3):
                for j in range(2):
                    S.op('dve', (lambda e, l=l, i=i, j=j: e.scalar_tensor_tensor(
                        out=self.modA[:, l, i, j, :], in0=self.mods[:, l, (3 * i + 1) * 8:(3 * i + 2) * 8, j], scalar=1.0,
                        in1=self.ngT[:, l, i, :], op0=ALU.add, op1=ALU.mult)),
                        reads=[('mods', l), 'ng'], writes=[('modA', l)])
                    S.op('dve', (lambda e, l=l, i=i, j=j: e.tensor_scalar(
                        out=self.modC[:, l, i, j, :], in0=self.mods[:, l, (3 * i + 2) * 8:(3 * i + 3) * 8, j],
                        scalar1=(1.0 if i == 1 else 0.5), scalar2=None, op0=ALU.mult)),
                        reads=[('mods', l)], writes=[('modC', l)])

    def shiftAP(self, l, i, j, k):
        return self.mods[:, l, 3 * i * 8 + k, j:j + 1]

    def alloc_norm_scratch(self, n=TT, share_tmp=None):
        M = self.M
        self.sq = M.alloc([128, KC, n], BF16, "sq")
        self.rstd = M.alloc([128, n], F32, "rstd")
        if share_tmp is not None:
            self.ntmp = share_tmp
        else:
            self.ntmp = [M.alloc([128, n], F32, "ntmp") for _ in range(2)]
        self.ntmp_rr = RR([0, 1])

    def xkeys(self, t0, n):
        return [('x', k, tt) for k in range(KC) for tt in range(t0 // TT, (t0 + n - 1) // TT + 1)]

    def rstd_range(self, t0, n):
        S, ps = self.S, self.ps
        sq, rstd, xT, ones_b, epsT = self.sq, self.rstd, self.xT, self.ones_b, self.epsT
        sl = slice(t0, t0 + n)
        S.op('act', lambda e: e.activation(out=sq[:, :, 0:n], in_=xT[:, :, sl], func=AF.Square), reads=self.xkeys(t0, n), writes=['sq'])
        bank = self.bank_m.next()
        self.mm(ps[:, bank, 0:n], [(ones_b, sq[:, k, 0:n]) for k in range(KC)], ['sq', 'cb'], bank)
        S.op('act', lambda e: e.activation(out=rstd[:, 0:n], in_=ps[:, bank, 0:n], func=AF.Ln, scale=1.0 / D, bias=epsT[:, 0:1]),
             reads=[('ps', bank), 'eps'], writes=['rstd'])
        S.op('act', lambda e: e.activation(out=rstd[:, 0:n], in_=rstd[:, 0:n], func=AF.Exp, scale=-0.5), reads=['rstd'], writes=['rstd'])

    def rstd_tile(self, tt):
        self.rstd_range(tt * TT, TT)

    def norm_mod(self, l, i, tts, hT, hoff=0):
        for tt in tts:
            self.norm_mod_range(l, i, tt * TT, TT, hT, tt * TT - hoff, [('h', k, tt) for k in range(KC)])

    def norm_mod_range(self, l, i, t0, n, hT, h0, hkeys, kset=None):
        S = self.S
        j = 0 if t0 < TP else 1
        sl = slice(t0, t0 + n)
        hsl = slice(h0, h0 + n)
        self.rstd_range(t0, n)
        ntmp, xT, modA, rstd, mods = self.ntmp, self.xT, self.modA, self.rstd, self.mods
        xk = self.xkeys(t0, n)
        for k in (range(KC) if kset is None else kset):
            tb = self.ntmp_rr.next()
            S.op('dve', (lambda e, k=k, tb=tb: e.scalar_tensor_tensor(
                out=ntmp[tb][:, 0:n], in0=xT[:, k, sl], scalar=modA[:, l, i, j, k:k + 1], in1=rstd[:, 0:n],
                op0=ALU.mult, op1=ALU.mult)), reads=[xk[kk] for kk in range(len(xk)) if xk[kk][1] == k] + ['rstd', ('modA', l)],
                writes=[('ntmp', tb)])
            S.op('act', (lambda e, k=k, tb=tb: e.activation(
                out=hT[:, k, hsl], in_=ntmp[tb][:, 0:n], func=AF.Identity, bias=mods[:, l, 3 * i * 8 + k, j:j + 1])),
                reads=[('ntmp', tb), ('mods', l)], writes=[hkeys[k]])

    def mm(self, out, pairs, reads, bank):
        pairs = list(pairs)

        def f(e):
            n = len(pairs)
            for q, (lt, r) in enumerate(pairs):
                ins = e.matmul(out, lhsT=lt, rhs=r, start=(q == 0), stop=(q == n - 1))
            return ins
        return self.S.op('pe', f, reads=reads, writes=[('ps', bank)])

    def xupdate(self, l, si, t0, n, oc, psrc, reads):
        j = 0 if t0 < TP else 1
        xT, modC = self.xT, self.modC
        sl = slice(t0, t0 + n)
        xk = [('x', oc, tt) for tt in range(t0 // TT, (t0 + n - 1) // TT + 1)]
        self.S.op('dve', lambda e: e.scalar_tensor_tensor(
            out=xT[:, oc, sl], in0=psrc, scalar=modC[:, l, si, j, oc:oc + 1], in1=xT[:, oc, sl],
            op0=ALU.mult, op1=ALU.add), reads=list(reads) + xk + [('modC', l)], writes=xk)

    G = 2

    def ffn_groups(self):
        return [(c0, min(self.G, FC - c0)) for c0 in range(0, FC, self.G)]

    def alloc_ffn(self):
        M = self.M
        G = self.G
        self.w1b = [M.alloc([128, KC, G * 128], BF16, "w1b") for _ in range(2)]
        self.w3b = [M.alloc([128, KC, G * 128], BF16, "w3b") for _ in range(2)]
        self.w2b = [M.alloc([128, G, D], BF16, "w2b") for _ in range(2)]
        self.ub = [M.alloc([128, G, TT], BF16, "ub") for _ in range(2)]
        self.slb = [M.alloc([128, TT], F32, "ntmp") for _ in range(2)]
        self.wbuf_i = 0

    def ffn_load(self, l, i, gi):
        c0, g = self.ffn_groups()[gi]
        b = self.wbuf_i % 2
        self.wbuf_i += 1
        ins = self.ins
        w1v = ins["ffn_w1"][l, i].rearrange("(k p) n -> p k n", p=128)
        w3v = ins["ffn_w3"][l, i].rearrange("(k p) n -> p k n", p=128)
        w2v = ins["ffn_w2"][l, i].rearrange("(g p) n -> p g n", p=128)
        self.dma(self.w1b[b][:, :, 0:g * 128], w1v[:, :, c0 * 128:(c0 + g) * 128], writes=[('w1', b)], q='pool')
        self.dma(self.w3b[b][:, :, 0:g * 128], w3v[:, :, c0 * 128:(c0 + g) * 128], writes=[('w3', b)], q='pool')
        self.dma(self.w2b[b][:, 0:g, :], w2v[:, c0:c0 + g, :], writes=[('w2', b)], q='pool')
        return b

    def ada_unit(self, ln, su):
        S, ps, ins, sc = self.S, self.ps, self.ins, self.sc
        wv = ins["w_ada"][ln].rearrange("(k p) n -> p k n", p=128)
        nb = len(self.aslab)
        if su == 0:
            for q in range(min(2, 36)):
                self.dma(self.aslab[q % nb][:], wv[:, :, q * 256:(q + 1) * 256], writes=[('aslab', q % nb)], q='pool')
        if su + 2 < 36:
            q = su + 2
            self.dma(self.aslab[q % nb][:], wv[:, :, q * 256:(q + 1) * 256], writes=[('aslab', q % nb)], q='pool')
        b = su % nb
        wb = self.aslab[b]
        bank = self.bank_m.next()

        def mm(e):
            for oc in range(2):
                for k in range(KC):
                    r = e.matmul(ps[:, bank, oc * 2:oc * 2 + 2], lhsT=wb[:, k, oc * 128:(oc + 1) * 128], rhs=sc[:, k, :],
                                 start=(k == 0 and oc == 0), stop=(k == KC - 1), skip_group_check=True)
            return r
        S.op('pe', mm, reads=[('aslab', b), 'sc'], writes=[('ps', bank)])
        mods, badaT = self.mods, self.badaT
        for j in range(2):
            S.op('dve', (lambda e, j=j: e.tensor_tensor(
                out=mods[:, ln, su * 2:(su + 1) * 2, j], in0=ps[:, bank, 0:4].rearrange("p (c j) -> p c j", j=2)[:, :, j],
                in1=badaT[:, ln, su * 2:(su + 1) * 2], op=ALU.add)), reads=[('ps', bank), 'bada'], writes=[('mods', ln)])

    def ffn(self, l, i, hT, first_buf, norm_i=None, ada_next=None):
        S, ps = self.S, self.ps
        groups = self.ffn_groups()
        si = 0 if i == 0 else 2
        w1b, w3b, w2b, ubuf, slb, xT, modC = self.w1b, self.w3b, self.w2b, self.ub, self.slb, self.xT, self.modC
        nxt = first_buf
        ucnt = [0]

        def U_part(b, g, tt, ub, jj):
            sl = slice(tt * TT, (tt + 1) * TT)
            ba = self.bank_a.next()
            bb = self.bank_b.next()
            hk = [('h', k, tt) for k in range(KC)]
            self.mm(ps[:, ba, :], [(w1b[b][:, k, jj * 128:(jj + 1) * 128], hT[:, k, sl]) for k in range(KC)], [('w1', b)] + hk, ba)
            self.mm(ps[:, bb, :], [(w3b[b][:, k, jj * 128:(jj + 1) * 128], hT[:, k, sl]) for k in range(KC)], [('w3', b)] + hk, bb)
            sb = self.sl_rr.next()
            S.op('act', lambda e: e.activation(out=slb[sb][:], in_=ps[:, ba, :], func=AF.Silu), reads=[('ps', ba)], writes=[('ntmp', sb)])
            S.op('dve', lambda e: e.tensor_tensor(out=ubuf[ub][:, jj, :], in0=slb[sb][:], in1=ps[:, bb, :], op=ALU.mult),
                 reads=[('ntmp', sb), ('ps', bb)], writes=[('u', ub, jj)])

        def Y_part(b, g, tt, ub, oc):
            j = 0 if tt < TP // TT else 1
            sl = slice(tt * TT, (tt + 1) * TT)
            by = self.bank_y.next()
            self.mm(ps[:, by, :], [(w2b[b][:, jj, oc * 128:(oc + 1) * 128], ubuf[ub][:, jj, :]) for jj in range(g)],
                    [('w2', b)] + [('u', ub, jj) for jj in range(g)], by)
            S.op('dve', lambda e: e.scalar_tensor_tensor(
                out=xT[:, oc, sl], in0=ps[:, by, :], scalar=modC[:, l, si, j, oc:oc + 1], in1=xT[:, oc, sl],
                op0=ALU.mult, op1=ALU.add), reads=[('ps', by), ('x', oc, tt), ('modC', l)], writes=[('x', oc, tt)])

        ada_cnt = [0]
        for gi, (c0, g) in enumerate(groups):
            b = nxt
            if gi + 1 < len(groups):
                nxt = self.ffn_load(l, i, gi + 1)
            prev = None
            for step in range(NT + 1):
                cur = None
                ub = ucnt[0] % 2
                if step < NT:
                    ucnt[0] += 1
                    cur = (step, ub)
                nparts = g if step < NT else 1
                per = (KC + nparts - 1) // nparts
                if ada_next is not None and ada_cnt[0] < 36 and (gi * (NT + 1) + step) % 2 == 0:
                    self.ada_unit(ada_next, ada_cnt[0])
                    ada_cnt[0] += 1
                if norm_i is not None and gi == 0 and step + 1 < NT:
                    self.norm_mod(l, norm_i, [step + 1], hT)
                for part in range(nparts):
                    if step < NT:
                        U_part(b, g, step, ub, part)
                    if prev is not None:
                        for oc in range(part * per, min(KC, (part + 1) * per)):
                            Y_part(b, g, prev[0], prev[1], oc)
                prev = cur
        if ada_next is not None:
            while ada_cnt[0] < 36:
                self.ada_unit(ada_next, ada_cnt[0])
                ada_cnt[0] += 1
            self.derive_mods(ada_next)

    def final(self):
        M = self.M
        M.mark()
        self.alloc_norm_scratch()
        yb = [M.alloc([128, KC, TT], F32, "yb") for _ in range(2)]
        for tt in range(NT):
            self.final_tile(tt, yb)
        M.release()

    def final_tile(self, tt, yb):
        S = self.S
        sl = slice(tt * TT, (tt + 1) * TT)
        self.rstd_tile(tt)
        b = tt % 2
        xT, fgT, rstd = self.xT, self.fgT, self.rstd
        for k in range(KC):
            S.op('dve', (lambda e, k=k: e.scalar_tensor_tensor(
                out=yb[b][:, k, :], in0=xT[:, k, sl], scalar=fgT[:, k:k + 1], in1=rstd[:],
                op0=ALU.mult, op1=ALU.mult)), reads=[('x', k, tt), 'rstd', 'fg'], writes=[('yb', b, k)])
        self.dma(self.outs["yT"][:, :, sl], yb[b][:], reads=[('yb', b, k) for k in range(KC)])

    def build(self):
        S, M = self.S, self.M
        self.declare()
        self.epsT = M.alloc([128, 1], F32, "eps")
        S.op('dve', lambda e: e.memset(self.epsT[:], EPS), writes=['eps'])
        self.prologue()
        S.barrier()
        self.dump("x0")
        self.dump_tile("mods", self.mods[:], [128, L, 72, 2], [('mods', l) for l in range(L)])
        self.dump_tile("modA", self.modA[:], [128, L, 3, 2, KC], [('modA', l) for l in range(L)])
        self.dump_tile("modC", self.modC[:], [128, L, 3, 2, KC], [('modC', l) for l in range(L)])
        for l in self.layer_list:
            for i in range(3):
                if i == 1:
                    if self.do_mixer:
                        self.mixer(l)
                    self.dump("l%d_b" % l)
                    continue
                if not self.do_ffn:
                    continue
                M.mark()
                hT = M.alloc([128, KC, T], BF16, "hT")
                self.alloc_ffn()
                self.alloc_norm_scratch(TT, share_tmp=self.slb)
                self.sl_rr = self.ntmp_rr
                li = self.layer_list.index(l)
                ada_next = self.layer_list[li + 1] if (i == 2 and li + 1 < len(self.layer_list)) else None
                if ada_next is not None:
                    self.aslab = [M.alloc([128, KC, 256], BF16, "aslab") for _ in range(3)]
                fi = 0 if i == 0 else 1
                first = self.ffn_load(l, fi, 0)
                self.norm_mod(l, i, [0], hT)
                self.ffn(l, fi, hT, first, norm_i=i, ada_next=ada_next)
                M.release()
                S.barrier()
                self.dump("l%d_%s" % (l, "a" if i == 0 else "c"))
        self.final()
        S.final_wait('sp')
        S.emit()
        return self.nc

    def mixer(self, l):
        kind = l % 4
        if kind == 2:
            self.mixer_gmlp(l)
        elif kind == 1:
            self.mixer_fourier(l)
        elif kind == 3:
            self.mixer_na(l)
        elif kind == 0:
            nm = len(self.M.marks)
            try:
                if getattr(self, "old_mlstm", False):
                    self.mixer_mlstm(l)
                else:
                    self.mixer_mlstm2(l)
            except _Stop:
                while len(self.M.marks) > nm:
                    self.M.release()
        self.S.barrier()

    def mixer_mlstm2(self, l):
        S, M, ins, ps, psb = self.S, self.M, self.ins, self.ps, self.psb
        ones_f, id_f, id_b, epsT = self.ones_f, self.id_f, self.id_b, self.epsT
        wqkv_d = ins["ml_w_qkv"].rearrange("(k p) n -> p k n", p=128)
        wog_d = ins["ml_w_og"].rearrange("(k p) n -> p k n", p=128)
        wout_d = ins["ml_w_out"].rearrange("(k p) n -> p k n", p=128)
        M.mark()
        cfm = M.alloc([128, 3, 128], F32, "mcfm")
        self.dma(cfm[:, 0:2, :], ins["cf32"][:, 2:4, :], writes=['mcfm'])
        self.dma(cfm[:, 2, :], ins["cf32"][:, 8, :], writes=['mcfm'])
        Jf = cfm[:, 2, :]
        Wif = M.alloc([128, KC, 16], BF16, "mWif")
        self.dma(Wif[:], ins["ml_w_if"].rearrange("(k p) n -> p k n", p=128), writes=['mWif'], q='pool')
        bif = M.alloc([128, 16], F32, "mbif")
        self.dma(bif[:], ins["ml_b_if"].partition_broadcast(128), writes=['mbif'])
        hgT = M.alloc([128, KC], F32, "mhg")
        self.dma(hgT[:], ins["ml_head_gT"], writes=['mhg'])
        m0b = M.alloc([128, 8], F32, "mm0")
        self.dma(m0b[:], ins["ml_m0"].partition_broadcast(128), writes=['mm0'])
        uid = [0]

        def K_(name):
            uid[0] += 1
            return (name, uid[0])

        def do_group(grp):
            M.mark()
            g0, gn = (0, TP) if grp == 0 else (TP, TS)
            NCH = gn // 128
            NCOL = NCH * 8
            nseq, cps = (NPS, 2) if grp == 0 else (1, 16)
            hT = M.alloc([128, KC, gn], BF16, "mhT")
            hk = [('mh', k, tq) for k in range(KC) for tq in range(gn // 256)]
            Uu = M.alloc([128, NCH, 8], F32, "mU")
            Wi = M.alloc([128, NCH, 8], F32, "mWi")
            Fl = M.alloc([128, NCH, 8], F32, "mFl")
            Ws = M.alloc([128, NCH, 8], F32, "mWs")
            Dc = M.alloc([128, NCH, 8], F32, "mDc")
            mfin = M.alloc([128, nseq, 8], F32, "mfin")
            M.mark()
            self.alloc_norm_scratch(256)
            for tq in range(gn // 256):
                self.norm_mod_range(l, 1, g0 + tq * 256, 256, hT, tq * 256, [('mh', k, tq) for k in range(KC)])
            S.barrier()
            M.release()
            M.mark()
            gpre = M.alloc([128, NCH, 16], F32, "mgpre")
            gi = M.alloc([128, NCH, 8], F32, "mgi")
            gl = M.alloc([128, NCH, 8], F32, "mgl")
            bb = M.alloc([128, NCH, 8], F32, "mb")
            bt = M.alloc([128, NCH, 8], F32, "mbt")
            gg = M.alloc([128, NCH, 8], F32, "mg")
            cmx = M.alloc([128, NCH, 8], F32, "mcmx")
            gmx = M.alloc([128, NCH, 8], F32, "mgmx")
            MP = M.alloc([128, NCH, 8], F32, "mMP")
            ML = M.alloc([128, NCH, 8], F32, "mML")
            Mt = M.alloc([128, NCH, 8], F32, "mMt")
            tq_ = M.alloc([128, NCH, 8], F32, "mtq")
            Rf = M.alloc([128, 128], F32, "mRf")
            Rb = M.alloc([128, 128], F32, "mRb")
            Cf = M.alloc([128, 128], F32, "mCf")
            Cb_ = M.alloc([128, 128], F32, "mCb")
            Tb = M.alloc([128, 128], F32, "mTb")
            rmax = M.alloc([128, 1], F32, "mrmax")
            dg = M.alloc([128, 128], F32, "mdg")
            for c8 in range(0, NCH, 8):
                bank = self.bank_any.next()

                def fg(e, bank=bank, c8=c8):
                    for ci in range(8):
                        ch = c8 + ci
                        for k in range(KC):
                            r = e.matmul(ps[:, bank, ci * 16:(ci + 1) * 16], lhsT=hT[:, k, ch * 128:(ch + 1) * 128], rhs=Wif[:, k, :],
                                         start=(k == 0 and ci == 0), stop=(k == KC - 1), skip_group_check=True)
                    return r
                S.op('pe', fg, reads=hk + ['mWif'], writes=[('ps', bank)])
                for ci in range(8):
                    S.op('dve', (lambda e, bank=bank, ci=ci, c8=c8: e.tensor_tensor(out=gpre[:, c8 + ci, :], in0=ps[:, bank, ci * 16:(ci + 1) * 16], in1=bif[:], op=ALU.add)),
                         reads=[('ps', bank), 'mbif'], writes=['mgpre'])
            gpv = gpre[:].rearrange("p c (d x) -> p c d x", d=2)
            v4 = lambda t: t[:].rearrange("p c (d h) -> p c d h", d=2)
            S.op('dve', lambda e: e.tensor_copy(out=v4(gi), in_=gpv[:, :, :, 0:4]), reads=['mgpre'], writes=['mgi'])
            S.op('act', lambda e: e.activation(out=v4(gl), in_=gpv[:, :, :, 4:8], func=AF.Exp, scale=-1.0), reads=['mgpre'], writes=['mgl'])
            S.op('act', lambda e: e.activation(out=gl[:], in_=gl[:], func=AF.Ln, bias=1.0), reads=['mgl'], writes=['mgl'])
            S.op('dve', lambda e: e.tensor_scalar(out=gl[:], in0=gl[:], scalar1=-1.0, scalar2=None, op0=ALU.mult), reads=['mgl'], writes=['mgl'])
            for c8 in range(0, NCH, 8):
                bank = self.bank_any.next()

                def fc(e, bank=bank, c8=c8):
                    for ci in range(8):
                        ch = c8 + ci
                        e.matmul(ps[:, bank, ci * 16:ci * 16 + 4], lhsT=cfm[:, 0, :], rhs=gl[:, ch, 0:4], start=True, stop=True)
                        e.matmul(ps[:, bank, ci * 16 + 4:ci * 16 + 8], lhsT=cfm[:, 1, :], rhs=gl[:, ch, 4:8], start=True, stop=True)
                        r = e.matmul(ps[:, bank, ci * 16 + 8:ci * 16 + 16], lhsT=ones_f, rhs=gl[:, ch, :], start=True, stop=True)
                    return r
                S.op('pe', fc, reads=['mgl', 'mcfm', 'cf'], writes=[('ps', bank)])
                pv = ps[:, bank, 0:128].rearrange("p (c x) -> p c x", x=16)
                S.op('dve', (lambda e, pv=pv, c8=c8: e.tensor_copy(out=bb[:, c8:c8 + 8, :], in_=pv[:, :, 0:8])), reads=[('ps', bank)], writes=['mb'])
                S.op('dve', (lambda e, pv=pv, c8=c8: e.tensor_copy(out=bt[:, c8:c8 + 8, :], in_=pv[:, :, 8:16])), reads=[('ps', bank)], writes=['mbt'])
            S.op('dve', lambda e: e.tensor_tensor(out=gg[:], in0=gi[:], in1=bb[:], op=ALU.subtract), reads=['mgi', 'mb'], writes=['mg'])
            ggf = gg[:].rearrange("p c k -> p (c k)")
            bankf = self.bank_any.next()
            self.mm(ps[0:NCOL, bankf, 0:128], [(ggf, id_f)], ['mg', 'cf'], bankf)
            bankb = self.bank_any.next()
            self.mm(ps[0:NCOL, bankb, 0:128], [(ggf, Jf)], ['mg', 'mcfm'], bankb)
            S.op('act', lambda e: e.activation(out=Rf[0:NCOL, :], in_=ps[0:NCOL, bankf, 0:128], func=AF.Identity), reads=[('ps', bankf)], writes=['mRf'])
            S.op('act', lambda e: e.activation(out=Rb[0:NCOL, :], in_=ps[0:NCOL, bankb, 0:128], func=AF.Identity), reads=[('ps', bankb)], writes=['mRb'])
            S.op('dve', lambda e: e.tensor_tensor_scan(out=Cf[0:NCOL, :], data0=Rf[0:NCOL, :], data1=Rf[0:NCOL, :], initial=-1e30, op0=ALU.max, op1=ALU.max),
                 reads=['mRf'], writes=['mCf'])
            S.op('dve', lambda e: e.tensor_tensor_scan(out=Cb_[0:NCOL, :], data0=Rb[0:NCOL, :], data1=Rb[0:NCOL, :], initial=-1e30, op0=ALU.max, op1=ALU.max),
                 reads=['mRb'], writes=['mCb'])
            S.op('dve', lambda e: e.tensor_reduce(out=rmax[0:NCOL, :], in_=Rf[0:NCOL, :], axis=AX.X, op=ALU.max), reads=['mRf'], writes=['mrmax'])
            bank1 = self.bank_any.next()
            self.mm(ps[:, bank1, 0:NCOL], [(Cf[0:NCOL, :], id_f[0:NCOL, 0:NCOL])], ['mCf', 'cf'], bank1)
            bank2 = self.bank_any.next()
            self.mm(ps[:, bank2, 0:NCOL], [(Cb_[0:NCOL, :], id_f[0:NCOL, 0:NCOL])], ['mCb', 'cf'], bank2)
            S.op('act', lambda e: e.activation(out=Tb[:, 0:NCOL], in_=ps[:, bank2, 0:NCOL], func=AF.Identity), reads=[('ps', bank2)], writes=['mTb'])
            bank3 = self.bank_any.next()
            self.mm(ps[:, bank3, 0:NCOL], [(Jf, Tb[:, 0:NCOL])], ['mTb', 'mcfm'], bank3)
            p1 = ps[:, bank1, 0:NCOL].rearrange("p (c d h) -> p c d h", d=2, h=4)
            p3 = ps[:, bank3, 0:NCOL].rearrange("p (c d h) -> p c d h", d=2, h=4)
            S.op('dve', lambda e: e.tensor_copy(out=v4(cmx)[:, :, 0, :], in_=p1[:, :, 0, :]), reads=[('ps', bank1)], writes=['mcmx'])
            S.op('dve', lambda e: e.tensor_copy(out=v4(cmx)[:, :, 1, :], in_=p3[:, :, 1, :]), reads=[('ps', bank3)], writes=['mcmx'])
            S.op('dve', lambda e: e.tensor_scalar(out=dg[0:NCOL, 0:NCOL], in0=id_f[0:NCOL, 0:NCOL], scalar1=rmax[0:NCOL, 0:1], scalar2=None, op0=ALU.mult),
                 reads=['mrmax', 'cf'], writes=['mdg'])
            bank4 = self.bank_any.next()
            self.mm(ps[:, bank4, 0:NCOL], [(ones_f[0:NCOL, :], dg[0:NCOL, 0:NCOL])], ['mdg', 'cf'], bank4)
            S.op('dve', lambda e: e.tensor_copy(out=gmx[:].rearrange("p c k -> p (c k)"), in_=ps[:, bank4, 0:NCOL]), reads=[('ps', bank4)], writes=['mgmx'])
            def sview(t, ci, d):
                return t[:].rearrange("p (s c) k -> p s c k", c=cps)[:, :, ci, d * 4:(d + 1) * 4]
            for d in range(2):
                order = list(range(cps)) if d == 0 else list(range(cps - 1, -1, -1))
                for oi, ci in enumerate(order):
                    if oi == 0:
                        if grp == 0:
                            S.op('dve', (lambda e, ci=ci, d=d: e.memset(sview(MP, ci, d), 0.0)), reads=[], writes=['mMP'])
                        else:
                            S.op('dve', (lambda e, ci=ci, d=d: e.tensor_copy(out=sview(MP, ci, d)[:, 0, :], in_=m0b[:, d * 4:(d + 1) * 4])), reads=['mm0'], writes=['mMP'])
                    S.op('dve', (lambda e, ci=ci, d=d: e.tensor_tensor(out=sview(ML, ci, d), in0=sview(gmx, ci, d), in1=sview(MP, ci, d), op=ALU.max)),
                         reads=['mgmx', 'mMP'], writes=['mML'])
                    if oi + 1 < len(order):
                        nx = order[oi + 1]
                        S.op('dve', (lambda e, ci=ci, d=d, nx=nx: e.tensor_tensor(out=sview(MP, nx, d), in0=sview(ML, ci, d), in1=sview(bt, ci, d), op=ALU.add)),
                             reads=['mML', 'mbt'], writes=['mMP'])
                    else:
                        S.op('dve', (lambda e, ci=ci, d=d: e.tensor_tensor(out=mfin[:, :, d * 4:(d + 1) * 4], in0=sview(ML, ci, d), in1=sview(bt, ci, d), op=ALU.add)),
                             reads=['mML', 'mbt'], writes=['mfin'])
            S.op('dve', lambda e: e.tensor_tensor(out=Mt[:], in0=cmx[:], in1=MP[:], op=ALU.max), reads=['mcmx', 'mMP'], writes=['mMt'])
            for dst, a_, b_, kk in ((Uu, ML, Mt, 'mU'), (Wi, MP, Mt, 'mWi'), (Ws, gg, ML, 'mWs'), (Dc, MP, ML, 'mDc')):
                S.op('dve', (lambda e, a_=a_, b_=b_: e.tensor_tensor(out=tq_[:], in0=a_[:], in1=b_[:], op=ALU.subtract)),
                     reads=['mML', 'mMt', 'mMP', 'mg'], writes=['mtq'])
                S.op('act', (lambda e, dst=dst: e.activation(out=dst[:], in_=tq_[:], func=AF.Exp)), reads=['mtq'], writes=[kk])
            S.op('dve', lambda e: e.tensor_tensor(out=tq_[:], in0=bb[:], in1=Mt[:], op=ALU.add), reads=['mb', 'mMt'], writes=['mtq'])
            S.op('act', lambda e: e.activation(out=Fl[:], in_=tq_[:], func=AF.Exp, scale=-1.0), reads=['mtq'], writes=['mFl'])
            if grp == 0:
                for seq in range(NPS):
                    self.dma(self.outs["m_out"][seq:seq + 1, :], mfin[0:1, seq, :], reads=['mfin'])
            S.barrier()
            M.release()
            scal_keys = ['mU', 'mWi', 'mFl', 'mWs', 'mDc']

            def do_head(hd):
                M.mark()
                wrot = [M.alloc([128, KC, 256], BF16, "mwrot")]
                Wo = M.alloc([128, 2, D], BF16, "mWo")
                qT = M.alloc([128, 2, gn], BF16, "mqT")
                kT = M.alloc([128, 2, gn], BF16, "mkT")
                ktok = M.alloc([128, NCH, 256], BF16, "mktok")
                vaug = M.alloc([128, NCH, 257], BF16, "mvaug")
                hacc = M.alloc([128, NCH, 256], BF16, "mhacc")
                chains = [(seq, d) for seq in range(nseq) for d in range(2)]
                nchn = len(chains)
                Cst = [[M.alloc([128, 257], F32, "mCst") for _ in range(2)] for _ in chains]
                Cbf = [[M.alloc([128, 257], BF16, "mCbf") for _ in range(2)] for _ in chains]
                NB = 4
                aT = [M.alloc([128, 128], BF16, "maT") for _ in range(NB)]
                num = [M.alloc([128, 257], F32, "mnum") for _ in range(NB)]
                isb = [M.alloc([128, 257], F32, "misb") for _ in range(NB)]
                kw = [M.alloc([128, 256], BF16, "mkw") for _ in range(NB)]
                sm = [M.alloc([128, 4], F32, "msm") for _ in range(NB)]
                ssq = M.alloc([128, NCH], F32, "mssq")
                junk = M.alloc([128, 256], BF16, "mjunk")
                ogT = qT
                gT = kT
                self.dma(Wo[:], wout_d[:, 2 * hd:2 * hd + 2, :], writes=['mWo'], q='pool')
                S.op('dve', lambda e: e.memset(vaug[:, :, 256:257], 1.0), writes=[K_('mv1')])

                def loadw(src):
                    self.dma(wrot[0][:], src, writes=[('mwrot', 0)], q='pool')
                    return 0
                for q_, dst, sc_ in ((0, qT, 1.0), (1, kT, 0.0625)):
                    b = loadw(wqkv_d[:, :, q_ * D + hd * 256:q_ * D + (hd + 1) * 256])
                    for kc2 in range(2):
                        for t0 in range(0, gn, 512):
                            bank = self.bank_any.next()
                            self.mm(ps[:, bank, :], [(wrot[b][:, k, kc2 * 128:(kc2 + 1) * 128], hT[:, k, t0:t0 + 512]) for k in range(KC)],
                                    [('mwrot', b)] + hk, bank)
                            S.op('act', (lambda e, dst=dst, bank=bank, kc2=kc2, t0=t0, sc_=sc_: e.activation(
                                out=dst[:, kc2, t0:t0 + 512], in_=ps[:, bank, :], func=AF.Identity, scale=sc_)),
                                reads=[('ps', bank)], writes=[('mqk', q_)])
                for q_, dst, sc_ in ((1, ktok, 0.0625), (2, vaug, 1.0)):
                    b = loadw(wqkv_d[:, :, q_ * D + hd * 256:q_ * D + (hd + 1) * 256])
                    for ch in range(0, NCH, 2):
                        bank = self.bank_any.next()

                        def ft(e, bank=bank, ch=ch, b=b):
                            for ci in range(2):
                                for k in range(KC):
                                    r = e.matmul(ps[:, bank, ci * 256:(ci + 1) * 256], lhsT=hT[:, k, (ch + ci) * 128:(ch + ci + 1) * 128],
                                                 rhs=wrot[b][:, k, :], start=(k == 0 and ci == 0), stop=(k == KC - 1), skip_group_check=True)
                            return r
                        S.op('pe', ft, reads=[('mwrot', b)] + hk, writes=[('ps', bank)])
                        S.op('act', (lambda e, dst=dst, bank=bank, ch=ch, sc_=sc_: e.activation(
                            out=dst[:, ch:ch + 2, 0:256], in_=ps[:, bank, :].rearrange("p (a b) -> p a b", b=256), func=AF.Identity, scale=sc_)),
                            reads=[('ps', bank)], writes=[('mtok', q_)])
                qkk = [('mqk', 0), ('mqk', 1)]
                tokk = [('mtok', 1), ('mtok', 2)]
                for ci_, (seq, d) in enumerate(chains):
                    for kc2 in range(2):
                        if grp == 0:
                            S.op('dve', (lambda e, t=Cst[ci_][kc2]: e.memset(t[:], 0.0)), writes=[('mCst', ci_, kc2)])
                            S.op('dve', (lambda e, t=Cbf[ci_][kc2]: e.memset(t[:], 0.0)), writes=[('mCbf', ci_)])
                        else:
                            self.dma(Cst[ci_][kc2][:], ins["ml_caug0"][d, hd, kc2 * 128:(kc2 + 1) * 128, :], writes=[('mCst', ci_, kc2)])
                            S.op('act', (lambda e, t=Cbf[ci_][kc2], s_=Cst[ci_][kc2]: e.activation(out=t[:], in_=s_[:], func=AF.Identity)),
                                 reads=[('mCst', ci_, kc2)], writes=[('mCbf', ci_)])
                hwritten = set()
                slot = [0]

                def stage1(ci_, step, lane):
                    seq, d = chains[ci_]
                    ci = step if d == 0 else cps - 1 - step
                    ch = seq * cps + ci
                    col = d * 4 + hd
                    sb = slot[0] % NB
                    slot[0] += 1
                    tsl = slice(ch * 128, (ch + 1) * 128)
                    bankA = 6 + lane
                    self.mm(ps[:, bankA, 0:128], [(kT[:, kc2, tsl], qT[:, kc2, tsl]) for kc2 in range(2)], qkk, bankA)
                    S.op('dve', lambda e: e.scalar_tensor_tensor(out=aT[sb][:], in0=ps[:, bankA, 0:128], scalar=Ws[:, ch, col:col + 1], in1=cfm[:, d, :],
                                                                 op0=ALU.mult, op1=ALU.mult),
                         reads=[('ps', bankA), 'mWs', 'mcfm'], writes=[('maT', sb)])
                    S.op('dve', lambda e: e.tensor_scalar(out=kw[sb][:], in0=ktok[:, ch, :], scalar1=Ws[:, ch, col:col + 1], scalar2=None, op0=ALU.mult),
                         reads=tokk + ['mWs'], writes=[('mkw', sb)])
                    return (ci_, ch, col, d, sb, tsl, lane)

                def stage2(st):
                    ci_, ch, col, d, sb, tsl, lane = st
                    b1, b2, b3 = 3 * lane, 3 * lane + 1, 3 * lane + 2

                    def f1(e):
                        e.matmul(ps[:, b1, 0:257], lhsT=aT[sb][:], rhs=vaug[:, ch, :], start=True, stop=True)
                        for kc2 in range(2):
                            r = e.matmul(ps[:, b1, 257 + kc2:258 + kc2], lhsT=kw[sb][:, kc2 * 128:(kc2 + 1) * 128], rhs=vaug[:, ch, 256:257], start=True, stop=True)
                        return r
                    S.op('pe', f1, reads=[('maT', sb), ('mkw', sb)] + tokk + [K_('x')], writes=[('ps', b1)])
                    self.mm(ps[:, b2, 0:257], [(qT[:, kc2, tsl], Cbf[ci_][kc2][:]) for kc2 in range(2)], qkk + [('mCbf', ci_)], b2)

                    def f3(e):
                        for kc2 in range(2):
                            r = e.matmul(ps[:, b3, kc2 * 256:(kc2 + 1) * 256], lhsT=kw[sb][:, kc2 * 128:(kc2 + 1) * 128], rhs=vaug[:, ch, 0:256], start=True, stop=True)
                        return r
                    S.op('pe', f3, reads=[('mkw', sb)] + tokk, writes=[('ps', b3)])
                    return st

                def stage3(st):
                    ci_, ch, col, d, sb, tsl, lane = st
                    b1, b2, b3 = 3 * lane, 3 * lane + 1, 3 * lane + 2
                    for kc2 in range(2):
                        S.op('dve', (lambda e, kc2=kc2: e.scalar_tensor_tensor(out=Cst[ci_][kc2][:, 0:256], in0=Cst[ci_][kc2][:, 0:256], scalar=Dc[:, ch, col:col + 1],
                                                                               in1=ps[:, b3, kc2 * 256:(kc2 + 1) * 256], op0=ALU.mult, op1=ALU.add)),
                             reads=[('ps', b3), 'mDc', ('mCst', ci_, kc2), ('mCbf', ci_)], writes=[('mCst', ci_, kc2)])
                        S.op('dve', (lambda e, kc2=kc2: e.scalar_tensor_tensor(out=Cst[ci_][kc2][:, 256:257], in0=Cst[ci_][kc2][:, 256:257], scalar=Dc[:, ch, col:col + 1],
                                                                               in1=ps[:, b1, 257 + kc2:258 + kc2], op0=ALU.mult, op1=ALU.add)),
                             reads=[('ps', b1), 'mDc', ('mCst', ci_, kc2)], writes=[('mCst', ci_, kc2)])
                    S.op('act', lambda e: e.activation(out=isb[sb][:], in_=ps[:, b2, 0:257], func=AF.Identity, scale=Wi[:, ch, col:col + 1]),
                         reads=[('ps', b2), 'mWi'], writes=[('misb', sb)])
                    for kc2 in range(2):
                        S.op('act', (lambda e, kc2=kc2: e.activation(out=Cbf[ci_][kc2][:], in_=Cst[ci_][kc2][:], func=AF.Identity)),
                             reads=[('mCst', ci_, kc2)], writes=[('mCbf', ci_)])
                    S.op('dve', lambda e: e.scalar_tensor_tensor(out=num[sb][:], in0=ps[:, b1, 0:257], scalar=Uu[:, ch, col:col + 1], in1=isb[sb][:],
                                                                 op0=ALU.mult, op1=ALU.add),
                         reads=[('ps', b1), 'mU', ('misb', sb)], writes=[('mnum', sb)])
                    S.op('dve', lambda e: e.scalar_tensor_tensor(out=sm[sb][:, 0:1], in0=num[sb][:, 256:257], scalar=-1.0, in1=num[sb][:, 256:257], op0=ALU.mult, op1=ALU.max),
                         reads=[('mnum', sb)], writes=[('msm', sb)])
                    S.op('dve', lambda e: e.tensor_tensor(out=sm[sb][:, 0:1], in0=sm[sb][:, 0:1], in1=Fl[:, ch, col:col + 1], op=ALU.max),
                         reads=[('msm', sb), 'mFl'], writes=[('msm', sb)])
                    S.op('dve', lambda e: e.reciprocal(out=sm[sb][:, 0:1], in_=sm[sb][:, 0:1]), reads=[('msm', sb)], writes=[('msm', sb)])
                    if ch not in hwritten:
                        hwritten.add(ch)
                        S.op('dve', lambda e: e.tensor_scalar(out=hacc[:, ch, :], in0=num[sb][:, 0:256], scalar1=sm[sb][:, 0:1], scalar2=None, op0=ALU.mult),
                             reads=[('mnum', sb), ('msm', sb)], writes=[('mhacc', ch)])
                    else:
                        S.op('dve', lambda e: e.scalar_tensor_tensor(out=hacc[:, ch, :], in0=num[sb][:, 0:256], scalar=sm[sb][:, 0:1], in1=hacc[:, ch, :],
                                                                     op0=ALU.mult, op1=ALU.add),
                             reads=[('mnum', sb), ('msm', sb), ('mhacc', ch)], writes=[('mhacc', ch)])

                items = [(ci_, step) for step in range(cps) for ci_ in range(nchn)]
                st1, st2 = {}, {}
                for idx in range(len(items) + 2):
                    if idx - 2 >= 0:
                        stage3(st2[idx - 2])
                    if 0 <= idx - 1 < len(items):
                        st2[idx - 1] = stage2(st1[idx - 1])
                    if idx < len(items):
                        st1[idx] = stage1(items[idx][0], items[idx][1], idx % 2)
                if grp == 0:
                    for ci_, (seq, d) in enumerate(chains):
                        for kc2 in range(2):
                            self.dma(self.outs["caug_out"][seq, d, hd, kc2 * 128:(kc2 + 1) * 128, :], Cst[ci_][kc2][:], reads=[('mCst', ci_, kc2)])
                S.barrier()
                for ch in range(NCH):
                    S.op('dve', (lambda e, ch=ch: e.scalar_tensor_tensor(out=junk[:], in0=hacc[:, ch, :], scalar=1.0, in1=hacc[:, ch, :], op0=ALU.mult, op1=ALU.mult,
                                                                         accum_out=ssq[:, ch:ch + 1])),
                         reads=[('mhacc', ch)], writes=['mjunk', ('mssq', ch)])
                S.op('act', lambda e: e.activation(out=ssq[:], in_=ssq[:], func=AF.Sqrt, scale=1.0 / 256.0, bias=epsT[:, 0:1]),
                     reads=[('mssq', ch) for ch in range(NCH)] + ['eps'], writes=['mssqr'])
                S.op('dve', lambda e: e.reciprocal(out=ssq[:], in_=ssq[:]), reads=['mssqr'], writes=['mssqr'])
                for ch in range(NCH):
                    S.op('dve', (lambda e, ch=ch: e.tensor_scalar(out=hacc[:, ch, :], in0=hacc[:, ch, :], scalar1=ssq[:, ch:ch + 1], scalar2=None, op0=ALU.mult)),
                         reads=[('mhacc', ch), 'mssqr'], writes=[('mhacc', ch)])
                b = loadw(wog_d[:, :, hd * 256:(hd + 1) * 256])
                for kc2 in range(2):
                    for t0 in range(0, gn, 512):
                        bank = self.bank_any.next()
                        self.mm(ps[:, bank, :], [(wrot[b][:, k, kc2 * 128:(kc2 + 1) * 128], hT[:, k, t0:t0 + 512]) for k in range(KC)],
                                [('mwrot', b)] + hk, bank)
                        S.op('act', (lambda e, bank=bank, kc2=kc2, t0=t0: e.activation(out=ogT[:, kc2, t0:t0 + 512], in_=ps[:, bank, :], func=AF.Sigmoid)),
                             reads=[('ps', bank)], writes=[('mog', kc2, t0)])
                for ch in range(NCH):
                    bank = self.bank_any.next()

                    def ftr(e, bank=bank, ch=ch):
                        for kc2 in range(2):
                            r = e.transpose(psb[:, bank, kc2 * 128:(kc2 + 1) * 128], hacc[:, ch, kc2 * 128:(kc2 + 1) * 128], id_b)
                        return r
                    S.op('pe', ftr, reads=[('mhacc', ch), 'cb'], writes=[('ps', bank)])
                    for kc2 in range(2):
                        S.op('dve', (lambda e, bank=bank, ch=ch, kc2=kc2: e.scalar_tensor_tensor(
                            out=gT[:, kc2, ch * 128:(ch + 1) * 128], in0=psb[:, bank, kc2 * 128:(kc2 + 1) * 128], scalar=hgT[:, 2 * hd + kc2:2 * hd + kc2 + 1],
                            in1=ogT[:, kc2, ch * 128:(ch + 1) * 128], op0=ALU.mult, op1=ALU.mult)),
                            reads=[('ps', bank), 'mhg', ('mog', kc2, (ch * 128) // 512 * 512)], writes=[('mgT', ch // 4)])
                for t0 in range(0, gn, 512):
                    for oc in range(KC):
                        bank = self.bank_any.next()
                        self.mm(ps[:, bank, :], [(Wo[:, kc2, oc * 128:(oc + 1) * 128], gT[:, kc2, t0:t0 + 512]) for kc2 in range(2)],
                                ['mWo', ('mgT', t0 // 512)], bank)
                        self.xupdate(l, 1, g0 + t0, 512, oc, ps[:, bank, :], [('ps', bank)])
                S.barrier()
                M.release()

            for hd in range(4):
                do_head(hd)
            S.barrier()
            M.release()

        do_group(0)
        do_group(1)
        M.release()

    def mixer_mlstm(self, l):
        S, M, ins, ps, psb = self.S, self.M, self.ins, self.ps, self.psb
        ones_f, id_f, id_b, epsT = self.ones_f, self.id_f, self.id_b, self.epsT
        wqkv_d = ins["ml_w_qkv"].rearrange("(k p) n -> p k n", p=128)
        wog_d = ins["ml_w_og"].rearrange("(k p) n -> p k n", p=128)
        wout_d = ins["ml_w_out"].rearrange("(k p) n -> p k n", p=128)
        M.mark()
        cfm = M.alloc([128, 6, 128], F32, "mcfm")
        self.dma(cfm[:], ins["cf32"][:, 2:8, :], writes=['mcfm'])
        Wif = M.alloc([128, KC, 16], BF16, "mWif")
        self.dma(Wif[:], ins["ml_w_if"].rearrange("(k p) n -> p k n", p=128), writes=['mWif'], q='pool')
        bif = M.alloc([128, 16], F32, "mbif")
        self.dma(bif[:], ins["ml_b_if"].partition_broadcast(128), writes=['mbif'])
        hgT = M.alloc([128, KC], F32, "mhg")
        self.dma(hgT[:], ins["ml_head_gT"], writes=['mhg'])
        m0b = M.alloc([128, 8], F32, "mm0")
        self.dma(m0b[:], ins["ml_m0"].partition_broadcast(128), writes=['mm0'])
        zero1 = M.alloc([128, 1], F32, "mzero")
        S.op('dve', lambda e: e.memset(zero1[:], 0.0), writes=['mzero'])
        uid = [0]

        def K_(name):
            uid[0] += 1
            return (name, uid[0])

        def do_group(grp):
            M.mark()
            g0, gn = (0, TP) if grp == 0 else (TP, TS)
            NCH = gn // 128
            nseq, cps = (NPS, 2) if grp == 0 else (1, 16)
            hT = M.alloc([128, KC, gn], BF16, "mhT")
            hk = [('mh', k, tq) for k in range(KC) for tq in range(gn // 256)]
            gi = M.alloc([128, NCH, 8], F32, "mgi")
            gl = M.alloc([128, NCH, 8], F32, "mgl")
            bb = M.alloc([128, NCH, 8], F32, "mb")
            bt = M.alloc([128, NCH, 8], F32, "mbt")
            gg = M.alloc([128, NCH, 8], F32, "mg")
            M.mark()
            self.alloc_norm_scratch(256)
            for tq in range(gn // 256):
                self.norm_mod_range(l, 1, g0 + tq * 256, 256, hT, tq * 256, [('mh', k, tq) for k in range(KC)])
            S.barrier()
            M.release()
            gpre = M.alloc([128, NCH, 16], F32, "mgpre")
            for c8 in range(0, NCH, 8):
                bank = self.bank_any.next()

                def fg(e, bank=bank, c8=c8):
                    for ci in range(8):
                        ch = c8 + ci
                        for k in range(KC):
                            r = e.matmul(ps[:, bank, ci * 16:(ci + 1) * 16], lhsT=hT[:, k, ch * 128:(ch + 1) * 128], rhs=Wif[:, k, :],
                                         start=(k == 0 and ci == 0), stop=(k == KC - 1), skip_group_check=True)
                    return r
                S.op('pe', fg, reads=hk + ['mWif'], writes=[('ps', bank)])
                for ci in range(8):
                    S.op('dve', (lambda e, bank=bank, ci=ci, c8=c8: e.tensor_tensor(out=gpre[:, c8 + ci, :], in0=ps[:, bank, ci * 16:(ci + 1) * 16], in1=bif[:], op=ALU.add)),
                         reads=[('ps', bank), 'mbif'], writes=['mgpre'])
            gpv = gpre[:].rearrange("p c (d x) -> p c d x", d=2)
            S.op('dve', lambda e: e.tensor_copy(out=gi[:].rearrange("p c (d h) -> p c d h", d=2), in_=gpv[:, :, :, 0:4]), reads=['mgpre'], writes=['mgi'])
            glv = gl[:].rearrange("p c (d h) -> p c d h", d=2)
            S.op('act', lambda e: e.activation(out=glv, in_=gpv[:, :, :, 4:8], func=AF.Exp, scale=-1.0), reads=['mgpre'], writes=['mgl'])
            S.op('act', lambda e: e.activation(out=gl[:], in_=gl[:], func=AF.Ln, bias=1.0), reads=['mgl'], writes=['mgl'])
            S.op('dve', lambda e: e.tensor_scalar(out=gl[:], in0=gl[:], scalar1=-1.0, scalar2=None, op0=ALU.mult), reads=['mgl'], writes=['mgl'])
            for c8 in range(0, NCH, 8):
                bank = self.bank_any.next()

                def fc(e, bank=bank, c8=c8):
                    for ci in range(8):
                        ch = c8 + ci
                        e.matmul(ps[:, bank, ci * 16:ci * 16 + 4], lhsT=cfm[:, 0, :], rhs=gl[:, ch, 0:4], start=True, stop=True)
                        e.matmul(ps[:, bank, ci * 16 + 4:ci * 16 + 8], lhsT=cfm[:, 1, :], rhs=gl[:, ch, 4:8], start=True, stop=True)
                        r = e.matmul(ps[:, bank, ci * 16 + 8:ci * 16 + 16], lhsT=ones_f, rhs=gl[:, ch, :], start=True, stop=True)
                    return r
                S.op('pe', fc, reads=['mgl', 'mcfm', 'cf'], writes=[('ps', bank)])
                pv = ps[:, bank, 0:128].rearrange("p (c x) -> p c x", x=16)
                S.op('dve', (lambda e, pv=pv, c8=c8: e.tensor_copy(out=bb[:, c8:c8 + 8, :], in_=pv[:, :, 0:8])), reads=[('ps', bank)], writes=['mb'])
                S.op('dve', (lambda e, pv=pv, c8=c8: e.tensor_copy(out=bt[:, c8:c8 + 8, :], in_=pv[:, :, 8:16])), reads=[('ps', bank)], writes=['mbt'])
            S.op('dve', lambda e: e.tensor_tensor(out=gg[:], in0=gi[:], in1=bb[:], op=ALU.subtract), reads=['mgi', 'mb'], writes=['mg'])
            stg = getattr(self, "dbg_mls", 99)
            if grp == 0:
                self.dump_tile("mWif", Wif[:], [128, KC, 16], ['mWif'], BF16)
                self.dump_tile("mhT", hT[:], [128, KC, gn], hk, BF16)
                self.dump_tile("mgpre", gpre[:], [128, NCH, 16], ['mgpre'])
                for nm_, tl_, ky_ in (("mgi", gi, 'mgi'), ("mgl", gl, 'mgl'), ("mbb", bb, 'mb'), ("mbt", bt, 'mbt'), ("mgg", gg, 'mg')):
                    self.dump_tile(nm_, tl_[:], [128, NCH, 8], [ky_])
            if stg <= 1:
                raise _Stop()

            def do_head(hd):
                M.mark()
                wrot = [M.alloc([128, KC, 256], BF16, "mwrot") for _ in range(2)]
                Wo = M.alloc([128, 2, D], BF16, "mWo")
                qT = M.alloc([128, 2, gn], BF16, "mqT")
                kT = M.alloc([128, 2, gn], BF16, "mkT")
                ktok = M.alloc([128, NCH, 256], BF16, "mktok")
                vaug = M.alloc([128, NCH, 257], BF16, "mvaug")
                hacc = M.alloc([128, NCH, 256], BF16, "mhacc")
                Cst = [M.alloc([128, 257], F32, "mCst") for _ in range(2)]
                Cbf = [M.alloc([128, 257], BF16, "mCbf") for _ in range(2)]
                mpv = [M.alloc([128, 1], F32, "mmp") for _ in range(2)]
                sm = M.alloc([128, 16], F32, "msm")
                t128 = [M.alloc([128, 128], F32, "mt128") for _ in range(4)]
                aT = M.alloc([128, 128], BF16, "maT")
                num = M.alloc([128, 257], F32, "mnum")
                isb = M.alloc([128, 257], F32, "misb")
                kw = M.alloc([128, 256], BF16, "mkw")
                hs = M.alloc([128, 256], F32, "mhs")
                hn = M.alloc([128, 256], BF16, "mhn")
                junk = M.alloc([128, 256], BF16, "mjunk")
                ogT = qT
                gT = kT
                self.dma(Wo[:], wout_d[:, 2 * hd:2 * hd + 2, :], writes=['mWo'], q='pool')
                S.op('dve', lambda e: e.memset(vaug[:, :, 256:257], 1.0), writes=[K_('mv1')])
                wi = [0]

                def loadw(src):
                    b = wi[0] % 2
                    wi[0] += 1
                    self.dma(wrot[b][:], src, writes=[('mwrot', b)], q='pool')
                    return b
                for q_, dst, sc_ in ((0, qT, 1.0), (1, kT, 0.0625)):
                    b = loadw(wqkv_d[:, :, q_ * D + hd * 256:q_ * D + (hd + 1) * 256])
                    for kc2 in range(2):
                        for t0 in range(0, gn, 512):
                            bank = self.bank_any.next()
                            self.mm(ps[:, bank, :], [(wrot[b][:, k, kc2 * 128:(kc2 + 1) * 128], hT[:, k, t0:t0 + 512]) for k in range(KC)],
                                    [('mwrot', b)] + hk, bank)
                            S.op('act', (lambda e, dst=dst, bank=bank, kc2=kc2, t0=t0, sc_=sc_: e.activation(
                                out=dst[:, kc2, t0:t0 + 512], in_=ps[:, bank, :], func=AF.Identity, scale=sc_)),
                                reads=[('ps', bank)], writes=[('mqk', q_)])
                for q_, dst, sc_ in ((1, ktok, 0.0625), (2, vaug, 1.0)):
                    b = loadw(wqkv_d[:, :, q_ * D + hd * 256:q_ * D + (hd + 1) * 256])
                    for ch in range(0, NCH, 2):
                        bank = self.bank_any.next()

                        def ft(e, bank=bank, ch=ch, b=b):
                            for ci in range(2):
                                for k in range(KC):
                                    r = e.matmul(ps[:, bank, ci * 256:(ci + 1) * 256], lhsT=hT[:, k, (ch + ci) * 128:(ch + ci + 1) * 128],
                                                 rhs=wrot[b][:, k, :], start=(k == 0 and ci == 0), stop=(k == KC - 1), skip_group_check=True)
                            return r
                        S.op('pe', ft, reads=[('mwrot', b)] + hk, writes=[('ps', bank)])
                        S.op('act', (lambda e, dst=dst, bank=bank, ch=ch, sc_=sc_: e.activation(
                            out=dst[:, ch:ch + 2, 0:256], in_=ps[:, bank, :].rearrange("p (a b) -> p a b", b=256), func=AF.Identity, scale=sc_)),
                            reads=[('ps', bank)], writes=[('mtok', q_)])
                qkk = [('mqk', 0), ('mqk', 1)]
                tokk = [('mtok', 1), ('mtok', 2)]
                if stg <= 2:
                    raise _Stop()
                icnt = [0]

                def inst(seq, d, ch, first, mprev, mnew):
                    col = d * 4 + hd
                    cg = gg[:, ch, col:col + 1]
                    cb_ = bb[:, ch, col:col + 1]
                    cbt = bt[:, ch, col:col + 1]
                    tsl = slice(ch * 128, (ch + 1) * 128)
                    kA = K_('A')
                    bankA = self.bank_any.next()
                    self.mm(ps[:, bankA, 0:128], [(kT[:, kc2, tsl], qT[:, kc2, tsl]) for kc2 in range(2)], qkk, bankA)
                    S.op('dve', lambda e: e.tensor_scalar(out=t128[0][:], in0=id_f, scalar1=cg, scalar2=None, op0=ALU.mult),
                         reads=['mg', 'cf'], writes=[('mt128', 0)])
                    bankB = self.bank_any.next()
                    self.mm(ps[:, bankB, 0:128], [(ones_f, t128[0][:])], [('mt128', 0), 'cf'], bankB)
                    S.op('dve', lambda e: e.tensor_tensor(out=t128[1][:], in0=ps[:, bankB, 0:128], in1=cfm[:, 2 + d, :], op=ALU.add),
                         reads=[('ps', bankB), 'mcfm'], writes=[('mt128', 1)])
                    S.op('dve', lambda e: e.tensor_reduce(out=sm[:, 0:1], in_=t128[1][:], axis=AX.X, op=ALU.max), reads=[('mt128', 1)], writes=[('msm', 0)])
                    S.op('dve', lambda e: e.tensor_reduce(out=sm[:, 1:2], in_=ps[:, bankB, 0:128], axis=AX.X, op=ALU.max), reads=[('ps', bankB)], writes=[('msm', 1)])
                    S.op('dve', lambda e: e.tensor_tensor(out=sm[:, 2:3], in0=sm[:, 0:1], in1=mprev, op=ALU.max), reads=[('msm', 0), 'mmp'], writes=[('msm', 2)])
                    S.op('dve', lambda e: e.tensor_tensor(out=sm[:, 3:4], in0=sm[:, 1:2], in1=mprev, op=ALU.max), reads=[('msm', 1), 'mmp'], writes=[('msm', 3)])
                    S.op('dve', lambda e: e.tensor_scalar(out=t128[2][:], in0=id_f, scalar1=sm[:, 2:3], scalar2=None, op0=ALU.mult),
                         reads=[('msm', 2), 'cf'], writes=[('mt128', 2)])
                    bankC = self.bank_any.next()
                    self.mm(ps[:, bankC, 0:128], [(ones_f, t128[2][:]), (id_f, cfm[:, 4 + d, :])], [('mt128', 2), 'cf', 'mcfm'], bankC)
                    S.op('act', lambda e: e.activation(out=t128[3][:], in_=ps[:, bankC, 0:128], func=AF.Exp, scale=-1.0, bias=cg),
                         reads=[('ps', bankC), 'mg'], writes=[('mt128', 3)])
                    S.op('dve', lambda e: e.tensor_tensor(out=aT[:], in0=t128[3][:], in1=ps[:, bankA, 0:128], op=ALU.mult),
                         reads=[('mt128', 3), ('ps', bankA)], writes=['maT'])
                    bankD = self.bank_any.next()
                    self.mm(ps[:, bankD, 0:257], [(aT[:], vaug[:, ch, :])], ['maT'] + tokk + [K_('x')], bankD)
                    bankE = self.bank_any.next()
                    self.mm(ps[:, bankE, 0:257], [(qT[:, kc2, tsl], Cbf[kc2][:]) for kc2 in range(2)], qkk + ['mCbf'], bankE)
                    S.op('act', lambda e: e.activation(out=sm[:, 4:5], in_=sm[:, 2:3], func=AF.Exp, scale=-1.0, bias=mprev),
                         reads=[('msm', 2), 'mmp'], writes=[('msm', 4)])
                    S.op('act', lambda e: e.activation(out=isb[:], in_=ps[:, bankE, 0:257], func=AF.Identity, scale=sm[:, 4:5]),
                         reads=[('ps', bankE), ('msm', 4)], writes=['misb'])
                    S.op('dve', lambda e: e.tensor_tensor(out=num[:], in0=isb[:], in1=ps[:, bankD, 0:257], op=ALU.add),
                         reads=['misb', ('ps', bankD)], writes=['mnum'])
                    S.op('dve', lambda e: e.tensor_tensor(out=sm[:, 5:6], in0=sm[:, 2:3], in1=cb_, op=ALU.add), reads=[('msm', 2), 'mb'], writes=[('msm', 5)])
                    S.op('act', lambda e: e.activation(out=sm[:, 5:6], in_=sm[:, 5:6], func=AF.Exp, scale=-1.0), reads=[('msm', 5)], writes=[('msm', 5)])
                    S.op('dve', lambda e: e.scalar_tensor_tensor(out=sm[:, 6:7], in0=num[:, 256:257], scalar=-1.0, in1=num[:, 256:257], op0=ALU.mult, op1=ALU.max),
                         reads=['mnum'], writes=[('msm', 6)])
                    S.op('dve', lambda e: e.tensor_tensor(out=sm[:, 6:7], in0=sm[:, 6:7], in1=sm[:, 5:6], op=ALU.max),
                         reads=[('msm', 6), ('msm', 5)], writes=[('msm', 6)])
                    S.op('dve', lambda e: e.reciprocal(out=sm[:, 6:7], in_=sm[:, 6:7]), reads=[('msm', 6)], writes=[('msm', 6)])
                    if d == 0:
                        S.op('dve', lambda e: e.tensor_scalar(out=hacc[:, ch, :], in0=num[:, 0:256], scalar1=sm[:, 6:7], scalar2=None, op0=ALU.mult),
                             reads=['mnum', ('msm', 6)], writes=[('mhacc', ch)])
                    else:
                        S.op('dve', lambda e: e.scalar_tensor_tensor(out=hs[:], in0=num[:, 0:256], scalar=sm[:, 6:7], in1=hacc[:, ch, :], op0=ALU.mult, op1=ALU.add),
                             reads=['mnum', ('msm', 6), ('mhacc', ch)], writes=['mhs'])
                        S.op('dve', lambda e: e.scalar_tensor_tensor(out=junk[:], in0=hs[:], scalar=1.0, in1=hs[:], op0=ALU.mult, op1=ALU.mult, accum_out=sm[:, 7:8]),
                             reads=['mhs'], writes=['mjunk', ('msm', 7)])
                        S.op('act', lambda e: e.activation(out=sm[:, 7:8], in_=sm[:, 7:8], func=AF.Sqrt, scale=1.0 / 256.0, bias=epsT[:, 0:1]),
                             reads=[('msm', 7), 'eps'], writes=[('msm', 7)])
                        S.op('dve', lambda e: e.reciprocal(out=sm[:, 7:8], in_=sm[:, 7:8]), reads=[('msm', 7)], writes=[('msm', 7)])
                        S.op('dve', lambda e: e.tensor_scalar(out=hacc[:, ch, :], in0=hs[:], scalar1=sm[:, 7:8], scalar2=None, op0=ALU.mult),
                             reads=['mhs', ('msm', 7)], writes=[('mhacc', ch)])
                    S.op('dve', lambda e: e.tensor_scalar(out=sm[:, 8:9], in0=sm[:, 3:4], scalar1=-1.0, scalar2=None, op0=ALU.mult), reads=[('msm', 3)], writes=[('msm', 8)])
                    S.op('act', lambda e: e.activation(out=sm[:, 9:10], in_=cg, func=AF.Exp, bias=sm[:, 8:9]), reads=['mg', ('msm', 8)], writes=[('msm', 9)])
                    S.op('act', lambda e: e.activation(out=sm[:, 10:11], in_=mprev, func=AF.Exp, bias=sm[:, 8:9]), reads=['mmp', ('msm', 8)], writes=[('msm', 10)])
                    S.op('dve', lambda e: e.tensor_scalar(out=kw[:], in0=ktok[:, ch, :], scalar1=sm[:, 9:10], scalar2=None, op0=ALU.mult),
                         reads=tokk + [('msm', 9)], writes=['mkw'])
                    for kc2 in range(2):
                        bankF = self.bank_any.next()
                        self.mm(ps[:, bankF, 0:257], [(kw[:, kc2 * 128:(kc2 + 1) * 128], vaug[:, ch, :])], ['mkw'] + tokk, bankF)
                        S.op('dve', (lambda e, kc2=kc2, bankF=bankF: e.scalar_tensor_tensor(out=Cst[kc2][:], in0=Cst[kc2][:], scalar=sm[:, 10:11], in1=ps[:, bankF, 0:257],
                                                                                           op0=ALU.mult, op1=ALU.add)),
                             reads=[('ps', bankF), ('msm', 10), ('mCst', kc2)], writes=[('mCst', kc2)])
                        S.op('act', (lambda e, kc2=kc2: e.activation(out=Cbf[kc2][:], in_=Cst[kc2][:], func=AF.Identity)),
                             reads=[('mCst', kc2)], writes=['mCbf'])
                    S.op('dve', lambda e: e.tensor_tensor(out=mnew, in0=sm[:, 3:4], in1=cbt, op=ALU.add), reads=[('msm', 3), 'mbt'], writes=['mmp'])
                    icnt[0] += 1
                    if stg == 3 and icnt[0] >= 1:
                        raise _Stop()

                for seq in range(nseq):
                    for d in range(2):
                        if grp == 0:
                            for kc2 in range(2):
                                S.op('dve', (lambda e, kc2=kc2: e.memset(Cst[kc2][:], 0.0)), writes=[('mCst', kc2)])
                                S.op('dve', (lambda e, kc2=kc2: e.memset(Cbf[kc2][:], 0.0)), writes=['mCbf'])
                            S.op('dve', lambda e: e.memset(mpv[0][:], 0.0), writes=['mmp'])
                        else:
                            for kc2 in range(2):
                                self.dma(Cst[kc2][:], ins["ml_caug0"][d, hd, kc2 * 128:(kc2 + 1) * 128, :], writes=[('mCst', kc2)])
                                S.op('act', (lambda e, kc2=kc2: e.activation(out=Cbf[kc2][:], in_=Cst[kc2][:], func=AF.Identity)),
                                     reads=[('mCst', kc2)], writes=['mCbf'])
                            S.op('dve', (lambda e, d=d: e.tensor_copy(out=mpv[0][:], in_=m0b[:, d * 4 + hd:d * 4 + hd + 1])), reads=['mm0'], writes=['mmp'])
                        order = list(range(cps)) if d == 0 else list(range(cps - 1, -1, -1))
                        cur = 0
                        for ci in order:
                            inst(seq, d, seq * cps + ci, ci == order[0], mpv[cur][:], mpv[1 - cur][:])
                            cur = 1 - cur
                        if grp == 0:
                            for kc2 in range(2):
                                self.dma(self.outs["caug_out"][seq, d, hd, kc2 * 128:(kc2 + 1) * 128, :], Cst[kc2][:], reads=[('mCst', kc2)])
                            self.dma(self.outs["m_out"][seq:seq + 1, d * 4 + hd:d * 4 + hd + 1], mpv[cur][0:1, 0:1], reads=['mmp'])
                if stg <= 4:
                    raise _Stop()
                S.barrier()
                b = loadw(wog_d[:, :, hd * 256:(hd + 1) * 256])
                for kc2 in range(2):
                    for t0 in range(0, gn, 512):
                        bank = self.bank_any.next()
                        self.mm(ps[:, bank, :], [(wrot[b][:, k, kc2 * 128:(kc2 + 1) * 128], hT[:, k, t0:t0 + 512]) for k in range(KC)],
                                [('mwrot', b)] + hk + qkk, bank)
                        S.op('act', (lambda e, bank=bank, kc2=kc2, t0=t0: e.activation(out=ogT[:, kc2, t0:t0 + 512], in_=ps[:, bank, :], func=AF.Sigmoid)),
                             reads=[('ps', bank)], writes=[('mog', kc2, t0)])
                if stg <= 5:
                    raise _Stop()
                for ch in range(NCH):
                    bank = self.bank_any.next()

                    def ftr(e, bank=bank, ch=ch):
                        for kc2 in range(2):
                            r = e.transpose(psb[:, bank, kc2 * 128:(kc2 + 1) * 128], hacc[:, ch, kc2 * 128:(kc2 + 1) * 128], id_b)
                        return r
                    S.op('pe', ftr, reads=[('mhacc', ch), 'cb'], writes=[('ps', bank)])
                    for kc2 in range(2):
                        S.op('dve', (lambda e, bank=bank, ch=ch, kc2=kc2: e.scalar_tensor_tensor(
                            out=gT[:, kc2, ch * 128:(ch + 1) * 128], in0=psb[:, bank, kc2 * 128:(kc2 + 1) * 128], scalar=hgT[:, 2 * hd + kc2:2 * hd + kc2 + 1],
                            in1=ogT[:, kc2, ch * 128:(ch + 1) * 128], op0=ALU.mult, op1=ALU.mult)),
                            reads=[('ps', bank), 'mhg', ('mog', kc2, (ch * 128) // 512 * 512)] + qkk, writes=[('mgT', ch // 4)])
                if stg <= 6:
                    raise _Stop()
                for t0 in range(0, gn, 512):
                    for oc in range(KC):
                        bank = self.bank_any.next()
                        self.mm(ps[:, bank, :], [(Wo[:, kc2, oc * 128:(oc + 1) * 128], gT[:, kc2, t0:t0 + 512]) for kc2 in range(2)],
                                ['mWo', ('mgT', t0 // 512)], bank)
                        self.xupdate(l, 1, g0 + t0, 512, oc, ps[:, bank, :], [('ps', bank)])
                S.barrier()
                M.release()
                if stg <= 7:
                    raise _Stop()

            for hd in range(4):
                do_head(hd)
            S.barrier()
            M.release()
            if stg <= 8:
                raise _Stop()

        if not getattr(self, "dbg_skipg0", False):
            do_group(0)
        do_group(1)
        M.release()

    def mixer_na(self, l):
        S, M, ins, ps = self.S, self.M, self.ins, self.ps
        SCALE = 0.125
        ones_b = self.ones_b
        id_b = self.id_b
        psb = self.psb
        wq_d = ins["na_w_qkv"].rearrange("(k p) n -> p k n", p=128)
        wo_d = ins["na_w_out"].rearrange("(k p) n -> p k n", p=128)
        M.mark()
        rowsel = M.alloc([32, 16, 128], BF16, "nrowsel")
        RM = M.alloc([32, TS], BF16, "nRM")
        cm = M.alloc([128, 256], BF16, "ncm")
        self.dma(rowsel[:], ins["na_rowsel"], writes=['nrowsel'])
        self.dma(RM[:], ins["na_rm"], writes=['nRM'])
        self.dma(cm[:], ins["na_cm"], writes=['ncm'])
        wbuf = [[M.alloc([128, KC, 128], BF16, "nW%d" % q) for q in range(3)] + [M.alloc([128, D], BF16, "nWo")] for _ in range(2)]
        tmpf_rr = RR([0, 1])
        rec = M.alloc([128, 256], F32, "nrec")
        NQ = 256

        def load_pair_w(c, b):
            for q in range(3):
                self.dma(wbuf[b][q][:], wq_d[:, :, q * D + c * 128:q * D + (c + 1) * 128], writes=[('nW', b, q)], q='pool')
            self.dma(wbuf[b][3][:], wo_d[:, c, :], writes=[('nW', b, 3)], q='pool')

        def do_group(grp):
            M.mark()
            g0, gn = (0, TP) if grp == 0 else (TP, TS)
            hT = M.alloc([128, KC, gn], BF16, "nhT")
            hk = [('nh', k, tq) for k in range(KC) for tq in range(gn // 256)]
            self.alloc_norm_scratch(256)
            for tq in range(gn // 256):
                self.norm_mod_range(l, 1, g0 + tq * 256, 256, hT, tq * 256, [('nh', k, tq) for k in range(KC)])
            if grp == 0:
                M.mark()
                tmpf = [M.alloc([128, 512], F32, "ntf") for _ in range(2)]
                Wfull = M.alloc([128, KC, D], BF16, "nWfull")
                for q, oname in ((1, "k_out"), (2, "v_out")):
                    self.dma(Wfull[:], wq_d[:, :, q * D:(q + 1) * D], writes=['nWfull'], q='pool')
                    for tc in range(TP // 128):
                        for half in range(2):
                            bank = self.bank_any.next()
                            self.mm(ps[:, bank, :], [(hT[:, k, tc * 128:(tc + 1) * 128], Wfull[:, k, half * 512:(half + 1) * 512]) for k in range(KC)],
                                    ['nWfull'] + hk, bank)
                            tb = tmpf_rr.next()
                            S.op('act', (lambda e, bank=bank, tb=tb: e.activation(out=tmpf[tb][:], in_=ps[:, bank, :], func=AF.Identity)),
                                 reads=[('ps', bank)], writes=[('ntf', tb)])
                            self.dma(self.outs[oname][tc * 128:(tc + 1) * 128, half * 512:(half + 1) * 512], tmpf[tb][:], reads=[('ntf', tb)])
                S.barrier()
                M.release()
            qT = M.alloc([128, gn], BF16, "nqT")
            kT = M.alloc([128, gn], BF16, "nkT")
            vtok = M.alloc([128, gn // 128, 128], BF16, "nvtok")
            oT = M.alloc([128, gn], BF16, "noT")
            PT = [M.alloc([128, 10, NQ], BF16, "nPT") for _ in range(2)]
            if grp == 1:
                ckb = M.alloc([128, 4, 128], BF16, "nckb")
                cvb = M.alloc([128, 4, 128], BF16, "ncvb")
                kcT = M.alloc([128, 512], BF16, "nkcT")
                Tt = M.alloc([128, 6, 256], F32, "nTt")
                ExT = M.alloc([128, 6, 256], BF16, "nExT")
                etmp = [M.alloc([128, 512], BF16, "netmp") for _ in range(2)]
                S.op('dve', lambda e: e.memset(Tt[:], NEG), writes=['nTt'])
            load_pair_w(0, 0)
            for c in range(KC):
                b = c % 2
                if c + 1 < KC:
                    load_pair_w(c + 1, (c + 1) % 2)
                Wq, Wk, Wv, Wo = wbuf[b]
                for t0 in range(0, gn, 512):
                    for q, dst, eng in ((0, qT, 'act'), (1, kT, 'dve')):
                        bank = self.bank_any.next()
                        self.mm(ps[:, bank, :], [(wbuf[b][q][:, k, :], hT[:, k, t0:t0 + 512]) for k in range(KC)], [('nW', b, q)] + hk, bank)
                        if eng == 'act':
                            S.op('act', (lambda e, dst=dst, bank=bank, t0=t0: e.activation(out=dst[:, t0:t0 + 512], in_=ps[:, bank, :], func=AF.Identity)),
                                 reads=[('ps', bank)], writes=[('nq', t0)])
                        else:
                            S.op('dve', (lambda e, dst=dst, bank=bank, t0=t0: e.tensor_copy(out=dst[:, t0:t0 + 512], in_=ps[:, bank, :])),
                                 reads=[('ps', bank)], writes=[('nk', t0)])
                    bank = self.bank_any.next()

                    def fv(e, bank=bank, t0=t0, Wv=Wv):
                        for tc in range(4):
                            for k in range(KC):
                                r = e.matmul(ps[:, bank, tc * 128:(tc + 1) * 128], lhsT=hT[:, k, t0 + tc * 128:t0 + (tc + 1) * 128], rhs=Wv[:, k, :],
                                             start=(k == 0 and tc == 0), stop=(k == KC - 1), skip_group_check=True)
                        return r
                    S.op('pe', fv, reads=[('nW', b, 2)] + hk, writes=[('ps', bank)])
                    S.op('act', (lambda e, bank=bank, t0=t0: e.activation(out=vtok[:, t0 // 128:t0 // 128 + 4, :], in_=ps[:, bank, :].rearrange("p (a b) -> p a b", b=128),
                                                                         func=AF.Identity)),
                         reads=[('ps', bank)], writes=[('nv', t0)])
                qk_all = [('nq', t0) for t0 in range(0, gn, 512)] + [('nk', t0) for t0 in range(0, gn, 512)]
                v_all = [('nv', t0) for t0 in range(0, gn, 512)]
                if grp == 1:
                    self.dma(ckb[:], ins["cache_k"].rearrange("(a p) n -> p a n", p=128)[:, :, c * 128:(c + 1) * 128], writes=['nckb'], q='pool')
                    self.dma(cvb[:], ins["cache_v"].rearrange("(a p) n -> p a n", p=128)[:, :, c * 128:(c + 1) * 128], writes=['ncvb'], q='pool')
                    bank = self.bank_any.next()

                    def ft(e, bank=bank):
                        for tc in range(4):
                            r = e.transpose(psb[:, bank, tc * 128:(tc + 1) * 128], ckb[:, tc, :], id_b)
                        return r
                    S.op('pe', ft, reads=['nckb', 'cb'], writes=[('ps', bank)])
                    S.op('dve', (lambda e, bank=bank: e.tensor_copy(out=kcT[:], in_=psb[:, bank, 0:512])), reads=[('ps', bank)], writes=['nkcT'])
                for half in range(2):
                    h = 2 * c + half
                    p0 = half * 64
                    pr = slice(p0, p0 + 64)
                    if grp == 0:
                        def stageA(sq_, pb, pr=pr):
                            t0 = sq_ * 256
                            bank = self.bank_any.next()

                            def fs(e):
                                for kc in range(2):
                                    r = e.matmul(ps[:, bank, kc * 256:(kc + 1) * 256], lhsT=kT[pr, t0 + kc * 128:t0 + (kc + 1) * 128],
                                                 rhs=qT[pr, t0:t0 + 256], start=True, stop=True)
                                return r
                            S.op('pe', fs, reads=qk_all, writes=[('ps', bank)])
                            S.op('act', lambda e: e.activation(out=PT[pb][:, 0:2, :], in_=ps[:, bank, :].rearrange("p (a b) -> p a b", b=256),
                                                               func=AF.Exp, scale=SCALE),
                                 reads=[('ps', bank)], writes=[('nPT', pb)])

                        def stageB(sq_, pb, pr=pr, half=half):
                            t0 = sq_ * 256
                            bank2 = self.bank_any.next()

                            def fo(e):
                                for kc in range(2):
                                    e.matmul(ps[:, bank2, 0:256], lhsT=vtok[:, t0 // 128 + kc, :], rhs=PT[pb][:, kc, :], start=(kc == 0), stop=(kc == 1),
                                             skip_group_check=True)
                                for kc in range(2):
                                    r = e.matmul(ps[:, bank2, 256:512], lhsT=ones_b, rhs=PT[pb][:, kc, :], start=False, stop=(kc == 1), skip_group_check=True)
                                return r
                            S.op('pe', fo, reads=[('nPT', pb), 'cb'] + v_all, writes=[('ps', bank2)])
                            S.op('act', lambda e: e.activation(out=rec[pr, 0:256], in_=ps[pr, bank2, 256:512], func=AF.Ln),
                                 reads=[('ps', bank2)], writes=['nrec'])
                            S.op('act', lambda e: e.activation(out=rec[pr, 0:256], in_=rec[pr, 0:256], func=AF.Exp, scale=-1.0),
                                 reads=['nrec'], writes=['nrec'])
                            S.op('dve', lambda e: e.tensor_tensor(out=oT[pr, t0:t0 + 256], in0=ps[pr, bank2, 0:256], in1=rec[pr, 0:256], op=ALU.mult),
                                 reads=[('ps', bank2), 'nrec'], writes=[('no', half)])
                        nblk = NPS
                    else:
                        for d in range(6):
                            for kl in range(2):
                                lo = max(0, 2 * d + kl - 11)
                                hi = min(4, 2 * d + kl + 4)
                                if hi <= lo:
                                    continue
                                m0 = 11 - 2 * d - kl + lo
                                src = ins["na_rexp"][h, m0:m0 + hi - lo].rearrange("m k q -> k m q")
                                self.dma(Tt[kl * 64:(kl + 1) * 64, d, lo * 64:hi * 64].rearrange("p (a b) -> p a b", b=64), src, writes=['nTt'])
                        for d in range(6):
                            S.op('dve', (lambda e, d=d: e.tensor_tensor(out=Tt[:, d, :], in0=Tt[:, d, :], in1=cm[:], op=ALU.add)),
                                 reads=['nTt', 'ncm'], writes=['nTt'])
                        S.op('act', lambda e: e.activation(out=ExT[:], in_=Tt[:], func=AF.Exp), reads=['nTt'], writes=['nExT'])

                        def wins_of(jb):
                            return [(d, 2 * jb - 2 + d) for d in range(6) if 0 <= 2 * jb - 2 + d < 16]

                        def stageA(jb, pb, pr=pr):
                            q0 = jb * NQ
                            wins = wins_of(jb)
                            for wp in range(0, len(wins), 2):
                                (d0, ck0), (d1, ck1) = wins[wp], wins[wp + 1]
                                bank = self.bank_any.next()

                                def fs(e, bank=bank, ck0=ck0, ck1=ck1):
                                    e.matmul(ps[:, bank, 0:NQ], lhsT=kT[pr, ck0 * 128:(ck0 + 1) * 128], rhs=qT[pr, q0:q0 + NQ], start=True, stop=False,
                                             skip_group_check=True)
                                    e.matmul(ps[:, bank, 0:NQ], lhsT=rowsel[:, ck0, :], rhs=RM[:, q0:q0 + NQ], start=False, stop=True, skip_group_check=True)
                                    e.matmul(ps[:, bank, NQ:2 * NQ], lhsT=kT[pr, ck1 * 128:(ck1 + 1) * 128], rhs=qT[pr, q0:q0 + NQ], start=False, stop=False,
                                             skip_group_check=True)
                                    return e.matmul(ps[:, bank, NQ:2 * NQ], lhsT=rowsel[:, ck1, :], rhs=RM[:, q0:q0 + NQ], start=False, stop=True,
                                                    skip_group_check=True)
                                S.op('pe', fs, reads=qk_all + ['nrowsel', 'nRM'], writes=[('ps', bank)])
                                tb = tmpf_rr.next()
                                S.op('act', (lambda e, bank=bank, tb=tb: e.activation(out=etmp[tb][:], in_=ps[:, bank, 0:2 * NQ], func=AF.Exp, scale=SCALE)),
                                     reads=[('ps', bank)], writes=[('netmp', tb)])
                                S.op('dve', (lambda e, tb=tb, wp=wp, d0=d0: e.tensor_tensor(
                                    out=PT[pb][:, wp:wp + 2, :], in0=etmp[tb][:].rearrange("p (a b) -> p a b", b=NQ), in1=ExT[:, d0:d0 + 2, :], op=ALU.mult)),
                                    reads=[('netmp', tb), 'nExT'], writes=[('nPTs', pb, wp), ('nPTs', pb, wp + 1)])
                            nw = len(wins)
                            for t2 in range(2):
                                bank = self.bank_any.next()

                                def fc(e, bank=bank, t2=t2):
                                    for q in range(2):
                                        tc = 2 * t2 + q
                                        r = e.matmul(ps[:, bank, q * NQ:(q + 1) * NQ], lhsT=kcT[pr, tc * 128:(tc + 1) * 128], rhs=qT[pr, q0:q0 + NQ],
                                                     start=True, stop=True)
                                    return r
                                S.op('pe', fc, reads=qk_all + ['nkcT'], writes=[('ps', bank)])
                                S.op('act', (lambda e, bank=bank, t2=t2, nw=nw: e.activation(
                                    out=PT[pb][:, nw + 2 * t2:nw + 2 * t2 + 2, :], in_=ps[:, bank, 0:2 * NQ].rearrange("p (a b) -> p a b", b=NQ), func=AF.Exp, scale=SCALE)),
                                    reads=[('ps', bank)], writes=[('nPTs', pb, nw + 2 * t2), ('nPTs', pb, nw + 2 * t2 + 1)])

                        def stageB(jb, pb, pr=pr, half=half):
                            q0 = jb * NQ
                            wins = wins_of(jb)
                            np_ = len(wins) + 4
                            bank2 = self.bank_any.next()
                            lhs = [vtok[:, ck, :] for (d, ck) in wins] + [cvb[:, tc, :] for tc in range(4)]

                            def fo(e):
                                for q in range(np_):
                                    e.matmul(ps[:, bank2, 0:NQ], lhsT=lhs[q], rhs=PT[pb][:, q, :], start=(q == 0), stop=(q == np_ - 1), skip_group_check=True)
                                for q in range(np_):
                                    r = e.matmul(ps[:, bank2, NQ:2 * NQ], lhsT=ones_b, rhs=PT[pb][:, q, :], start=False, stop=(q == np_ - 1), skip_group_check=True)
                                return r
                            S.op('pe', fo, reads=[('nPTs', pb, q) for q in range(np_)] + ['cb', 'ncvb'] + v_all, writes=[('ps', bank2)])
                            S.op('act', lambda e: e.activation(out=rec[pr, 0:NQ], in_=ps[pr, bank2, NQ:2 * NQ], func=AF.Ln),
                                 reads=[('ps', bank2)], writes=['nrec'])
                            S.op('act', lambda e: e.activation(out=rec[pr, 0:NQ], in_=rec[pr, 0:NQ], func=AF.Exp, scale=-1.0),
                                 reads=['nrec'], writes=['nrec'])
                            S.op('dve', lambda e: e.tensor_tensor(out=oT[pr, q0:q0 + NQ], in0=ps[pr, bank2, 0:NQ], in1=rec[pr, 0:NQ], op=ALU.mult),
                                 reads=[('ps', bank2), 'nrec'], writes=[('no', half)])
                        nblk = TS // NQ
                    prevb = None
                    for blk in range(nblk):
                        stageA(blk, blk % 2)
                        if prevb is not None:
                            stageB(prevb, prevb % 2)
                        prevb = blk
                    stageB(prevb, prevb % 2)
                if grp == 1 and c == 0:
                    self.dump_tile("noTs", oT[:], [128, TS], [('no', 0), ('no', 1)], BF16)
                    self.dump_tile("nvtoks", vtok[:], [128, 16, 128], v_all, BF16)
                    self.dump_tile("ncvb", cvb[:], [128, 4, 128], ['ncvb'], BF16)
                    self.dump_tile("nrec", rec[:], [128, 512], ['nrec'])
                if grp == 0 and c == 0:
                    self.dump_tile("noT", oT[:], [128, TP], [('no', 0), ('no', 1)], BF16)
                    self.dump_tile("nqT", qT[:], [128, TP], qk_all, BF16)
                    self.dump_tile("nkT", kT[:], [128, TP], qk_all, BF16)
                    self.dump_tile("nvtok", vtok[:], [128, TP // 128, 128], v_all, BF16)
                for t0 in range(0, gn, 512):
                    for oc in range(KC):
                        bank = self.bank_any.next()
                        self.mm(ps[:, bank, :], [(Wo[:, oc * 128:(oc + 1) * 128], oT[:, t0:t0 + 512])], [('nW', b, 3), ('no', 0), ('no', 1)], bank)
                        self.xupdate(l, 1, g0 + t0, 512, oc, ps[:, bank, :], [('ps', bank)])
            S.barrier()
            M.release()
        do_group(0)
        do_group(1)
        M.release()

    def mixer_fourier(self, l):
        S, M, ins, ps = self.S, self.M, self.ins, self.ps
        NTK = 256
        M.mark()
        Wout = M.alloc([128, KC, D], BF16, "fWout")
        boT = M.alloc([128, KC], F32, "fbo")
        Y1 = M.alloc([128, 16, D], BF16, "fY1")
        Y2 = M.alloc([128, 16, D], BF16, "fY2")
        tmp = [M.alloc([128, NTK], F32, "ftmp") for _ in range(2)]
        tmp_rr = RR([0, 1])
        self.dma(Wout[:], ins["fn_w_out"].rearrange("(k p) n -> p k n", p=128), writes=['fWout'], q='pool')
        self.dma(boT[:], ins["fn_b_outT"], writes=['fbo'])
        hk = [('fh', k) for k in range(KC)]
        tabs = {}

        def step1(t0, hT, ych, ykeys):
            self.norm_mod_range(l, 1, t0, NTK, hT, 0, hk)
            for tc in range(NTK // 128):
                for tab, Yd, eng in ((tabs['c'], Y1, 'act'), (tabs['s'], Y2, 'dve')):
                    for g0 in (0, 2):
                        bank = self.bank_any.next()

                        def f(e, tab=tab, g0=g0, bank=bank, tc=tc):
                            for gi in range(2):
                                g = g0 + gi
                                for kk in range(2):
                                    r = e.matmul(ps[:, bank, gi * 256:(gi + 1) * 256], lhsT=hT[:, 2 * g + kk, tc * 128:(tc + 1) * 128],
                                                 rhs=tab[:, kk, :], start=(kk == 0 and gi == 0), stop=(kk == 1), skip_group_check=True)
                            return r
                        S.op('pe', f, reads=hk[2 * g0:2 * g0 + 4] + ['fd256'], writes=[('ps', bank)])
                        c0 = g0 * 256
                        if eng == 'act':
                            S.op('act', (lambda e, Yd=Yd, bank=bank, tc=tc, c0=c0: e.activation(out=Yd[:, ych + tc, c0:c0 + 512], in_=ps[:, bank, :], func=AF.Identity)),
                                 reads=[('ps', bank)], writes=[ykeys[0]])
                        else:
                            S.op('dve', (lambda e, Yd=Yd, bank=bank, tc=tc, c0=c0: e.tensor_copy(out=Yd[:, ych + tc, c0:c0 + 512], in_=ps[:, bank, :])),
                                 reads=[('ps', bank)], writes=[ykeys[1]])

        def step23(t0, nsc, ych, CS, NS, fT, ykeys, tkeys):
            for cc in range(KC):
                bank = self.bank_any.next()
                pairs = []
                for sc in range(nsc):
                    pairs.append((Y1[:, ych + sc, cc * 128:(cc + 1) * 128], CS(sc)))
                    pairs.append((Y2[:, ych + sc, cc * 128:(cc + 1) * 128], NS(sc)))
                self.mm(ps[:, bank, 0:NTK], pairs, ykeys + tkeys, bank)
                S.op('act', (lambda e, cc=cc, bank=bank: e.activation(out=fT[:, cc, :], in_=ps[:, bank, 0:NTK], func=AF.Identity)),
                     reads=[('ps', bank)], writes=[('ffT', cc)])
            for oc in range(KC):
                bank = self.bank_any.next()
                self.mm(ps[:, bank, 0:NTK], [(Wout[:, kc, oc * 128:(oc + 1) * 128], fT[:, kc, :]) for kc in range(KC)],
                        ['fWout'] + [('ffT', cc) for cc in range(KC)], bank)
                tb = tmp_rr.next()
                S.op('act', (lambda e, oc=oc, bank=bank, tb=tb: e.activation(out=tmp[tb][:], in_=ps[:, bank, 0:NTK], func=AF.Identity,
                                                                            bias=boT[:, oc:oc + 1])),
                     reads=[('ps', bank), 'fbo'], writes=[('ftmp', tb)])
                self.xupdate(l, 1, t0, NTK, oc, tmp[tb][:], [('ftmp', tb)])

        M.mark()
        d256 = M.alloc([128, 3, 2, 256], BF16, "fd256")
        self.dma(d256[:], ins["dft256"], writes=['fd256'])
        c256, s256, ns256 = d256[:, 0], d256[:, 1], d256[:, 2]
        tabs['c'], tabs['s'] = c256, s256
        hT = M.alloc([128, KC, NTK], BF16, "fhT")
        fTp = M.alloc([128, KC, NTK], BF16, "ffTp")
        self.alloc_norm_scratch(NTK)
        for sq_ in range(NPS):
            t0 = sq_ * 256
            step1(t0, hT, 0, ['fY1p', 'fY2p'])
            step23(t0, 2, 0, (lambda sc: c256[:, sc, :]), (lambda sc: ns256[:, sc, :]), fTp, ['fY1p', 'fY2p'], ['fd256'])
        S.barrier()
        nq = TS // NTK
        for tq in range(nq):
            step1(TP + tq * NTK, hT, tq * 2, [('fY1s', tq), ('fY2s', tq)])
        ykeys = [('fY1s', tq) for tq in range(nq)] + [('fY2s', tq) for tq in range(nq)]
        S.barrier()
        M.release()
        M.mark()
        CSt = M.alloc([128, 16, NTK], BF16, "fCS")
        NSt = M.alloc([128, 16, NTK], BF16, "fNS")
        fT = M.alloc([128, KC, NTK], BF16, "ffT")
        cs_d = ins["dft2048"][0].rearrange("(k p) n -> p k n", p=128)
        ns_d = ins["dft2048"][1].rearrange("(k p) n -> p k n", p=128)
        for st in range(nq):
            self.dma(CSt[:], cs_d[:, :, st * NTK:(st + 1) * NTK], writes=['fCS'])
            self.dma(NSt[:], ns_d[:, :, st * NTK:(st + 1) * NTK], writes=['fNS'])
            step23(TP + st * NTK, 16, 0, (lambda sc: CSt[:, sc, :]), (lambda sc: NSt[:, sc, :]), fT, ykeys, ['fCS', 'fNS'])
        M.release()
        M.release()

    def mixer_gmlp(self, l):
        S, M, ins, ps = self.S, self.M, self.ins, self.ps
        NTK = 256
        NC_ = NTK // 128
        M.mark()
        Win = M.alloc([128, KC, 2 * D], BF16, "gWin")
        Wout = M.alloc([128, KC, D], BF16, "gWout")
        wsT = M.alloc([128, 4, 128], BF16, "gws")
        binT = M.alloc([128, 16], F32, "gbinT")
        binV = M.alloc([128, D], F32, "gbinV")
        vgb = M.alloc([128, D], F32, "gvgb")
        bsr = M.alloc([1, 512], F32, "gbsr")
        hT2 = [M.alloc([128, KC, NTK], BF16, "ghT") for _ in range(2)]
        uT2 = [M.alloc([128, KC, NTK], BF16, "guT") for _ in range(2)]
        vn2 = [M.alloc([128, NC_, D], BF16, "gvn") for _ in range(2)]
        vraw = M.alloc([128, D], F32, "gvraw")
        junk = M.alloc([128, 512], BF16, "gjunk")
        tmp = [M.alloc([128, 512], F32, "gtmp") for _ in range(2)]
        ssq = M.alloc([128, 2], F32, "gssq")
        rsv = M.alloc([128, 1], F32, "grsv")
        self.alloc_norm_scratch(NTK)
        epsT, ones_f = self.epsT, self.ones_f
        self.dma(Win[:], ins["gm_w_in"].rearrange("(k p) n -> p k n", p=128), writes=['gWin'], q='pool')
        self.dma(Wout[:], ins["gm_w_out"].rearrange("(k p) n -> p k n", p=128), writes=['gWout'], q='pool')
        self.dma(wsT[:], ins["gm_w_sT"], writes=['gws'], q='pool')
        self.dma(binT[:], ins["gm_b_inT"], writes=['gbinT'])
        self.dma(binV[:], ins["gm_b_in"][:, D:2 * D].partition_broadcast(128), writes=['gbinV'])
        self.dma(vgb[:], ins["gm_v_g"].partition_broadcast(128), writes=['gvgb'])
        self.dma(bsr[:], ins["gm_b_s"], writes=['gbsr'])
        tmp_rr = RR([0, 1])
        tiles = list(range(0, T, NTK))
        env = locals()
        for q, t0 in enumerate(tiles):
            self.gmlp_tile(l, t0, q % 2, env, 'A')
            if q > 0:
                self.gmlp_tile(l, tiles[q - 1], (q - 1) % 2, env, 'B')
        self.gmlp_tile(l, tiles[-1], (len(tiles) - 1) % 2, env, 'B')
        M.release()

    def gmlp_tile(self, l, t0, pb, env, stage):
        S, ps = self.S, self.ps
        NTK, NC_, Win, Wout, wsT, binT, binV, vgb, bsr = (env[k] for k in ("NTK", "NC_", "Win", "Wout", "wsT", "binT", "binV", "vgb", "bsr"))
        vraw, junk, tmp, ssq, rsv, tmp_rr, epsT, ones_f = (env[k] for k in ("vraw", "junk", "tmp", "ssq", "rsv", "tmp_rr", "epsT", "ones_f"))
        hT, uT, vn = env["hT2"][pb], env["uT2"][pb], env["vn2"][pb]
        hk = [('gh', pb, k) for k in range(KC)]
        if stage == 'A':
            self.norm_mod_range(l, 1, t0, NTK, hT, 0, hk)
            for oc in range(KC):
                bank = self.bank_any.next()
                self.mm(ps[:, bank, 0:NTK], [(Win[:, k, oc * 128:(oc + 1) * 128], hT[:, k, :]) for k in range(KC)], ['gWin'] + hk, bank)
                S.op('act', (lambda e, oc=oc, bank=bank: e.activation(out=uT[:, oc, :], in_=ps[:, bank, 0:NTK], func=AF.Gelu_apprx_tanh,
                                                                     bias=binT[:, oc:oc + 1])),
                     reads=[('ps', bank), 'gbinT'], writes=[('gu', pb, oc)])
            for tc in range(NC_):
                for half in range(2):
                    bank = self.bank_any.next()
                    self.mm(ps[:, bank, :], [(hT[:, k, tc * 128:(tc + 1) * 128], Win[:, k, D + half * 512:D + (half + 1) * 512]) for k in range(KC)],
                            ['gWin'] + hk, bank)
                    tb = tmp_rr.next()
                    S.op('dve', (lambda e, bank=bank, tb=tb, half=half: e.tensor_tensor(
                        out=tmp[tb][:], in0=ps[:, bank, :], in1=binV[:, half * 512:(half + 1) * 512], op=ALU.add)),
                        reads=[('ps', bank), 'gbinV'], writes=[('gtmp', tb)])
                    S.op('act', (lambda e, tb=tb, half=half: e.activation(out=vraw[:, half * 512:(half + 1) * 512], in_=tmp[tb][:],
                                                                         func=AF.Gelu_apprx_tanh)),
                         reads=[('gtmp', tb)], writes=[('gvraw', half)])
                    S.op('dve', (lambda e, half=half: e.scalar_tensor_tensor(
                        out=junk[:], in0=vraw[:, half * 512:(half + 1) * 512], scalar=1.0, in1=vraw[:, half * 512:(half + 1) * 512],
                        op0=ALU.mult, op1=ALU.mult, accum_out=ssq[:, half:half + 1])),
                        reads=[('gvraw', half)], writes=['gjunk', ('gssq', half)])
                S.op('dve', lambda e: e.tensor_tensor(out=rsv[:], in0=ssq[:, 0:1], in1=ssq[:, 1:2], op=ALU.add),
                     reads=[('gssq', 0), ('gssq', 1)], writes=['grsv'])
                S.op('act', lambda e: e.activation(out=rsv[:], in_=rsv[:], func=AF.Sqrt, scale=1.0 / D, bias=epsT[:, 0:1]),
                     reads=['grsv', 'eps'], writes=['grsv'])
                S.op('dve', lambda e: e.reciprocal(out=rsv[:], in_=rsv[:]), reads=['grsv'], writes=['grsv'])
                S.op('dve', (lambda e, tc=tc: e.scalar_tensor_tensor(
                    out=vn[:, tc, :], in0=vraw[:], scalar=rsv[:, 0:1], in1=vgb[:], op0=ALU.mult, op1=ALU.mult)),
                    reads=[('gvraw', 0), ('gvraw', 1), 'grsv', 'gvgb'], writes=[('gvn', pb, tc)])
        else:
            for oc in range(KC):
                g = oc // 2
                bank = self.bank_any.next()

                def f(e, oc=oc, g=g, bank=bank):
                    for tc in range(NC_):
                        e.matmul(ps[:, bank, tc * 128:(tc + 1) * 128], lhsT=vn[:, tc, oc * 128:(oc + 1) * 128], rhs=wsT[:, g, :],
                                 start=(tc == 0), stop=False, skip_group_check=True)
                        r = e.matmul(ps[:, bank, tc * 128:(tc + 1) * 128], lhsT=ones_f[0:1, :], rhs=bsr[0:1, g * 128:(g + 1) * 128],
                                     start=False, stop=True, skip_group_check=True)
                    return r
                S.op('pe', f, reads=[('gvn', pb, tc) for tc in range(NC_)] + ['gws', 'gbsr', 'cf'], writes=[('ps', bank)])
                S.op('dve', (lambda e, oc=oc, bank=bank: e.tensor_tensor(out=uT[:, oc, :], in0=uT[:, oc, :], in1=ps[:, bank, 0:NTK], op=ALU.mult)),
                     reads=[('gu', pb, oc), ('ps', bank)], writes=[('gu', pb, oc)])
            for oc in range(KC):
                bank = self.bank_any.next()
                self.mm(ps[:, bank, 0:NTK], [(Wout[:, k, oc * 128:(oc + 1) * 128], uT[:, k, :]) for k in range(KC)],
                        ['gWout'] + [('gu', pb, k) for k in range(KC)], bank)
                self.xupdate(l, 1, t0, NTK, oc, ps[:, bank, 0:NTK], [('ps', bank)])


def _consts():
    cf = np.zeros((9, 128, 128), np.float32)
    s = np.arange(128)[:, None]
    t = np.arange(128)[None, :]
    cf[0] = 1.0
    cf[1] = np.eye(128)
    cf[2] = (s <= t)
    cf[3] = (s >= t)
    cf[4] = np.where(t <= s, 0.0, NEG)
    cf[5] = np.where(t >= s, 0.0, NEG)
    cf[6] = np.where(s <= t, 0.0, -NEG)
    cf[7] = np.where(s >= t, 0.0, -NEG)
    cf[8] = np.eye(128)[::-1]
    cb = np.zeros((2, 128, 128), np.float32)
    cb[0] = np.eye(128)
    cb[1] = 1.0
    return (np.ascontiguousarray(cf.transpose(1, 0, 2)),
            np.ascontiguousarray(cb.transpose(1, 0, 2)).astype(ml_dtypes.bfloat16))


def _na_tables(rpb):
    bf = ml_dtypes.bfloat16
    dc = np.clip(np.arange(64)[:, None] - np.arange(64)[None, :] + 15, 0, 30)
    rexp = np.ascontiguousarray(rpb[:, ::-1, :][:, :, dc])
    qc = np.arange(64)
    qs = np.clip(qc - 8, 0, 48)
    kc = np.arange(64)[:, None]
    colmask = np.where((kc >= qs[None, :]) & (kc < qs[None, :] + 16), 0.0, NEG).astype(np.float32)
    cm = np.tile(colmask, (2, 4))
    rowsel = np.zeros((32, 16, 128), np.float32)
    for c in range(16):
        rowsel[2 * c, c, :64] = 1.0
        rowsel[2 * c + 1, c, 64:] = 1.0
    qr = np.arange(32)
    rs = np.clip(qr - 4, 0, 24)
    kr = np.arange(32)[:, None]
    rm = np.where((kr >= rs[None, :]) & (kr < rs[None, :] + 8), 0.0, NEG).astype(np.float32)
    rm = np.repeat(rm, 64, axis=1)
    return {"na_rexp": rexp, "na_cm": np.ascontiguousarray(cm).astype(bf), "na_rowsel": rowsel.astype(bf), "na_rm": np.ascontiguousarray(rm).astype(bf)}


def _fourier_tables():
    bf = ml_dtypes.bfloat16
    n = np.arange(256, dtype=np.float64)
    ang = 2.0 * np.pi * ((n[:, None] * n[None, :]) % 256) / 256.0
    c, sn = np.cos(ang) / 16.0, np.sin(ang) / 16.0
    t = np.stack([c, sn, -sn], 0).reshape(3, 2, 128, 256).transpose(2, 0, 1, 3)
    n2 = np.arange(2048, dtype=np.int64)
    ang2 = 2.0 * np.pi * ((n2[:, None] * n2[None, :]) % 2048).astype(np.float64) / 2048.0
    r = 1.0 / np.sqrt(2048.0)
    d2048 = np.stack([np.cos(ang2) * r, -np.sin(ang2) * r], 0)
    return {"dft256": np.ascontiguousarray(t).astype(bf), "dft2048": np.ascontiguousarray(d2048).astype(bf)}


def _fm(a):
    t, d = a.shape
    return np.ascontiguousarray(a.reshape(t, d // 128, 128).transpose(2, 1, 0))


def _fm_inv(a):
    p, c, t = a.shape
    return np.ascontiguousarray(a.transpose(2, 1, 0).reshape(t, c * p))


def _vecT(v):
    sh = v.shape
    n = sh[-1] // 128
    a = v.reshape(sh[:-1] + (n, 128))
    return np.ascontiguousarray(np.moveaxis(a, -1, 0))


def make_in_maps(inp, builder):
    cf, cb = _consts()
    maps = []
    f32 = lambda a: np.ascontiguousarray(np.asarray(a, dtype=np.float32))
    shared = {
        "b_adaT": _vecT(f32(inp["b_ada"])),
        "norm_gT": _vecT(f32(inp["norm_g"])),
        "final_gT": _vecT(f32(inp["final_g"])),
        "cf32": cf, "cbf16": cb,
        "w_ada": f32(inp["w_ada"]), "ffn_w1": f32(inp["ffn_w1"]), "ffn_w3": f32(inp["ffn_w3"]), "ffn_w2": f32(inp["ffn_w2"]),
    }
    shared.update(_fourier_tables())
    shared.update(_na_tables(f32(inp["na_rpb"][0])))
    shared.update({
        "ml_w_qkv": f32(inp["ml_w_qkv"][0]), "ml_w_og": f32(inp["ml_w_og"][0]), "ml_w_out": f32(inp["ml_w_out"][0]),
        "ml_w_if": np.ascontiguousarray(np.concatenate([f32(inp["ml_w_if"][0, 0]), f32(inp["ml_w_if"][0, 1])], axis=1)),
        "ml_b_if": f32(inp["ml_b_if"][0]).reshape(1, 16),
        "ml_head_gT": _vecT(f32(inp["ml_head_g"][0])),
    })
    shared.update({"na_w_qkv": f32(inp["na_w_qkv"][0]), "na_w_out": f32(inp["na_w_out"][0])})
    shared.update({
        "fn_w_out": f32(inp["fn_w_out"][0]), "fn_b_outT": _vecT(f32(inp["fn_b_out"][0])),
        "gm_w_in": f32(inp["gm_w_in"][0]), "gm_w_out": f32(inp["gm_w_out"][0]),
        "gm_b_inT": _vecT(f32(inp["gm_b_in"][0])), "gm_b_in": f32(inp["gm_b_in"]), "gm_v_g": f32(inp["gm_v_g"]),
        "gm_w_sT": np.ascontiguousarray(f32(inp["gm_w_s"][0]).transpose(2, 0, 1)),
        "gm_b_s": f32(inp["gm_b_s"][0]).reshape(1, 512),
    })
    for ci in range(8):
        b = ci // 4
        xp = f32(inp["x_prompt"][NPS * ci:NPS * (ci + 1)]).reshape(TP, D)
        xs = f32(inp["x_sample"][b])
        m = dict(shared)
        m["xT"] = _fm(np.concatenate([xp, xs], axis=0))
        cond = np.stack([f32(inp["c_ctx"]), f32(inp["c"][b])], axis=-1)
        m["condT"] = np.ascontiguousarray(cond.reshape(KC, 128, 2).transpose(1, 0, 2))
        m["ml_caug0"] = np.ascontiguousarray(np.concatenate([f32(inp["state_mlstm_C"][b, 0]), f32(inp["state_mlstm_n"][b, 0])[..., None]], axis=-1))
        m["ml_m0"] = f32(inp["state_mlstm_m"][b, 0]).reshape(1, 8)
        m["cache_k"] = f32(inp["cache_na_k"][b, 0]).reshape(512, D)
        m["cache_v"] = f32(inp["cache_na_v"][b, 0]).reshape(512, D)
        maps.append({k: v for k, v in m.items() if k in builder.ins})
    return maps


def kernel(**inp):
    B = Builder()
    nc = B.build()
    maps = make_in_maps(inp, B)
    res = run_bass_kernel_spmd(nc, maps, core_ids=list(range(8)))
    nb = 8 * NPS
    yp = np.zeros((nb, 256, D), np.float32)
    ys = np.zeros((2, 2048, D), np.float32)
    sC = np.zeros((nb, 1, 2, 4, 256, 256), np.float32)
    sn = np.zeros((nb, 1, 2, 4, 256), np.float32)
    sm = np.zeros((nb, 1, 2, 4), np.float32)
    ck = np.zeros((nb, 1, 256, 16, 64), np.float32)
    cv = np.zeros((nb, 1, 256, 16, 64), np.float32)
    for ci in range(8):
        r = res.results[ci]
        y = _fm_inv(np.asarray(r["yT"]))
        sl = slice(NPS * ci, NPS * (ci + 1))
        yp[sl] = y[:TP].reshape(NPS, 256, D)
        if ci % 4 == 0:
            ys[ci // 4] = y[TP:]
        ca = np.asarray(r["caug_out"])
        sC[sl, 0] = ca[..., :256]
        sn[sl, 0] = ca[..., 256]
        sm[sl, 0] = np.asarray(r["m_out"]).reshape(NPS, 2, 4)
        ck[sl, 0] = np.asarray(r["k_out"]).reshape(NPS, 256, 16, 64)
        cv[sl, 0] = np.asarray(r["v_out"]).reshape(NPS, 256, 16, 64)
    return yp, ys, sC, sn, sm, ck, cv
```
